# Optimizing a Trainium2 kernel written in Bass

```python
import jax, jax.numpy as jnp
from jax import lax
import numpy as np

D_MODEL = 1024
BATCH = 4
SEQ = 4096
DEPTH = 2

MIX_WIDTH = D_MODEL
MLSTM_HEADS = 4
MLSTM_WIDTH = MIX_WIDTH // 2
MLSTM_V_DIM = MLSTM_WIDTH // MLSTM_HEADS
MLSTM_QK_DIM = MLSTM_V_DIM // 2
QK_WIDTH = MLSTM_HEADS * MLSTM_QK_DIM
QK_CONV = 4
CHUNK = 64
POOL_WIDTH = MIX_WIDTH - MLSTM_WIDTH
POOL_WINDOWS = (2, 4, 8, 16)
POOL_GROUPS = len(POOL_WINDOWS)
POOL_GROUP_DIM = POOL_WIDTH // POOL_GROUPS
D_FF = 2816
FFN_CONV = 3
EPS = 1e-6
SPLITS = (2 * QK_WIDTH,
          2 * QK_WIDTH + MLSTM_WIDTH,
          2 * QK_WIDTH + 2 * MLSTM_WIDTH,
          2 * QK_WIDTH + 2 * MLSTM_WIDTH + 2 * MLSTM_HEADS)
IN_WIDTH = SPLITS[-1] + POOL_WIDTH

kernel_name = "hymba_mlstm_multiscale_pool_convffn"


def rmsnorm(x, g):
    xf = x.astype(jnp.float32)
    y = xf * lax.rsqrt(jnp.mean(xf * xf, axis=-1, keepdims=True) + EPS)
    return (y * g.astype(jnp.float32)).astype(x.dtype)


def causal_dwconv(x, w):
    K = w.shape[0]
    T = x.shape[1]
    xp = jnp.pad(x, ((0, 0), (K - 1, 0), (0, 0)))
    return sum(xp[:, j:j + T] * w[j] for j in range(K))


def mlstm_chunkwise(q, k, v, i_pre, f_pre):
    B, T, H, dk = q.shape
    dv = v.shape[-1]
    nc = T // CHUNK
    f32 = jnp.float32

    def chunks(a):
        a = a.reshape((B, nc, CHUNK, H) + a.shape[3:])
        return jnp.moveaxis(a, 3, 1)

    qc = chunks(q.astype(f32))
    kc = chunks(k.astype(f32)) * (dk ** -0.5)
    vc = chunks(v.astype(f32))
    ig = chunks(i_pre.astype(f32))
    lf = jax.nn.log_sigmoid(chunks(f_pre.astype(f32)))
    b = jnp.cumsum(lf, axis=-1)
    b_last = b[..., -1]

    a = b_last[..., None] - b + ig
    m_loc = jnp.max(a, axis=-1)
    wgt = jnp.exp(a - m_loc[..., None])
    c_loc = jnp.einsum('bhcs,bhcsk,bhcsv->bhckv', wgt, kc, vc)
    n_loc = jnp.einsum('bhcs,bhcsk->bhck', wgt, kc)

    def step(carry, inp):
        c, n, m = carry
        g, ml, cl, nl = inp
        m_new = jnp.maximum(g + m, ml)
        s_prev = jnp.exp(g + m - m_new)
        s_loc = jnp.exp(ml - m_new)
        c_new = s_prev[..., None, None] * c + s_loc[..., None, None] * cl
        n_new = s_prev[..., None] * n + s_loc[..., None] * nl
        return (c_new, n_new, m_new), (c, n, m)

    init = (jnp.zeros((B, H, dk, dv), f32), jnp.zeros((B, H, dk), f32), jnp.zeros((B, H), f32))
    xs = (jnp.moveaxis(b_last, 2, 0), jnp.moveaxis(m_loc, 2, 0),
          jnp.moveaxis(c_loc, 2, 0), jnp.moveaxis(n_loc, 2, 0))
    _, (c_prev, n_prev, m_prev) = lax.scan(step, init, xs)
    c_prev = jnp.moveaxis(c_prev, 0, 2)
    n_prev = jnp.moveaxis(n_prev, 0, 2)
    m_prev = jnp.moveaxis(m_prev, 0, 2)

    causal = np.tril(np.ones((CHUNK, CHUNK), dtype=bool))
    log_d = jnp.where(causal, b[..., :, None] - b[..., None, :] + ig[..., None, :], -jnp.inf)
    m_inter = b + m_prev[..., None]
    m = jnp.maximum(m_inter, jnp.max(log_d, axis=-1))
    dmat = jnp.exp(log_d - m[..., None])
    s = jnp.einsum('bhcjk,bhcsk->bhcjs', qc, kc) * dmat
    sc = jnp.exp(m_inter - m)
    num = (jnp.einsum('bhcjs,bhcsv->bhcjv', s, vc)
           + sc[..., None] * jnp.einsum('bhcjk,bhckv->bhcjv', qc, c_prev))
    den = jnp.sum(s, axis=-1) + sc * jnp.einsum('bhcjk,bhck->bhcj', qc, n_prev)
    h = num / jnp.maximum(jnp.abs(den), jnp.exp(-m))[..., None]
    h = jnp.moveaxis(h, 1, 3).reshape(B, T, H, dv)
    return h.astype(v.dtype)


def head_rmsnorm(h, g):
    B, T, H, dv = h.shape
    hf = h.astype(jnp.float32)
    y = hf * lax.rsqrt(jnp.mean(hf * hf, axis=-1, keepdims=True) + EPS)
    return (y.reshape(B, T, H * dv) * g.astype(jnp.float32)).astype(h.dtype)


def multiscale_pool(u, w_pool, pool_scale):
    B, T, _ = u.shape
    uf = u.astype(jnp.float32).reshape(B, T, POOL_GROUPS, POOL_GROUP_DIM)
    cs = jnp.pad(jnp.cumsum(uf, axis=1), ((0, 0), (1, 0), (0, 0), (0, 0)))
    pos = jnp.arange(1, T + 1, dtype=jnp.float32)
    outs = []
    for g, w in enumerate(POOL_WINDOWS):
        c = cs[:, :, g]
        lag = jnp.pad(c, ((0, 0), (w, 0), (0, 0)))[:, 1:T + 1]
        mean = (c[:, 1:] - lag) / jnp.minimum(pos, float(w))[None, :, None]
        outs.append(mean - uf[:, :, g])
    d = jnp.stack(outs, axis=2)
    y = jnp.einsum('btgc,gcd->btgd', d, w_pool.astype(jnp.float32)).reshape(B, T, POOL_WIDTH)
    return (y * pool_scale.astype(jnp.float32)).astype(u.dtype)


def hybrid_layer(x, g_mix, w_in, b_gates, w_qk_conv, g_head, w_pool, pool_scale, w_out,
                 g_ffn, w_up, w_ffn_conv, b_ffn_conv, w_down):
    B, T, _ = x.shape
    h = rmsnorm(x, g_mix)
    p = h @ w_in
    qk, v, o, gates, u = jnp.split(p, SPLITS, axis=-1)
    qk = jax.nn.silu(causal_dwconv(qk, w_qk_conv))
    q, k = qk[..., :QK_WIDTH], qk[..., QK_WIDTH:]
    gates = gates + b_gates
    i_pre, f_pre = gates[..., :MLSTM_HEADS], gates[..., MLSTM_HEADS:]
    hm = mlstm_chunkwise(q.reshape(B, T, MLSTM_HEADS, MLSTM_QK_DIM),
                         k.reshape(B, T, MLSTM_HEADS, MLSTM_QK_DIM),
                         v.reshape(B, T, MLSTM_HEADS, MLSTM_V_DIM), i_pre, f_pre)
    hm = head_rmsnorm(hm, g_head) * jax.nn.sigmoid(o)
    hp = multiscale_pool(u, w_pool, pool_scale)
    x = x + jnp.concatenate([hm, hp], axis=-1) @ w_out
    h = rmsnorm(x, g_ffn)
    up = causal_dwconv(h @ w_up, w_ffn_conv) + b_ffn_conv
    gate, val = up[..., :D_FF], up[..., D_FF:]
    return x + (jax.nn.silu(gate) * val) @ w_down


def setup_inputs(seed: int = 0) -> dict:
    key = jax.random.key(seed)
    ks = jax.random.split(key, 16)
    f32 = jnp.float32
    nrm = lambda k, shape, s: (jax.random.normal(k, shape, f32) * s).astype(f32)
    f_bias = jnp.broadcast_to(jnp.linspace(3.0, 6.0, MLSTM_HEADS, dtype=f32), (DEPTH, MLSTM_HEADS))
    b_gates = jnp.concatenate([nrm(ks[3], (DEPTH, MLSTM_HEADS), 0.1),
                               f_bias + nrm(ks[4], (DEPTH, MLSTM_HEADS), 0.1)], axis=-1)
    return {
        "x": nrm(ks[0], (BATCH, SEQ, D_MODEL), 1.0),
        "mix_norm": 1.0 + nrm(ks[1], (DEPTH, D_MODEL), 0.02),
        "w_in": nrm(ks[2], (DEPTH, D_MODEL, IN_WIDTH), D_MODEL ** -0.5),
        "b_gates": b_gates,
        "w_qk_conv": nrm(ks[5], (DEPTH, QK_CONV, 2 * QK_WIDTH), QK_CONV ** -0.5),
        "head_norm": 1.0 + nrm(ks[6], (DEPTH, MLSTM_WIDTH), 0.02),
        "w_pool": nrm(ks[7], (DEPTH, POOL_GROUPS, POOL_GROUP_DIM, POOL_GROUP_DIM), POOL_GROUP_DIM ** -0.5),
        "pool_scale": 1.0 + nrm(ks[8], (DEPTH, POOL_WIDTH), 0.02),
        "w_out": nrm(ks[9], (DEPTH, MIX_WIDTH, D_MODEL), MIX_WIDTH ** -0.5),
        "ffn_norm": 1.0 + nrm(ks[10], (DEPTH, D_MODEL), 0.02),
        "w_up": nrm(ks[11], (DEPTH, D_MODEL, 2 * D_FF), D_MODEL ** -0.5),
        "w_ffn_conv": nrm(ks[12], (DEPTH, FFN_CONV, 2 * D_FF), FFN_CONV ** -0.5),
        "b_ffn_conv": nrm(ks[13], (DEPTH, 2 * D_FF), 0.02),
        "w_down": nrm(ks[14], (DEPTH, D_FF, D_MODEL), D_FF ** -0.5),
        "final_norm": 1.0 + nrm(ks[15], (D_MODEL,), 0.02),
    }


def reference(x, mix_norm, w_in, b_gates, w_qk_conv, head_norm, w_pool, pool_scale, w_out,
              ffn_norm, w_up, w_ffn_conv, b_ffn_conv, w_down, final_norm):
    for l in range(DEPTH):
        x = hybrid_layer(x, mix_norm[l], w_in[l], b_gates[l], w_qk_conv[l], head_norm[l],
                         w_pool[l], pool_scale[l], w_out[l], ffn_norm[l], w_up[l],
                         w_ffn_conv[l], b_ffn_conv[l], w_down[l])
    return rmsnorm(x, final_norm)
```

```python
import numpy as np
import concourse.bass as bass
import concourse.mybir as mybir
from concourse.bass_utils import run_bass_kernel_spmd

F32 = mybir.dt.float32
BF16 = mybir.dt.bfloat16
AF = mybir.ActivationFunctionType
ALU = mybir.AluOpType
AX = mybir.AxisListType

D = 1024
KT = 8
NTL = 18
NT = NTL * 128
INW = 2056
DFF = 2816
FT = 22
EPS = 1e-6
GROUPS = [(0, 2), (2, 6), (6, 10), (10, 14), (14, 18)]
FCH = [(0, 4), (4, 8), (8, 12), (12, 16), (16, 19), (19, 22)]
VW = 232
V_GMIX, V_QKC, V_BG, V_GH, V_PS, V_GF, V_FC, V_FB, V_FIN = 0, 8, 24, 32, 36, 40, 48, 180, 224
SW = 258


class Prog:
    ENGS = ("pe", "act", "dve", "pool", "sp")

    def __init__(self, nc):
        self.nc = nc
        self.ops = []
        self.last_w = {}
        self.readers = {}
        self.final = []
        self.bar = set()
        self.bar_seen = set()

    def barrier(self):
        last = {}
        for i, o in enumerate(self.ops):
            if o["dma"]:
                last[("d", o["semkey"])] = i
            else:
                last[("e", o["eng"])] = i
        self.bar = set(last.values())
        self.bar_seen = set()

    def op(self, eng, fn, r=(), w=(), dma=False, semkey=None):
        i = len(self.ops)
        deps = set()
        if self.bar and eng not in self.bar_seen:
            deps |= self.bar
            self.bar_seen.add(eng)
        for k in r:
            if k in self.last_w:
                deps.add(self.last_w[k])
        for k in w:
            if k in self.last_w:
                deps.add(self.last_w[k])
            for rd in self.readers.get(k, ()):
                deps.add(rd)
        deps.discard(i)
        self.ops.append(dict(eng=eng, fn=fn, deps=deps, dma=dma, semkey=semkey, val=None))
        for k in r:
            lst = self.readers.setdefault(k, [])
            if not dma:
                lst[:] = [j for j in lst if self.ops[j]["dma"] or self.ops[j]["eng"] != eng]
            lst.append(i)
        for k in w:
            self.last_w[k] = i
            self.readers[k] = []
        return i

    def barrier_keys(self, newkeys, oldkeys):
        deps = set()
        for k in oldkeys:
            if k in self.last_w:
                deps.add(self.last_w[k])
            for rd in self.readers.get(k, ()):
                deps.add(rd)
        self._alias = getattr(self, "_alias", {})
        for k in newkeys:
            self._alias.setdefault(k, set()).update(deps)

    def emit(self):
        nc = self.nc
        ops = self.ops
        alias = getattr(self, "_alias", {})

        def skip(o, d):
            return (not o["dma"]) and (not d["dma"]) and o["eng"] == "pe" and d["eng"] == "pe"

        needed = set()
        for o in ops:
            for di in o["deps"]:
                if not skip(o, ops[di]):
                    needed.add(di)
        for i in self.final:
            needed.add(i)
        cnt = {e: 0 for e in self.ENGS}
        dcnt = {}
        for i, o in enumerate(ops):
            if o["dma"]:
                k = o["semkey"]
                dcnt[k] = dcnt.get(k, 0) + 16
                o["val"] = dcnt[k]
            elif i in needed:
                cnt[o["eng"]] += 1
                o["val"] = cnt[o["eng"]]
        import contextlib

        with contextlib.ExitStack() as st:
            esem = {e: st.enter_context(nc.semaphore("s_" + e)) for e in self.ENGS}
            dsem = {}
            for k in dcnt:
                dsem[k] = st.enter_context(nc.semaphore("d_%d" % len(dsem)))
            block = st.enter_context(nc.Block())

            def section(ename):
                def body(eng):
                    waited = {}

                    def wait_for(di):
                        d = ops[di]
                        if d["dma"]:
                            s = dsem[d["semkey"]]
                            key = ("d", d["semkey"])
                        else:
                            s = esem[d["eng"]]
                            key = ("e", d["eng"])
                        if waited.get(key, 0) >= d["val"]:
                            return
                        eng.wait_ge(s, d["val"])
                        waited[key] = d["val"]

                    for i, o in enumerate(ops):
                        if o["eng"] != ename:
                            continue
                        for di in sorted(o["deps"]):
                            if skip(o, ops[di]):
                                continue
                            wait_for(di)
                        ins = o["fn"](eng)
                        if o["val"] is not None:
                            if o["dma"]:
                                ins.then_inc(dsem[o["semkey"]], 16)
                            else:
                                ins.then_inc(esem[ename], 1)
                    if ename == "sp":
                        for i in self.final:
                            wait_for(i)

                return body

            block.tensor(section("pe"))
            block.scalar(section("act"))
            block.vector(section("dve"))
            block.gpsimd(section("pool"))
            block.sync(section("sp"))


def mm(P, out, lhsT, rhs, start, stop, r, w):
    return P.op("pe", lambda e: e.matmul(out, lhsT, rhs, start=start, stop=stop), r, w)


def tr(P, out, in_, ident, r, w):
    return P.op("pe", lambda e: e.transpose(out, in_, ident), r, w)


def act(P, out, in_, func, r, w, bias=0.0, scale=1.0):
    return P.op("act", lambda e: e.activation(out, in_, func, bias=bias, scale=scale), r, w)


def tt(P, eng, out, in0, in1, op, r, w):
    return P.op(eng, lambda e: e.tensor_tensor(out, in0, in1, op), r, w)


def stt(P, out, in0, scalar, in1, op0, op1, r, w):
    return P.op("dve", lambda e: e.scalar_tensor_tensor(out, in0, scalar, in1, op0, op1), r, w)


def ts(P, eng, out, in0, s1, s2, op0, op1, r, w):
    if s2 is None:
        return P.op(eng, lambda e: e.tensor_scalar(out, in0, s1, None, op0), r, w)
    return P.op(eng, lambda e: e.tensor_scalar(out, in0, s1, s2, op0, op1), r, w)


def cp(P, eng, out, in_, r, w):
    if eng == "act":
        return P.op("act", lambda e: e.copy(out, in_), r, w)
    return P.op(eng, lambda e: e.tensor_copy(out, in_), r, w)


def dma(P, eng, out, in_, r, w, semkey):
    return P.op(eng, lambda e: e.dma_start(out=out, in_=in_), r, w, dma=True, semkey=semkey)


def xkeys(a, b):
    return [("x", t) for t in range(a // 128, (b - 1) // 128 + 1)]


class Builder:
    def __init__(self, mode, layer, final):
        self.mode = mode
        self.layer = layer
        self.final = final
        self.nc = bass.Bass("TRN2", target_bir_lowering=False)
        self.P = Prog(self.nc)

    def declare(self):
        nc = self.nc
        dt = nc.dram_tensor
        self.d_x = dt("x_loc", [NT, D], F32, kind="ExternalInput").ap()
        self.d_consts = dt("consts", [128, 512], F32, kind="ExternalInput").ap()
        self.d_percore = dt("percore", [128, 80], F32, kind="ExternalInput").ap()
        self.d_vecs = dt("vecs", [2, 128, VW], F32, kind="ExternalInput").ap()
        self.d_w_in = dt("w_in", [2, D, INW], F32, kind="ExternalInput").ap()
        if self.mode == "main":
            self.d_w_pool = dt("w_pool", [2, 4, 128, 128], F32, kind="ExternalInput").ap()
            self.d_w_out = dt("w_out", [2, D, D], F32, kind="ExternalInput").ap()
            self.d_w_up = dt("w_up", [2, D, 2 * DFF], F32, kind="ExternalInput").ap()
            self.d_w_down = dt("w_down", [2, DFF, D], F32, kind="ExternalInput").ap()
            self.d_state_in = dt("state_in", [128, SW], F32, kind="ExternalInput").ap()
            self.d_out = dt("y", [2048, D], F32, kind="ExternalOutput").ap()
        else:
            self.d_state_out = dt("state_out", [128, SW], F32, kind="ExternalOutput").ap()

    def carve(self, nbytes_dtype, shape):
        dtype = nbytes_dtype
        n = int(np.prod(shape[1:]))
        nb = n * (4 if dtype == F32 else 2)
        nb = (nb + 63) // 64 * 64
        off = self.u_off
        assert off + nb <= self.u_bytes, ("U overflow", off, nb, self.u_bytes)
        self.u_off += nb
        ap = self.U[:, off // 2: off // 2 + nb // 2]
        if dtype == F32:
            ap = ap.bitcast(F32)
        ap = ap[:, 0:n]
        if len(shape) == 3:
            ap = ap.rearrange("p (a b) -> p a b", a=shape[1])
        elif len(shape) == 4:
            ap = ap.rearrange("p (a b c) -> p a b c", a=shape[1], b=shape[2])
        return ap

    def vec(self, off, n=1):
        return self.vecs[:, self.layer, off:off + n]

    def rmsnorm(self, a, b, gv_off, out_ap, out_keys, okey_r=()):
        P = self.P
        W = b - a
        xk = xkeys(a, b)
        bank = self.mmbank()
        ssps = self.ps[:, bank, 0:W]
        nsub = W // 128
        for j in range(nsub):
            sb = self.sq_i % 2
            self.sq_i += 1
            sq = self.sq[sb]
            act(P, sq, self.x[:, :, a + j * 128: a + (j + 1) * 128], AF.Square, xk, [("sq", sb)])
            for kt in range(KT):
                mm(P, self.ps[:, bank, j * 128:(j + 1) * 128], self.onesb, sq[:, kt, :], kt == 0, kt == KT - 1,
                   [("sq", sb), "onesb"], [("ps", bank)])
        act(P, self.rstd[:, 0:W], ssps, AF.Ln, [("ps", bank)], ["rstd"], bias=self.epsb[:, 0:1], scale=1.0 / D)
        act(P, self.rstd[:, 0:W], self.rstd[:, 0:W], AF.Exp, ["rstd"], ["rstd"], scale=-0.5)
        for kt in range(KT):
            stt(P, out_ap[:, kt, 0:W], self.x[:, kt, a:b], self.vec(gv_off + kt), self.rstd[:, 0:W],
                ALU.mult, ALU.mult, xk + ["rstd", "vecs"] + list(okey_r), out_keys)

    def mmbank(self):
        b = self.mm_i % self.n_mm
        self.mm_i += 1
        return self.mm_banks[b]

    def load_common(self):
        P = self.P
        dma(P, "sp", self.cf, self.d_consts, [], ["cf"], "cf")
        dma(P, "sp", self.percore, self.d_percore, [], ["percore"], "percore")
        dma(P, "sp", self.vecs, self.d_vecs.rearrange("l p v -> p l v"), [], ["vecs"], "vecs")
        cp(P, "dve", self.identb, self.cf[:, 0:128], ["cf"], ["identb"])
        cp(P, "dve", self.onesb, self.cf[:, 256:384], ["cf"], ["onesb"])
        P.op("dve", lambda e: e.memset(self.epsb, EPS), [], ["epsb"])

    def load_x(self):
        P = self.P
        for t in range(NTL):
            b = t % 2
            xt = self.xtok[b]
            xk_ = getattr(self, "xtok_keys", [("xtok", 0), ("xtok", 1)])[b]
            dma(P, "sp", xt, self.d_x[t * 128:(t + 1) * 128, :], [], [xk_], ("xtok", b))
            banks = (0, 1) if b == 0 else (2, 3)
            for kt in range(KT):
                bk = banks[kt // 4]
                tr(P, self.ps[:, bk, (kt % 4) * 128:(kt % 4 + 1) * 128], xt[:, kt * 128:(kt + 1) * 128],
                   self.cf[:, 0:128], [xk_, "cf"], [("ps", bk)])
            for hf in range(2):
                bk = banks[hf]
                src = self.ps[:, bk, :].rearrange("p (a b) -> p a b", a=4)
                dst = self.x[:, hf * 4:(hf + 1) * 4, t * 128:(t + 1) * 128]
                eng = "act" if hf == 0 else "dve"
                cp(P, eng, dst, src, [("ps", bk)], [("x", t)])

    def load_w_in(self, l, only_k_v_g=False):
        P = self.P
        src = self.d_w_in[l].rearrange("(kt p) n -> p kt n", p=128)
        dma(P, "pool", self.wA, src[:, :, 0:1024], [], ["wA"], "wA")
        if only_k_v_g:
            dma(P, "pool", self.wB[:, :, 512:520], src[:, :, 1536:1544], [], ["wB"], "wB")
        else:
            dma(P, "pool", self.wB, src[:, :, 1024:INW], [], ["wB"], "wB")

    def qk_tile(self, g, ct, W, dst, dst_key):
        P = self.P
        bank = self.mmbank()
        for kt in range(KT):
            mm(P, self.ps[:, bank, 0:W], self.wA[:, kt, ct * 128:(ct + 1) * 128], self.hT[:, kt, 0:W],
               kt == 0, kt == KT - 1, ["wA", "hT"], [("ps", bank)])
        pb = self.pre_i % 2
        self.pre_i += 1
        pre = self.pre[pb]
        acc = self.acc[pb]
        cp(P, "act", pre[:, 3:3 + W], self.ps[:, bank, 0:W], [("ps", bank)], [("pre", pb)])
        cp(P, "pool", pre[:, 0:3], self.qkcarry[:, ct, :], [("qkc", ct)], [("pre", pb)])
        cp(P, "pool", self.qkcarry[:, ct, :], pre[:, W:W + 3], [("pre", pb)], [("qkc", ct)])
        wv = lambda j: self.vec(V_QKC + ct * 4 + j)
        ts(P, "dve", acc[:, 0:W], pre[:, 3:3 + W], wv(3), None, ALU.mult, None, [("pre", pb), "vecs"], [("acc", pb)])
        for j in (2, 1, 0):
            stt(P, acc[:, 0:W], pre[:, j:j + W], wv(j), acc[:, 0:W], ALU.mult, ALU.add,
                [("pre", pb), ("acc", pb), "vecs"], [("acc", pb)])
        if dst is None:
            for e in range(2):
                es = slice(e * 64, (e + 1) * 64)
                act(P, self.qTz[es, ct, e, 0:W], acc[es, 0:W], AF.Silu, [("acc", pb)], [dst_key])
        else:
            act(P, dst, acc[:, 0:W], AF.Silu, [("acc", pb)], [dst_key])

    def gates_group(self, g, t0, ntg):
        P = self.P
        G3 = ("ps", 3)
        for j in range(ntg):
            for kt in range(KT):
                mm(P, self.ps[:, 3, 32 + j * 8: 40 + j * 8], self.hT[:, kt, j * 128:(j + 1) * 128],
                   self.wB[:, kt, 512:520], kt == 0, kt == KT - 1, ["hT", "wB"], [G3])
        gv = self.ps[:, 3, 32:32 + ntg * 8].rearrange("p (a b) -> p a b", a=ntg)
        bg = self.vecs[:, self.layer, V_BG:V_BG + 8].unsqueeze(1).to_broadcast([128, ntg, 8])
        tt(P, "dve", self.gsb[:, 0:ntg, :], gv, bg, ALU.add, [G3, "vecs"], ["gsb"])
        act(P, self.lfn[:, 0:ntg, :], self.gsb[:, 0:ntg, 4:8], AF.Exp, ["gsb"], ["lfn"], scale=-1.0)
        act(P, self.lfn[:, 0:ntg, :], self.lfn[:, 0:ntg, :], AF.Ln, ["lfn"], ["lfn"], bias=1.0)
        lf2 = self.lfn[:, 0:ntg, :]
        mm(P, self.ps[:, 3, 0:ntg * 4], self.cf[:, 128:256], lf2, True, True, ["cf", "lfn"], [G3])
        mm(P, self.ps[:, 3, 16:16 + ntg * 4], self.cf[:, 256:384], lf2, True, True, ["cf", "lfn"], [G3])
        bneg = self.ps[:, 3, 0:ntg * 4].rearrange("p (a b) -> p a b", a=ntg)
        totn = self.ps[:, 3, 16:16 + ntg * 4].rearrange("p (a b) -> p a b", a=ntg)
        tt(P, "dve", self.cS[:, t0:t0 + ntg, :], self.gsb[:, 0:ntg, 0:4], bneg, ALU.add, [G3, "gsb"], ["cS"])
        act(P, self.cS[:, t0:t0 + ntg, :], self.cS[:, t0:t0 + ntg, :], AF.Exp, ["cS"], ["cS"])
        act(P, self.emb[:, t0:t0 + ntg, :], bneg, AF.Exp, [G3], ["emb"])
        for e in range(2):
            sl = slice(e * 64, (e + 1) * 64)
            act(P, self.Gp[sl, t0:t0 + ntg, :], totn[sl, :, e::2], AF.Exp, [G3], ["Gp"], scale=-1.0)

    def v_tile(self, j, t):
        P = self.P
        bank = self.mmbank()
        for kt in range(KT):
            mm(P, self.ps[:, bank, :], self.hT[:, kt, j * 128:(j + 1) * 128], self.wA[:, kt, 512:1024],
               kt == 0, kt == KT - 1, ["hT", "wA"], [("ps", bank)])
        src = self.ps[:, bank, :].rearrange("p (a b) -> p a b", a=4)
        cb = self.cS[:, t, :].unsqueeze(2).to_broadcast([128, 4, 128])
        vb = j % 4
        tt(P, "dve", self.vt[:, vb, :, 0:128], src, cb, ALU.mult, [("ps", bank), "cS"], [("vt", vb)])
        cp(P, "dve", self.vt[:, vb, :, 128:129], self.cS[:, t, :].unsqueeze(2), ["cS"], [("vt", vb)])

    def k_transpose(self, j):
        P = self.P
        T5 = ("ps", 5)
        tp = self.ps[:, 5, :].bitcast(BF16)
        for pr in range(2):
            tr(P, tp[:, pr * 128:(pr + 1) * 128], self.kTw[:, pr, j * 128:(j + 1) * 128], self.identb,
               ["kTw", "identb"], [T5])
        cp(P, "act", self.ktok[:, j % 4, :], tp[:, 0:256], [T5], [("ktok", j % 4)])

    def kv_update(self, j, t):
        P = self.P
        vb = j % 4
        for h in range(4):
            pr, e = h // 2, h % 2
            bk = 6 + pr
            mm(P, self.ps[e * 64:(e + 1) * 64, bk, 258:387], self.ktok[:, j % 4, h * 64:(h + 1) * 64],
               self.vt[:, vb, h, :], True, True, [("ktok", j % 4), ("vt", vb)], [("psk", bk)])
        kv = self.ps[:, 6:8, 258:387]
        tt(P, "dve", self.Cst, self.Cst, kv, ALU.add, ["Cst", ("psk", 6), ("psk", 7)], ["Cst"])
        gb = self.Gp[:, t, :].unsqueeze(2).to_broadcast([128, 2, 129])
        tt(P, "dve", self.Cst, self.Cst, gb, ALU.mult, ["Cst", "Gp"], ["Cst"])
        if self.mode == "main":
            act(P, self.Cb, self.Cst, AF.Copy, ["Cst"], ["Cb"], scale=0.125)


    def o_tile(self, i, W):
        P = self.P
        bank = self.mmbank()
        for kt in range(KT):
            mm(P, self.ps[:, bank, 0:W], self.wB[:, kt, i * 128:(i + 1) * 128], self.hT[:, kt, 0:W],
               kt == 0, kt == KT - 1, ["wB", "hT"], [("ps", bank)])
        act(P, self.sigo[:, i, 0:W], self.ps[:, bank, 0:W], AF.Sigmoid, [("ps", bank)], ["sigo"])

    def u_tile(self, g, gi, W):
        P = self.P
        bank = self.mmbank()
        for kt in range(KT):
            mm(P, self.ps[:, bank, 0:W], self.wB[:, kt, 520 + gi * 128:520 + (gi + 1) * 128], self.hT[:, kt, 0:W],
               kt == 0, kt == KT - 1, ["wB", "hT"], [("ps", bank)])
        ub, sA, sB = self.pre[0], self.pre[1], self.acc[0]
        kU, kA, kB = ("pre", 0), ("pre", 1), ("acc", 0)
        cp(P, "act", ub[:, 16:16 + W], self.ps[:, bank, 0:W], [("ps", bank)], [kU])
        cp(P, "pool", ub[:, 0:16], self.ucarry[:, gi, :], [("ucar", gi)], [kU])
        cp(P, "pool", self.ucarry[:, gi, :], ub[:, W:W + 16], [kU], [("ucar", gi)])
        E = 16 + W
        tt(P, "pool", sA[:, 1:E], ub[:, 1:E], ub[:, 0:E - 1], ALU.add, [kU], [kA])
        fin, kf = sA, kA
        if gi >= 1:
            tt(P, "pool", sB[:, 3:E], sA[:, 3:E], sA[:, 1:E - 2], ALU.add, [kA], [kB])
            fin, kf = sB, kB
        if gi >= 2:
            tt(P, "pool", sA[:, 7:E], sB[:, 7:E], sB[:, 3:E - 4], ALU.add, [kB], [kA])
            fin, kf = sA, kA
        if gi >= 3:
            tt(P, "pool", sB[:, 15:E], sA[:, 15:E], sA[:, 7:E - 8], ALU.add, [kA], [kB])
            fin, kf = sB, kB
        w = float(2 ** (gi + 1))
        db = self.dT_i % 2
        self.dT_i += 1
        dT = self.dT[db]
        stt(P, dT[:, 0:W], fin[:, 16:16 + W], 1.0 / w, ub[:, 16:16 + W], ALU.mult, ALU.subtract,
            [kf, kU], [("dT", db)])
        if g == 1:
            tt(P, "dve", self.tmp16, fin[:, 16:32], self.percore[:, gi * 16:(gi + 1) * 16], ALU.mult,
               [kf, "percore"], ["tmp16"])
            tt(P, "dve", dT[:, 0:16], self.tmp16, ub[:, 16:32], ALU.subtract, ["tmp16", kU], [("dT", db)])
        bank2 = self.mmbank()
        mm(P, self.ps[:, bank2, 0:W], self.wpool[:, gi, :], dT[:, 0:W], True, True, ["wpool", ("dT", db)],
           [("ps", bank2)])
        act(P, self.hpT[:, gi, 0:W], self.ps[:, bank2, 0:W], AF.Copy, [("ps", bank2), "vecs"], ["hpT"],
            scale=self.vec(V_PS + gi))

    def s2_tile(self, j, t):
        P = self.P
        js = slice(j * 128, (j + 1) * 128)
        vb = j % 4
        for h in range(4):
            pr, e = h // 2, h % 2
            es = slice(e * 64, (e + 1) * 64)
            mm(P, self.ps[:, 4, h * 128:(h + 1) * 128], self.kTw[:, pr, js], self.qTz[:, pr, e, js], True, True,
               ["kTw", "qTw"], [("ps", 4)])
        ptv = self.ps[:, 4, :].rearrange("p (a b) -> p a b", a=4)
        mb = self.cf[:, 384:512].unsqueeze(1).to_broadcast([128, 4, 128])
        tt(P, "dve", self.PTm, ptv, mb, ALU.mult, [("ps", 4), "cf"], ["PTm"])
        for h in range(4):
            pr, e = h // 2, h % 2
            es = slice(e * 64, (e + 1) * 64)
            bk = 6 + pr
            o = self.ps[:, bk, e * 129:(e + 1) * 129]
            mm(P, o, self.PTm[:, h, :], self.vt[:, vb, h, :], True, False, ["PTm", ("vt", vb)], [("psn", bk)])
            mm(P, o, self.qTz[:, pr, e, js], self.Cb[:, pr, :], False, True, ["qTw", "Cb"], [("psn", bk)])
        self.kv_update(j, t)
        N4 = self.ps[:, 6:8, 0:258].rearrange("p a (e c) -> p a e c", e=2)
        NK = [("psn", 6), ("psn", 7)]
        sm = self.small
        v22 = lambda c0: sm[:, c0:c0 + 4].rearrange("p (a e) -> p a e", a=2)
        den, rden, ssq, t1, t2, scl = v22(0), v22(4), v22(8), v22(12), v22(16), v22(20)
        embv = self.emb[:, t, :].rearrange("p (a e) -> p a e", a=2)
        act(P, den, N4[:, :, :, 128], AF.Abs, NK, ["den"])
        tt(P, "dve", den, den, embv, ALU.max, ["den", "emb"], ["den"])
        P.op("dve", lambda e_: e_.reciprocal(rden, den), ["den"], ["rden"])
        sq4 = self.sqN.rearrange("p (a e) d -> p a e d", a=2)
        act(P, sq4, N4[:, :, :, 0:128], AF.Square, NK, ["sqN"])
        P.op("dve", lambda e_: e_.tensor_reduce(ssq, sq4, AX.X, ALU.add), ["sqN"], ["ssq"])
        tt(P, "dve", t1, rden, rden, ALU.mult, ["rden"], ["t1"])
        tt(P, "dve", t2, ssq, t1, ALU.mult, ["ssq", "t1"], ["t2"])
        act(P, t2, t2, AF.Ln, ["t2"], ["t2"], bias=self.epsb[:, 0:1], scale=1.0 / 128.0)
        act(P, t2, t2, AF.Exp, ["t2"], ["t2"], scale=-0.5)
        tt(P, "dve", scl, rden, t2, ALU.mult, ["rden", "t2"], ["scl"])
        hn4 = self.hn.rearrange("p (a e) d -> p a e d", a=2)
        tt(P, "dve", hn4, N4[:, :, :, 0:128], scl.unsqueeze(3).to_broadcast([128, 2, 2, 128]), ALU.mult,
           NK + ["scl"], ["hn"])
        tp = self.ps[:, 5, :].bitcast(BF16)
        T5 = ("ps", 5)
        for h in range(4):
            tr(P, tp[:, 256 + h * 128:256 + (h + 1) * 128], self.hn[:, h, :], self.identb, ["hn", "identb"], [T5])
        for h in range(4):
            stt(P, self.sigo[:, h, js], tp[:, 256 + h * 128:256 + (h + 1) * 128], self.vec(V_GH + h),
                self.sigo[:, h, js], ALU.mult, ALU.mult, [T5, "sigo", "vecs"], ["sigo"])

    def o_group(self, a, b):
        P = self.P
        W = b - a
        for dt_ in range(8):
            bank = self.mmbank()
            for kt in range(KT):
                rhs = self.sigo[:, kt, 0:W] if kt < 4 else self.hpT[:, kt - 4, 0:W]
                mm(P, self.ps[:, bank, 0:W], self.wout[:, kt, dt_ * 128:(dt_ + 1) * 128], rhs, kt == 0, kt == KT - 1,
                   ["wout", "sigo", "hpT"], [("ps", bank)])
            tt(P, "dve", self.x[:, dt_, a:b], self.ps[:, bank, 0:W], self.x[:, dt_, a:b], ALU.add,
               [("ps", bank)] + xkeys(a, b), xkeys(a, b))

    def load_ffn_chunk(self, l, c, slot):
        P = self.P
        f0, f1 = FCH[c]
        T = f1 - f0
        su = self.d_w_up[l].rearrange("(kt p) n -> p kt n", p=128)
        dma(P, "pool", self.wu[slot][:, :, 0:T * 128], su[:, :, f0 * 128:f1 * 128], [], [("wu", slot)], ("wu", slot))
        dma(P, "pool", self.wu[slot][:, :, 512:512 + T * 128], su[:, :, DFF + f0 * 128:DFF + f1 * 128], [],
            [("wu", slot)], ("wu", slot))
        sd = self.d_w_down[l].rearrange("(i p) n -> p i n", p=128)
        dma(P, "pool", self.wd[slot][:, 0:T, :], sd[:, f0:f1, :], [], [("wd", slot)], ("wd", slot))

    def ffn_conv(self, bank, Wo, f, acc, kacc):
        P = self.P
        Wn = Wo + 2
        act(P, acc[:, 0:Wo], self.ps[:, bank, 2:Wn], AF.Identity, [("ps", bank), "vecs"], [kacc],
            bias=self.vec(V_FB + f), scale=self.vec(V_FC + f * 3 + 2))
        stt(P, acc[:, 0:Wo], self.ps[:, bank, 1:Wn - 1], self.vec(V_FC + f * 3 + 1), acc[:, 0:Wo], ALU.mult, ALU.add,
            [("ps", bank), kacc, "vecs"], [kacc])
        stt(P, acc[:, 0:Wo], self.ps[:, bank, 0:Wo], self.vec(V_FC + f * 3 + 0), acc[:, 0:Wo], ALU.mult, ALU.add,
            [("ps", bank), kacc, "vecs"], [kacc])

    def ffn_phase(self, l, tok0):
        P = self.P
        n = NT - tok0
        nwin = -(-n // 510)
        base = n // nwin
        wins = []
        s = tok0
        for i in range(nwin):
            Wo = base + (1 if i < n - base * nwin else 0)
            wins.append((s, Wo))
            s += Wo
        assert s == NT
        pair_i = 0
        dn_i = 0
        ab_i = 0
        for c, (f0, f1) in enumerate(FCH):
            slot = c % 2
            T = f1 - f0
            for (s, Wo) in wins:
                Wn = Wo + 2
                ab = ab_i % 2
                ab_i += 1
                for i in range(T):
                    pb = pair_i % 2
                    pair_i += 1
                    bg, bv = (0, 1) if pb == 0 else (2, 3)
                    for kt in range(KT):
                        mm(P, self.ps[:, bg, 0:Wn], self.wu[slot][:, kt, i * 128:(i + 1) * 128],
                           self.hT2[:, kt, s:s + Wn], kt == 0, kt == KT - 1, [("wu", slot), "hT2"], [("ps", bg)])
                    for kt in range(KT):
                        mm(P, self.ps[:, bv, 0:Wn], self.wu[slot][:, kt, 512 + i * 128:512 + (i + 1) * 128],
                           self.hT2[:, kt, s:s + Wn], kt == 0, kt == KT - 1, [("wu", slot), "hT2"], [("ps", bv)])
                    self.ffn_conv(bg, Wo, f0 + i, self.accg[pb], ("accg", pb))
                    self.ffn_conv(bv, Wo, FT + f0 + i, self.accv[pb], ("accv", pb))
                    act(P, self.sg[pb][:, 0:Wo], self.accg[pb][:, 0:Wo], AF.Silu, [("accg", pb)], [("sg", pb)])
                    tt(P, "dve", self.actb[ab][:, i, 0:Wo], self.sg[pb][:, 0:Wo], self.accv[pb][:, 0:Wo], ALU.mult,
                       [("sg", pb), ("accv", pb)], [("actb", ab)])
                for dt_ in range(8):
                    bank = 4 + dn_i % 4
                    dn_i += 1
                    for i in range(T):
                        mm(P, self.ps[:, bank, 0:Wo], self.wd[slot][:, i, dt_ * 128:(dt_ + 1) * 128],
                           self.actb[ab][:, i, 0:Wo], i == 0, i == T - 1, [("wd", slot), ("actb", ab)], [("ps", bank)])
                    tt(P, "dve", self.x[:, dt_, s:s + Wo], self.ps[:, bank, 0:Wo], self.x[:, dt_, s:s + Wo], ALU.add,
                       [("ps", bank)] + xkeys(s, s + Wo), xkeys(s, s + Wo))
            if c + 2 < len(FCH):
                self.load_ffn_chunk(l, c + 2, slot)

    def build_main(self):
        nc, P = self.nc, self.P
        l = self.layer
        self.declare()
        import contextlib

        with contextlib.ExitStack() as st:
            sb = lambda name, shape, dt_: st.enter_context(nc.sbuf_tensor("sb_" + name, shape, dt_))[:]
            self.x = sb("x", [128, KT, NT], F32)
            self.cf = sb("cf", [128, 512], F32)
            self.percore = sb("percore", [128, 80], F32)
            self.vecs = sb("vecs", [128, 2, VW], F32)
            self.identb = sb("identb", [128, 128], BF16)
            self.onesb = sb("onesb", [128, 128], BF16)
            self.epsb = sb("epsb", [128, 1], F32)
            self.wA = sb("wA", [128, KT, 1024], BF16)
            self.wu = [sb("wu0", [128, KT, 1024], BF16), None]
            self.wd = [sb("wd0", [128, 4, 1024], BF16), None]
            self.qkcarry = sb("qkcarry", [128, 4, 3], F32)
            self.ucarry = sb("ucarry", [128, 4, 16], F32)
            self.gsb = sb("gsb", [128, 4, 8], F32)
            self.lfn = sb("lfn", [128, 4, 4], F32)
            self.cS = sb("cS", [128, NTL, 4], F32)
            self.emb = sb("emb", [128, NTL, 4], F32)
            self.Gp = sb("Gp", [128, NTL, 2], F32)
            self.Cst = sb("Cst", [128, 2, 129], F32)
            self.Cb = sb("Cb", [128, 2, 129], BF16)
            self.small = sb("small", [128, 32], F32)
            self.tmp16 = sb("tmp16", [128, 16], F32)
            self.u_bytes = (nc.sbuf_bytes_remaining - 256) // 64 * 64
            self.U = sb("U", [128, self.u_bytes // 2], BF16)
            self.ps = st.enter_context(nc.psum_tensor("ps", [128, 8, 512], F32))[:]
            self.mm_banks, self.n_mm, self.mm_i = [0, 1, 2], 3, 0
            self.sq_i = self.pre_i = self.dT_i = 0

            self.u_off = 0
            cv = self.carve
            self.wB = cv(BF16, [128, KT, 1032])
            self.wout = cv(BF16, [128, KT, 1024])
            self.wpool = cv(BF16, [128, 4, 128])
            self.sq = [cv(BF16, [128, KT, 128]) for _ in range(2)]
            self.rstd = cv(F32, [128, 512])
            self.hT = cv(BF16, [128, KT, 512])
            self.pre = [cv(F32, [128, 528]) for _ in range(2)]
            self.acc = [cv(F32, [128, 528]) for _ in range(2)]
            self.qTz = cv(BF16, [128, 2, 2, 512])
            self.kTw = cv(BF16, [128, 2, 512])
            self.ktok = cv(BF16, [128, 4, 256])
            self.vt = cv(BF16, [128, 4, 4, 129])
            self.sigo = cv(BF16, [128, 4, 512])
            self.hpT = cv(BF16, [128, 4, 512])
            self.dT = [cv(BF16, [128, 512]) for _ in range(2)]
            self.PTm = cv(BF16, [128, 4, 128])
            self.hn = cv(BF16, [128, 4, 128])
            self.sqN = cv(F32, [128, 4, 128])
            self.xtok = [self.sigo.rearrange("p a b -> p (a b)").bitcast(F32),
                         self.hpT.rearrange("p a b -> p (a b)").bitcast(F32)]
            self.xtok_keys = ["sigo", "hpT"]

            self.load_common()
            self.load_w_in(l)
            so = self.d_w_out[l].rearrange("(kt p) n -> p kt n", p=128)
            dma(P, "pool", self.wout, so, [], ["wout"], "wout")
            dma(P, "pool", self.wpool, self.d_w_pool[l].rearrange("g c d -> c g d"), [], ["wpool"], "wpool")
            self.load_ffn_chunk(l, 0, 0)
            self.load_x()
            dma(P, "sp", self.Cst.rearrange("p a b -> p (a b)"), self.d_state_in, [], ["Cst"], "cst")
            act(P, self.Cb, self.Cst, AF.Copy, ["Cst"], ["Cb"], scale=0.125)
            P.op("dve", lambda e: e.memset(self.qkcarry, 0.0), [], [("qkc", c) for c in range(4)])
            P.op("dve", lambda e: e.memset(self.ucarry, 0.0), [], [("ucar", c) for c in range(4)])
            P.op("pool", lambda e: e.memset(self.qTz[64:128, :, 0, :], 0.0), [], ["qTw"])
            P.op("pool", lambda e: e.memset(self.qTz[0:64, :, 1, :], 0.0), [], ["qTw"])

            for g, (t0, t1) in enumerate(GROUPS):
                a, b = t0 * 128, t1 * 128
                W = b - a
                ntg = t1 - t0
                self.rmsnorm(a, b, V_GMIX, self.hT, ["hT"])
                for ct in range(4):
                    dst = None if ct < 2 else self.kTw[:, ct - 2, 0:W]
                    self.qk_tile(g, ct, W, dst, "qTw" if ct < 2 else "kTw")
                self.gates_group(g, t0, ntg)
                for j in range(ntg):
                    if t0 + j >= 1:
                        self.v_tile(j, t0 + j)
                        self.k_transpose(j)
                for i in range(4):
                    self.o_tile(i, W)
                for gi in range(4):
                    self.u_tile(g, gi, W)
                for j in range(ntg):
                    if t0 + j >= 1:
                        self.s2_tile(j, t0 + j)
                self.o_group(a, b)

            P.barrier()
            self.u_off = 0
            self.wu[1] = cv(BF16, [128, KT, 1024])
            self.wd[1] = cv(BF16, [128, 4, 1024])
            self.hT2 = cv(BF16, [128, KT, NT + 2])
            self.sq = [cv(BF16, [128, KT, 128]) for _ in range(2)]
            self.rstd = cv(F32, [128, 512])
            self.accg = [cv(F32, [128, 512]) for _ in range(2)]
            self.accv = [cv(F32, [128, 512]) for _ in range(2)]
            self.sg = [cv(F32, [128, 512]) for _ in range(2)]
            self.actb = [cv(BF16, [128, 4, 512]) for _ in range(2)]
            self.load_ffn_chunk(l, 1, 1)
            P.op("dve", lambda e: e.memset(self.hT2[:, :, 0:2], 0.0), [], ["hT2"])
            for g, (t0, t1) in enumerate(GROUPS):
                a, b = t0 * 128, t1 * 128
                self.rmsnorm(a, b, V_GF, self.hT2[:, :, 2 + a:2 + b], ["hT2"])
            self.ffn_phase(l, 128)

            P.barrier()
            self.u_off = 0
            self.sq = [cv(BF16, [128, KT, 128]) for _ in range(2)]
            self.rstd = cv(F32, [128, 512])
            yT = [cv(F32, [128, KT, 128]) for _ in range(2)]
            ytok = [cv(F32, [128, D]) for _ in range(2)]
            for t in range(2, NTL):
                b2 = t % 2
                a, b = t * 128, (t + 1) * 128
                if self.final:
                    self.rmsnorm(a, b, V_FIN, yT[b2], [("yT", b2)])
                banks = (4, 5) if b2 == 0 else (6, 7)
                for kt in range(KT):
                    bk = banks[kt // 4]
                    if self.final:
                        src, sk = yT[b2][:, kt, :], [("yT", b2)]
                    else:
                        src, sk = self.x[:, kt, a:b], [("x", t)]
                    tr(P, self.ps[:, bk, (kt % 4) * 128:(kt % 4 + 1) * 128], src, self.cf[:, 0:128], sk + ["cf"],
                       [("ps", bk)])
                cp(P, "act", ytok[b2][:, 0:512], self.ps[:, banks[0], :], [("ps", banks[0])], [("ytok", b2)])
                cp(P, "dve", ytok[b2][:, 512:1024], self.ps[:, banks[1], :], [("ps", banks[1])], [("ytok", b2)])
                o = dma(P, "sp", self.d_out[(t - 2) * 128:(t - 1) * 128, :], ytok[b2], [("ytok", b2)], [("yo", t)],
                        ("yout", b2))
                P.final.append(o)
            P.emit()
        return nc

    def build_state(self):
        nc, P = self.nc, self.P
        l = self.layer
        self.declare()
        import contextlib

        with contextlib.ExitStack() as st:
            sb = lambda name, shape, dt_: st.enter_context(nc.sbuf_tensor("sb_" + name, shape, dt_))[:]
            self.x = sb("x", [128, KT, NT], F32)
            self.cf = sb("cf", [128, 512], F32)
            self.percore = sb("percore", [128, 80], F32)
            self.vecs = sb("vecs", [128, 2, VW], F32)
            self.identb = sb("identb", [128, 128], BF16)
            self.onesb = sb("onesb", [128, 128], BF16)
            self.epsb = sb("epsb", [128, 1], F32)
            self.wA = sb("wA", [128, KT, 1024], BF16)
            self.wB = sb("wB", [128, KT, 1032], BF16)
            self.xtok = [sb("xtok%d" % i, [128, D], F32) for i in range(2)]
            self.sq = [sb("sq%d" % i, [128, KT, 128], BF16) for i in range(2)]
            self.rstd = sb("rstd", [128, 512], F32)
            self.hT = sb("hT", [128, KT, 512], BF16)
            self.pre = [sb("pre%d" % i, [128, 515], F32) for i in range(2)]
            self.acc = [sb("acc%d" % i, [128, 512], F32) for i in range(2)]
            self.qkcarry = sb("qkcarry", [128, 4, 3], F32)
            self.kTw = sb("kTw", [128, 2, 512], BF16)
            self.ktok = sb("ktok", [128, 4, 256], BF16)
            self.vt = sb("vt", [128, 4, 4, 129], BF16)
            self.gsb = sb("gsb", [128, 4, 8], F32)
            self.lfn = sb("lfn", [128, 4, 4], F32)
            self.cS = sb("cS", [128, NTL, 4], F32)
            self.emb = sb("emb", [128, NTL, 4], F32)
            self.Gp = sb("Gp", [128, NTL, 2], F32)
            self.Cst = sb("Cst", [128, 2, 129], F32)
            self.ps = st.enter_context(nc.psum_tensor("ps", [128, 8, 512], F32))[:]
            self.mm_banks, self.n_mm, self.mm_i = [0, 1, 2], 3, 0
            self.sq_i = 0
            self.pre_i = 0

            self.load_common()
            self.load_w_in(l, only_k_v_g=True)
            self.load_x()
            P.op("dve", lambda e: e.memset(self.qkcarry, 0.0), [], [("qkc", c) for c in range(4)])
            P.op("dve", lambda e: e.memset(self.Cst, 0.0), [], ["Cst"])
            for g, (t0, t1) in enumerate(GROUPS):
                if t0 >= 17:
                    break
                a, b = t0 * 128, t1 * 128
                W = b - a
                ntg = t1 - t0
                self.rmsnorm(a, b, V_GMIX, self.hT, ["hT"])
                for ct in (2, 3):
                    self.qk_tile(g, ct, W, self.kTw[:, ct - 2, 0:W], "kTw")
                self.gates_group(g, t0, ntg)
                for j in range(ntg):
                    t = t0 + j
                    if t < 1 or t >= 17:
                        continue
                    self.v_tile(j, t)
                    self.k_transpose(j)
                    self.kv_update(j, t)
            o = dma(P, "sp", self.d_state_out, self.Cst.rearrange("p a b -> p (a b)"), ["Cst"], ["dout"], "dout")
            P.final.append(o)
            P.emit()
        return nc


def build_main_prog(layer, final):
    b = Builder("main", layer, final)
    return b.build_main()


def build_state_prog(layer):
    b = Builder("state", layer, False)
    return b.build_state()


def make_consts():
    c = np.zeros((128, 512), np.float32)
    c[:, 0:128] = np.eye(128, dtype=np.float32)
    tri = (np.arange(128)[:, None] <= np.arange(128)[None, :]).astype(np.float32)
    c[:, 128:256] = tri
    c[:, 256:384] = 1.0
    c[:, 384:512] = tri * 0.125
    return c


def make_percore():
    pcs = []
    for c in range(8):
        pc = np.zeros((128, 80), np.float32)
        first = (c % 2 == 0)
        for g, w in enumerate((2, 4, 8, 16)):
            for i in range(16):
                pc[:, g * 16 + i] = 1.0 / (min(i + 1, w) if first else w)
        pc[:, 64] = 0.0 if first else 1.0
        if not first:
            pc[:, 65 + c - 1] = 1.0
        pcs.append(pc)
    return pcs


def make_vecs(inp):
    v = np.zeros((2, 128, VW), np.float32)
    for l in range(2):
        v[l, :, V_GMIX:V_GMIX + 8] = inp["mix_norm"][l].reshape(8, 128).T
        v[l, :, V_QKC:V_QKC + 16] = inp["w_qk_conv"][l].reshape(4, 4, 128).transpose(2, 1, 0).reshape(128, 16)
        v[l, :, V_BG:V_BG + 8] = inp["b_gates"][l][None, :]
        v[l, :, V_GH:V_GH + 4] = inp["head_norm"][l].reshape(4, 128).T
        v[l, :, V_PS:V_PS + 4] = inp["pool_scale"][l].reshape(4, 128).T
        v[l, :, V_GF:V_GF + 8] = inp["ffn_norm"][l].reshape(8, 128).T
        v[l, :, V_FC:V_FC + 132] = inp["w_ffn_conv"][l].reshape(3, 44, 128).transpose(2, 1, 0).reshape(128, 132)
        v[l, :, V_FB:V_FB + 44] = inp["b_ffn_conv"][l].reshape(44, 128).T
        v[l, :, V_FIN:V_FIN + 8] = inp["final_norm"].reshape(8, 128).T
    return v


def make_xloc(xfull):
    out = []
    for c in range(8):
        b, half = c // 2, c % 2
        xl = np.zeros((NT, D), np.float32)
        s = half * 2048
        xl[256:] = xfull[b, s:s + 2048]
        if half == 1:
            xl[:256] = xfull[b, s - 256:s]
        out.append(xl)
    return out


def host_prep(inp):
    return dict(consts=make_consts(), percore=make_percore(), vecs=make_vecs(inp), x_loc=make_xloc(inp["x"]))


def _run_layer(l, final, host, xlocs, inp):
    common = dict(consts=host["consts"], vecs=host["vecs"], w_in=inp["w_in"])
    nc_s = build_state_prog(l)
    maps = [dict(common, x_loc=xlocs[c], percore=host["percore"][c]) for c in range(8)]
    res = run_bass_kernel_spmd(nc_s, maps, core_ids=list(range(8)))
    states = [np.asarray(res.results[c]["state_out"], np.float32) for c in range(8)]
    zero = np.zeros((128, SW), np.float32)
    nc_m = build_main_prog(l, final)
    maps = [dict(common, x_loc=xlocs[c], percore=host["percore"][c], w_pool=inp["w_pool"], w_out=inp["w_out"],
                 w_up=inp["w_up"], w_down=inp["w_down"], state_in=(states[c - 1] if c % 2 == 1 else zero))
            for c in range(8)]
    res = run_bass_kernel_spmd(nc_m, maps, core_ids=list(range(8)))
    y = np.zeros((4, 4096, D), np.float32)
    for c in range(8):
        y[c // 2, (c % 2) * 2048:(c % 2 + 1) * 2048] = res.results[c]["y"]
    return y


def kernel(**inputs):
    inp = {k: np.ascontiguousarray(np.asarray(v, dtype=np.float32)) for k, v in inputs.items()}
    host = host_prep(inp)
    x1 = _run_layer(0, False, host, host["x_loc"], inp)
    y = _run_layer(1, True, host, make_xloc(x1), inp)
    return y
```

```python
import numpy as np
import concourse.bass as bass
import concourse.mybir as mybir
from concourse.bass_utils import run_bass_kernel_spmd

F32 = mybir.dt.float32
BF16 = mybir.dt.bfloat16
AF = mybir.ActivationFunctionType
ALU = mybir.AluOpType
AX = mybir.AxisListType

D = 1024
KT = 8
NTL = 18
NT = NTL * 128
INW = 2056
DFF = 2816
FT = 22
EPS = 1e-6
GROUPS = [(0, 2), (2, 6), (6, 10), (10, 14), (14, 18)]
FCH = [(0, 4), (4, 8), (8, 12), (12, 16), (16, 19), (19, 22)]
VW = 232
V_GMIX, V_QKC, V_BG, V_GH, V_PS, V_GF, V_FC, V_FB, V_FIN = 0, 8, 24, 32, 36, 40, 48, 180, 224
SW = 258


class Prog:
    ENGS = ("pe", "act", "dve", "pool", "sp")

    def __init__(self, nc):
        self.nc = nc
        self.ops = []
        self.last_w = {}
        self.readers = {}
        self.final = []
        self.phase = 0
        self.inorder_engs = ("pe", "act", "dve", "pool", "sp")

    def barrier(self):
        self.phase += 1
        self.last_w = {}
        self.readers = {}

    def op(self, eng, fn, r=(), w=(), dma=False, semkey=None, inc=16, cost=0.2, lat=None, tab=None):
        i = len(self.ops)
        deps = set()
        for k in r:
            if k in self.last_w:
                deps.add(self.last_w[k])
        for k in w:
            if k in self.last_w:
                deps.add(self.last_w[k])
            for rd in self.readers.get(k, ()):
                deps.add(rd)
        deps.discard(i)
        self.ops.append(dict(eng=eng, fn=fn, deps=deps, dma=dma, semkey=semkey, val=None, inc=inc, phase=self.phase,
                             cost=cost, lat=(cost if lat is None else lat), tab=tab, wkey=tuple(w)))
        for k in r:
            self.readers.setdefault(k, []).append(i)
        for k in w:
            self.last_w[k] = i
            self.readers[k] = []
        return i

    def schedule(self):
        import heapq
        ops = self.ops
        order = {e: [] for e in self.ENGS}
        nph = self.phase + 1
        byphase = [[] for _ in range(nph)]
        for i, o in enumerate(ops):
            byphase[o["phase"]].append(i)
        tnow = 0.0
        for ph in range(nph):
            ids = byphase[ph]
            if not ids:
                continue
            unit_of = {}
            units = []
            last_pe_unit = None
            for i in ids:
                o = ops[i]
                if o["eng"] == "pe" and not o["dma"]:
                    wk = o.get("wkey")
                    if (last_pe_unit is not None and units[last_pe_unit][1] == wk
                            and len(units[last_pe_unit][0]) < 40
                            and all(d < units[last_pe_unit][0][0] or unit_of.get(d) == last_pe_unit
                                    for d in o["deps"])):
                        units[last_pe_unit][0].append(i)
                        unit_of[i] = last_pe_unit
                        continue
                    units.append(([i], wk))
                    last_pe_unit = len(units) - 1
                    unit_of[i] = last_pe_unit
                else:
                    units.append(([i], None))
                    unit_of[i] = len(units) - 1
            nu = len(units)
            ueng = [ops[units[u][0][0]]["eng"] for u in range(nu)]
            udeps = [set() for _ in range(nu)]
            for u in range(nu):
                for m in units[u][0]:
                    for d in ops[m]["deps"]:
                        if d in unit_of and unit_of[d] != u:
                            udeps[u].add(unit_of[d])
            succ = [[] for _ in range(nu)]
            indeg = [0] * nu
            for u in range(nu):
                indeg[u] = len(udeps[u])
                for d in udeps[u]:
                    succ[d].append(u)
            ready = {e: [] for e in self.ENGS}
            for u in range(nu):
                if indeg[u] == 0:
                    heapq.heappush(ready[ueng[u]], u)
            inorder = getattr(self, "inorder_engs", ())
            nxt = {e: [u for u in range(nu) if ueng[u] == e] for e in inorder}
            nptr = {e: 0 for e in inorder}
            efree = {e: tnow for e in self.ENGS}
            ufin = [0.0] * nu
            rdy_t = [tnow] * nu
            cur_tab = None
            left = nu
            while left:
                best = None
                for e in self.ENGS:
                    if not ready[e]:
                        continue
                    if e in inorder:
                        want = nxt[e][nptr[e]]
                        cands = [want] if want in ready[e] else []
                    else:
                        cands = heapq.nsmallest(6, ready[e])
                    for u in cands:
                        st_ = max(efree[e], rdy_t[u])
                        o0 = ops[units[u][0][0]]
                        if e == "act" and o0["tab"] is not None and o0["tab"] != cur_tab:
                            st_ += 1.3
                        key = (st_, u)
                        if best is None or key < best[0]:
                            best = (key, u, e, st_)
                _, u, e, st_ = best
                if e in inorder:
                    nptr[e] += 1
                ready[e].remove(u)
                heapq.heapify(ready[e])
                o0 = ops[units[u][0][0]]
                if e == "act" and o0["tab"] is not None:
                    cur_tab = o0["tab"]
                t = st_
                lat_extra = 0.0
                for m in units[u][0]:
                    t += ops[m]["cost"]
                    lat_extra = ops[m]["lat"] - ops[m]["cost"]
                    order[e].append(m)
                efree[e] = t
                ufin[u] = t + lat_extra
                left -= 1
                for s_ in succ[u]:
                    lat_sync = 0.0 if (ueng[s_] == "pe" and e == "pe" and not o0["dma"]) else 0.15
                    rdy_t[s_] = max(rdy_t[s_], ufin[u] + lat_sync)
                    indeg[s_] -= 1
                    if indeg[s_] == 0:
                        heapq.heappush(ready[ueng[s_]], s_)
            tnow = max(list(efree.values()) + ufin)
        self.est_total = tnow
        return order

    def check_progress(self, order, skip):
        ops = self.ops
        sem = {}
        ptr = {e: 0 for e in self.ENGS}
        total = sum(len(v) for v in order.values())
        done = 0
        while done < total:
            prog = False
            for e in self.ENGS:
                while ptr[e] < len(order[e]):
                    i = order[e][ptr[e]]
                    o = ops[i]
                    reqs = list(o["cdeps"].values()) + [di for di in o["deps"] if ops[di]["dma"]]
                    ok = True
                    for di in reqs:
                        d = ops[di]
                        key = ("d", d["semkey"]) if d["dma"] else ("e", d["eng"])
                        if sem.get(key, 0) < d["val"]:
                            ok = False
                            break
                    if not ok:
                        break
                    if o["val"] is not None:
                        key = ("d", o["semkey"]) if o["dma"] else ("e", e)
                        sem[key] = sem.get(key, 0) + (o["inc"] if o["dma"] else 1)
                        assert sem[key] == o["val"], ("sem value mismatch", i, e, sem[key], o["val"])
                    ptr[e] += 1
                    done += 1
                    prog = True
            if not prog:
                stuck = {e: (order[e][ptr[e]] if ptr[e] < len(order[e]) else None) for e in self.ENGS}
                raise RuntimeError("deadlock in semaphore protocol: %r" % (stuck,))

    def emit(self):
        nc = self.nc
        ops = self.ops
        order = self.schedule()

        def skip(o, d):
            return (not o["dma"]) and (not d["dma"]) and o["eng"] == "pe" and d["eng"] == "pe"

        last_eng = {}
        last_dma = {}
        extra = {}
        cur_phase_last = {}
        nph = self.phase + 1
        per_phase_order = {e: {} for e in self.ENGS}
        for e in self.ENGS:
            for i in order[e]:
                per_phase_order[e].setdefault(ops[i]["phase"], []).append(i)
        acc_last = set()
        for ph in range(nph):
            if ph > 0 and acc_last:
                for e in self.ENGS:
                    lst = per_phase_order[e].get(ph)
                    if lst:
                        extra[lst[0]] = set(acc_last)
            for e in self.ENGS:
                lst = per_phase_order[e].get(ph)
                if not lst:
                    continue
                comp = [i for i in lst if not ops[i]["dma"]]
                if comp:
                    last_eng[e] = comp[-1]
                for i in lst:
                    if ops[i]["dma"]:
                        last_dma[ops[i]["semkey"]] = i
            acc_last = set(last_eng.values()) | set(last_dma.values())
        for i, s_ in extra.items():
            ops[i]["deps"] = set(ops[i]["deps"]) | (s_ - {i})

        pos = {}
        for e in self.ENGS:
            for n_, i in enumerate(order[e]):
                pos[i] = n_
        needed = set()
        for o in ops:
            latest = {}
            for di in o["deps"]:
                d = ops[di]
                if skip(o, d) or d["dma"]:
                    continue
                if d["eng"] not in latest or pos[di] > pos[latest[d["eng"]]]:
                    latest[d["eng"]] = di
            needed.update(latest.values())
            o["cdeps"] = latest
        for i in self.final:
            needed.add(i)
        cnt = {e: 0 for e in self.ENGS}
        dcnt = {}
        for e in self.ENGS:
            for i in order[e]:
                o = ops[i]
                if o["dma"]:
                    k = o["semkey"]
                    dcnt[k] = dcnt.get(k, 0) + o["inc"]
                    o["val"] = dcnt[k]
                elif i in needed:
                    cnt[e] += 1
                    o["val"] = cnt[e]
        self.check_progress(order, skip)
        import contextlib

        with contextlib.ExitStack() as st:
            esem = {e: st.enter_context(nc.semaphore("s_" + e)) for e in self.ENGS}
            dsem = {}
            for k in dcnt:
                dsem[k] = st.enter_context(nc.semaphore("d_%d" % len(dsem)))
            block = st.enter_context(nc.Block())

            def section(ename):
                def body(eng):
                    waited = {}

                    def wait_all(dlist):
                        req = {}
                        for di in dlist:
                            d = ops[di]
                            if d["dma"]:
                                key = ("d", d["semkey"])
                                s = dsem[d["semkey"]]
                            else:
                                key = ("e", d["eng"])
                                s = esem[d["eng"]]
                            if d["val"] > req.get(key, (None, 0))[1]:
                                req[key] = (s, d["val"])
                        for key, (s, v) in req.items():
                            if waited.get(key, 0) >= v:
                                continue
                            eng.wait_ge(s, v)
                            waited[key] = v

                    for i in order[ename]:
                        o = ops[i]
                        wait_all(list(o["cdeps"].values()) + [di for di in o["deps"] if ops[di]["dma"]])
                        ins = o["fn"](eng)
                        if o["val"] is not None:
                            if o["dma"]:
                                ins.then_inc(dsem[o["semkey"]], o["inc"])
                            else:
                                ins.then_inc(esem[ename], 1)
                    if ename == "sp":
                        wait_all(self.final)

                return body

            block.tensor(section("pe"))
            block.scalar(section("act"))
            block.vector(section("dve"))
            block.gpsimd(section("pool"))
            block.sync(section("sp"))


def fsz(ap):
    n = 1
    for s in ap.shape[1:]:
        n *= int(s)
    return n


def mm(P, out, lhsT, rhs, start, stop, r, w):
    n = max(fsz(rhs), 64)
    c = n / 1950.0 * (4.0 if rhs.dtype == F32 else 1.0) + 0.012
    return P.op("pe", lambda e: e.matmul(out, lhsT, rhs, start=start, stop=stop), r, w, cost=c, lat=c + 0.2)


def tr(P, out, in_, ident, r, w):
    c = max(fsz(in_), 64) / 1950.0 * (4.0 if in_.dtype == F32 else 1.0) + 0.03
    return P.op("pe", lambda e: e.transpose(out, in_, ident), r, w, cost=c, lat=c + 0.2)


_TAB = {AF.Exp: "ln_exp", AF.Ln: "ln_exp", AF.Silu: "silu", AF.Sigmoid: "sigmoid"}


def act(P, out, in_, func, r, w, bias=0.0, scale=1.0):
    c = 0.2 + fsz(out) / 1150.0
    if not isinstance(bias, float):
        c += 0.09
    if not isinstance(scale, float):
        c += 0.09
    return P.op("act", lambda e: e.activation(out, in_, func, bias=bias, scale=scale), r, w, cost=c,
                tab=_TAB.get(func))


def _vcost(eng, n, f=1.0):
    if eng == "pool":
        return 0.15 + n * f / 480.0
    return 0.07 + n * f / 960.0


def tt(P, eng, out, in0, in1, op, r, w):
    return P.op(eng, lambda e: e.tensor_tensor(out, in0, in1, op), r, w, cost=_vcost(eng, fsz(out)))


def stt(P, out, in0, scalar, in1, op0, op1, r, w):
    return P.op("dve", lambda e: e.scalar_tensor_tensor(out, in0, scalar, in1, op0, op1), r, w,
                cost=_vcost("dve", fsz(out), 1.42))


def ts(P, eng, out, in0, s1, s2, op0, op1, r, w):
    c = _vcost(eng, fsz(out))
    if s2 is None:
        return P.op(eng, lambda e: e.tensor_scalar(out, in0, s1, None, op0), r, w, cost=c)
    return P.op(eng, lambda e: e.tensor_scalar(out, in0, s1, s2, op0, op1), r, w, cost=c)


def cp(P, eng, out, in_, r, w):
    if eng == "act":
        return P.op("act", lambda e: e.copy(out, in_), r, w, cost=0.2 + fsz(out) / 1150.0)
    return P.op(eng, lambda e: e.tensor_copy(out, in_), r, w, cost=_vcost(eng, fsz(out)))


def dma(P, eng, out, in_, r, w, semkey):
    nbytes = fsz(out) * 128 * (4 if in_.dtype == F32 else 2)
    lat = 2.5 + nbytes / 150e3
    return P.op(eng, lambda e: e.dma_start(out=out, in_=in_), r, w, dma=True, semkey=semkey,
                cost=(1.5 if eng == "pool" else 0.1), lat=lat)


def xkeys(a, b):
    return [("x", t) for t in range(a // 128, (b - 1) // 128 + 1)]


class Builder:
    def __init__(self, mode, layer, final):
        self.mode = mode
        self.layer = layer
        self.final = final
        self.nc = bass.Bass("TRN2", target_bir_lowering=False)
        self.P = Prog(self.nc)

    def declare(self):
        nc = self.nc
        dt = nc.dram_tensor
        self.d_x = dt("x_loc", [NT, D], F32, kind="ExternalInput").ap()
        self.d_consts = dt("consts", [128, 512], F32, kind="ExternalInput").ap()
        self.d_percore = dt("percore", [128, 80], F32, kind="ExternalInput").ap()
        self.d_vecs = dt("vecs", [2, 128, VW], F32, kind="ExternalInput").ap()
        self.d_w_in = dt("w_in", [2, D, INW], F32, kind="ExternalInput").ap()
        if self.mode == "main":
            self.d_w_pool = dt("w_pool", [2, 4, 128, 128], F32, kind="ExternalInput").ap()
            self.d_w_out = dt("w_out", [2, D, D], F32, kind="ExternalInput").ap()
            self.d_w_up = dt("w_up", [2, D, 2 * DFF], F32, kind="ExternalInput").ap()
            self.d_w_down = dt("w_down", [2, DFF, D], F32, kind="ExternalInput").ap()
            self.d_state_in = dt("state_in", [128, SW], F32, kind="ExternalInput").ap()
            self.d_out = dt("y", [2048, D], F32, kind="ExternalOutput").ap()
        else:
            self.d_state_out = dt("state_out", [128, SW], F32, kind="ExternalOutput").ap()

    def carve(self, nbytes_dtype, shape):
        dtype = nbytes_dtype
        n = int(np.prod(shape[1:]))
        nb = n * (4 if dtype == F32 else 2)
        nb = (nb + 63) // 64 * 64
        off = self.u_off
        assert off + nb <= self.u_bytes, ("U overflow", off, nb, self.u_bytes)
        self.u_off += nb
        ap = self.U[:, off // 2: off // 2 + nb // 2]
        if dtype == F32:
            ap = ap.bitcast(F32)
        ap = ap[:, 0:n]
        if len(shape) == 3:
            ap = ap.rearrange("p (a b) -> p a b", a=shape[1])
        elif len(shape) == 4:
            ap = ap.rearrange("p (a b c) -> p a b c", a=shape[1], b=shape[2])
        return ap

    def vec(self, off, n=1):
        return self.vecs[:, self.layer, off:off + n]

    def rmsnorm(self, a, b, gv_off, out_ap, out_keys, okey_r=()):
        P = self.P
        W = b - a
        xk = xkeys(a, b)
        bank = self.mmbank()
        ssps = self.ps[:, bank, 0:W]
        nsub = W // 128
        for j in range(nsub):
            sb = self.sq_i % 2
            self.sq_i += 1
            sq = self.sq[sb]
            act(P, sq, self.x[:, :, a + j * 128: a + (j + 1) * 128], AF.Square, xk, [("sq", sb)])
            for kt in range(KT):
                mm(P, self.ps[:, bank, j * 128:(j + 1) * 128], self.onesb, sq[:, kt, :], kt == 0, kt == KT - 1,
                   [("sq", sb), "onesb"], [("ps", bank)])
        act(P, self.rstd[:, 0:W], ssps, AF.Ln, [("ps", bank)], ["rstd"], bias=self.epsb[:, 0:1], scale=1.0 / D)
        act(P, self.rstd[:, 0:W], self.rstd[:, 0:W], AF.Exp, ["rstd"], ["rstd"], scale=-0.5)
        for kt in range(KT):
            stt(P, out_ap[:, kt, 0:W], self.x[:, kt, a:b], self.vec(gv_off + kt), self.rstd[:, 0:W],
                ALU.mult, ALU.mult, xk + ["rstd", "vecs"] + list(okey_r), out_keys)

    def mmbank(self):
        b = self.mm_i % self.n_mm
        self.mm_i += 1
        return self.mm_banks[b]

    def load_common(self):
        P = self.P
        dma(P, "sp", self.cf, self.d_consts, [], ["cf"], "cf")
        dma(P, "sp", self.percore, self.d_percore, [], ["percore"], "percore")
        dma(P, "sp", self.vecs, self.d_vecs.rearrange("l p v -> p l v"), [], ["vecs"], "vecs")
        cp(P, "dve", self.identb, self.cf[:, 0:128], ["cf"], ["identb"])
        cp(P, "dve", self.onesb, self.cf[:, 256:384], ["cf"], ["onesb"])
        P.op("dve", lambda e: e.memset(self.epsb, EPS), [], ["epsb"])

    def load_x(self):
        P = self.P
        for t in range(NTL):
            b = t % 2
            xt = self.xtok[b]
            xk_ = getattr(self, "xtok_keys", [("xtok", 0), ("xtok", 1)])[b]
            dma(P, "sp", xt, self.d_x[t * 128:(t + 1) * 128, :], [], [xk_], ("xtok", b))
            banks = (0, 1) if b == 0 else (2, 3)
            for kt in range(KT):
                bk = banks[kt // 4]
                tr(P, self.ps[:, bk, (kt % 4) * 128:(kt % 4 + 1) * 128], xt[:, kt * 128:(kt + 1) * 128],
                   self.cf[:, 0:128], [xk_, "cf"], [("ps", bk)])
            for hf in range(2):
                bk = banks[hf]
                src = self.ps[:, bk, :].rearrange("p (a b) -> p a b", a=4)
                dst = self.x[:, hf * 4:(hf + 1) * 4, t * 128:(t + 1) * 128]
                eng = "act" if hf == 0 else "dve"
                cp(P, eng, dst, src, [("ps", bk)], [("x", t)])

    def load_w_in(self, l, only_k_v_g=False):
        P = self.P
        src = self.d_w_in[l].rearrange("(kt p) n -> p kt n", p=128)
        dma(P, "pool", self.wA, src[:, :, 0:1024], [], ["wA"], "wA")
        if only_k_v_g:
            dma(P, "pool", self.wB[:, :, 512:520], src[:, :, 1536:1544], [], ["wB"], "wB")
        else:
            dma(P, "pool", self.wB, src[:, :, 1024:INW], [], ["wB"], "wB")

    def qk_tile(self, g, ct, W, dst, dst_key):
        P = self.P
        bank = self.mmbank()
        for kt in range(KT):
            mm(P, self.ps[:, bank, 0:W], self.wA[:, kt, ct * 128:(ct + 1) * 128], self.hT[:, kt, 0:W],
               kt == 0, kt == KT - 1, ["wA", "hT"], [("ps", bank)])
        pb = self.pre_i % 2
        self.pre_i += 1
        pre = self.pre[pb]
        acc = self.acc[pb]
        cp(P, "act", pre[:, 3:3 + W], self.ps[:, bank, 0:W], [("ps", bank)], [("pre", pb)])
        cp(P, "pool", pre[:, 0:3], self.qkcarry[:, ct, :], [("qkc", ct)], [("pre", pb)])
        cp(P, "pool", self.qkcarry[:, ct, :], pre[:, W:W + 3], [("pre", pb)], [("qkc", ct)])
        wv = lambda j: self.vec(V_QKC + ct * 4 + j)
        ts(P, "dve", acc[:, 0:W], pre[:, 3:3 + W], wv(3), None, ALU.mult, None, [("pre", pb), "vecs"], [("acc", pb)])
        for j in (2, 1, 0):
            stt(P, acc[:, 0:W], pre[:, j:j + W], wv(j), acc[:, 0:W], ALU.mult, ALU.add,
                [("pre", pb), ("acc", pb), "vecs"], [("acc", pb)])
        if dst is None:
            for e in range(2):
                es = slice(e * 64, (e + 1) * 64)
                act(P, self.qTz[es, ct, e, 0:W], acc[es, 0:W], AF.Silu, [("acc", pb)], [dst_key])
        else:
            act(P, dst, acc[:, 0:W], AF.Silu, [("acc", pb)], [dst_key])

    def gates_group(self, g, t0, ntg):
        P = self.P
        G3 = ("ps", 3)
        for j in range(ntg):
            for kt in range(KT):
                mm(P, self.ps[:, 3, 32 + j * 8: 40 + j * 8], self.hT[:, kt, j * 128:(j + 1) * 128],
                   self.wB[:, kt, 512:520], kt == 0, kt == KT - 1, ["hT", "wB"], [G3])
        gv = self.ps[:, 3, 32:32 + ntg * 8].rearrange("p (a b) -> p a b", a=ntg)
        bg = self.vecs[:, self.layer, V_BG:V_BG + 8].unsqueeze(1).to_broadcast([128, ntg, 8])
        tt(P, "dve", self.gsb[:, 0:ntg, :], gv, bg, ALU.add, [G3, "vecs"], ["gsb"])
        act(P, self.lfn[:, 0:ntg, :], self.gsb[:, 0:ntg, 4:8], AF.Exp, ["gsb"], ["lfn"], scale=-1.0)
        act(P, self.lfn[:, 0:ntg, :], self.lfn[:, 0:ntg, :], AF.Ln, ["lfn"], ["lfn"], bias=1.0)
        lf2 = self.lfn[:, 0:ntg, :]
        mm(P, self.ps[:, 3, 0:ntg * 4], self.cf[:, 128:256], lf2, True, True, ["cf", "lfn"], [G3])
        mm(P, self.ps[:, 3, 16:16 + ntg * 4], self.cf[:, 256:384], lf2, True, True, ["cf", "lfn"], [G3])
        bneg = self.ps[:, 3, 0:ntg * 4].rearrange("p (a b) -> p a b", a=ntg)
        totn = self.ps[:, 3, 16:16 + ntg * 4].rearrange("p (a b) -> p a b", a=ntg)
        tt(P, "dve", self.cS[:, t0:t0 + ntg, :], self.gsb[:, 0:ntg, 0:4], bneg, ALU.add, [G3, "gsb"], ["cS"])
        act(P, self.cS[:, t0:t0 + ntg, :], self.cS[:, t0:t0 + ntg, :], AF.Exp, ["cS"], ["cS"])
        act(P, self.emb[:, t0:t0 + ntg, :], bneg, AF.Exp, [G3], ["emb"])
        for e in range(2):
            sl = slice(e * 64, (e + 1) * 64)
            act(P, self.Gp[sl, t0:t0 + ntg, :], totn[sl, :, e::2], AF.Exp, [G3], ["Gp"], scale=-1.0)

    def v_tile(self, j, t):
        P = self.P
        bank = self.mmbank()
        for kt in range(KT):
            mm(P, self.ps[:, bank, :], self.hT[:, kt, j * 128:(j + 1) * 128], self.wA[:, kt, 512:1024],
               kt == 0, kt == KT - 1, ["hT", "wA"], [("ps", bank)])
        src = self.ps[:, bank, :].rearrange("p (a b) -> p a b", a=4)
        cb = self.cS[:, t, :].unsqueeze(2).to_broadcast([128, 4, 128])
        vb = j % 4
        tt(P, "dve", self.vt[:, vb, :, 0:128], src, cb, ALU.mult, [("ps", bank), "cS"], [("vt", vb)])
        cp(P, "dve", self.vt[:, vb, :, 128:129], self.cS[:, t, :].unsqueeze(2), ["cS"], [("vt", vb)])

    def k_transpose(self, j):
        P = self.P
        T5 = ("ps", 5)
        tp = self.ps[:, 5, :].bitcast(BF16)
        for pr in range(2):
            tr(P, tp[:, pr * 128:(pr + 1) * 128], self.kTw[:, pr, j * 128:(j + 1) * 128], self.identb,
               ["kTw", "identb"], [T5])
        cp(P, "act", self.ktok[:, j % 4, :], tp[:, 0:256], [T5], [("ktok", j % 4)])

    def kv_update(self, j, t):
        P = self.P
        vb = j % 4
        for h in range(4):
            pr, e = h // 2, h % 2
            bk = 6 + pr
            mm(P, self.ps[e * 64:(e + 1) * 64, bk, 258:387], self.ktok[:, j % 4, h * 64:(h + 1) * 64],
               self.vt[:, vb, h, :], True, True, [("ktok", j % 4), ("vt", vb)], [("ps", bk)])
        kv = self.ps[:, 6:8, 258:387]
        tt(P, "dve", self.Cst, self.Cst, kv, ALU.add, ["Cst", ("ps", 6), ("ps", 7)], ["Cst"])
        gb = self.Gp[:, t, :].unsqueeze(2).to_broadcast([128, 2, 129])
        tt(P, "dve", self.Cst, self.Cst, gb, ALU.mult, ["Cst", "Gp"], ["Cst"])
        if getattr(self, "make_cb", self.mode == "main"):
            act(P, self.Cb, self.Cst, AF.Copy, ["Cst"], ["Cb"], scale=0.125)


    def o_tile(self, i, W):
        P = self.P
        bank = self.mmbank()
        for kt in range(KT):
            mm(P, self.ps[:, bank, 0:W], self.wB[:, kt, i * 128:(i + 1) * 128], self.hT[:, kt, 0:W],
               kt == 0, kt == KT - 1, ["wB", "hT"], [("ps", bank)])
        act(P, self.sigo[:, i, 0:W], self.ps[:, bank, 0:W], AF.Sigmoid, [("ps", bank)], ["sigo"])

    def u_tile(self, g, gi, W):
        P = self.P
        bank = self.mmbank()
        for kt in range(KT):
            mm(P, self.ps[:, bank, 0:W], self.wB[:, kt, 520 + gi * 128:520 + (gi + 1) * 128], self.hT[:, kt, 0:W],
               kt == 0, kt == KT - 1, ["wB", "hT"], [("ps", bank)])
        ub, sA, sB = self.pre[0], self.pre[1], self.acc[0]
        kU, kA, kB = ("pre", 0), ("pre", 1), ("acc", 0)
        cp(P, "act", ub[:, 16:16 + W], self.ps[:, bank, 0:W], [("ps", bank)], [kU])
        cp(P, "pool", ub[:, 0:16], self.ucarry[:, gi, :], [("ucar", gi)], [kU])
        cp(P, "pool", self.ucarry[:, gi, :], ub[:, W:W + 16], [kU], [("ucar", gi)])
        E = 16 + W
        tt(P, "pool", sA[:, 1:E], ub[:, 1:E], ub[:, 0:E - 1], ALU.add, [kU], [kA])
        fin, kf = sA, kA
        if gi >= 1:
            tt(P, "pool", sB[:, 3:E], sA[:, 3:E], sA[:, 1:E - 2], ALU.add, [kA], [kB])
            fin, kf = sB, kB
        if gi >= 2:
            tt(P, "pool", sA[:, 7:E], sB[:, 7:E], sB[:, 3:E - 4], ALU.add, [kB], [kA])
            fin, kf = sA, kA
        if gi >= 3:
            tt(P, "pool", sB[:, 15:E], sA[:, 15:E], sA[:, 7:E - 8], ALU.add, [kA], [kB])
            fin, kf = sB, kB
        w = float(2 ** (gi + 1))
        db = self.dT_i % 2
        self.dT_i += 1
        dT = self.dT[db]
        stt(P, dT[:, 0:W], fin[:, 16:16 + W], 1.0 / w, ub[:, 16:16 + W], ALU.mult, ALU.subtract,
            [kf, kU], [("dT", db)])
        if g == 1:
            tt(P, "dve", self.tmp16, fin[:, 16:32], self.percore[:, gi * 16:(gi + 1) * 16], ALU.mult,
               [kf, "percore"], ["tmp16"])
            tt(P, "dve", dT[:, 0:16], self.tmp16, ub[:, 16:32], ALU.subtract, ["tmp16", kU], [("dT", db)])
        bank2 = self.mmbank()
        mm(P, self.ps[:, bank2, 0:W], self.wpool[:, gi, :], dT[:, 0:W], True, True, ["wpool", ("dT", db)],
           [("ps", bank2)])
        act(P, self.hpT[:, gi, 0:W], self.ps[:, bank2, 0:W], AF.Copy, [("ps", bank2), "vecs"], ["hpT"],
            scale=self.vec(V_PS + gi))

    def s2_tile(self, j, t):
        P = self.P
        js = slice(j * 128, (j + 1) * 128)
        vb = j % 4
        for h in range(4):
            pr, e = h // 2, h % 2
            es = slice(e * 64, (e + 1) * 64)
            mm(P, self.ps[:, 4, h * 128:(h + 1) * 128], self.kTw[:, pr, js], self.qTz[:, pr, e, js], True, True,
               ["kTw", "qTw"], [("ps", 4)])
        ptv = self.ps[:, 4, :].rearrange("p (a b) -> p a b", a=4)
        mb = self.cf[:, 384:512].unsqueeze(1).to_broadcast([128, 4, 128])
        tt(P, "dve", self.PTm, ptv, mb, ALU.mult, [("ps", 4), "cf"], ["PTm"])
        for h in range(4):
            pr, e = h // 2, h % 2
            es = slice(e * 64, (e + 1) * 64)
            bk = 6 + pr
            o = self.ps[:, bk, e * 129:(e + 1) * 129]
            mm(P, o, self.PTm[:, h, :], self.vt[:, vb, h, :], True, False, ["PTm", ("vt", vb)], [("ps", bk)])
            mm(P, o, self.qTz[:, pr, e, js], self.Cb[:, pr, :], False, True, ["qTw", "Cb"], [("ps", bk)])
        self.kv_update(j, t)
        N4 = self.ps[:, 6:8, 0:258].rearrange("p a (e c) -> p a e c", e=2)
        NK = [("ps", 6), ("ps", 7)]
        sm = self.small
        v22 = lambda c0: sm[:, c0:c0 + 4].rearrange("p (a e) -> p a e", a=2)
        den, rden, ssq, t1, t2, scl = v22(0), v22(4), v22(8), v22(12), v22(16), v22(20)
        embv = self.emb[:, t, :].rearrange("p (a e) -> p a e", a=2)
        act(P, den, N4[:, :, :, 128], AF.Abs, NK, ["den"])
        tt(P, "dve", den, den, embv, ALU.max, ["den", "emb"], ["den"])
        P.op("dve", lambda e_: e_.reciprocal(rden, den), ["den"], ["rden"])
        sq4 = self.sqN.rearrange("p (a e) d -> p a e d", a=2)
        act(P, sq4, N4[:, :, :, 0:128], AF.Square, NK, ["sqN"])
        P.op("dve", lambda e_: e_.tensor_reduce(ssq, sq4, AX.X, ALU.add), ["sqN"], ["ssq"])
        tt(P, "dve", t1, rden, rden, ALU.mult, ["rden"], ["t1"])
        tt(P, "dve", t2, ssq, t1, ALU.mult, ["ssq", "t1"], ["t2"])
        act(P, t2, t2, AF.Ln, ["t2"], ["t2"], bias=self.epsb[:, 0:1], scale=1.0 / 128.0)
        act(P, t2, t2, AF.Exp, ["t2"], ["t2"], scale=-0.5)
        tt(P, "dve", scl, rden, t2, ALU.mult, ["rden", "t2"], ["scl"])
        hn4 = self.hn.rearrange("p (a e) d -> p a e d", a=2)
        tt(P, "dve", hn4, N4[:, :, :, 0:128], scl.unsqueeze(3).to_broadcast([128, 2, 2, 128]), ALU.mult,
           NK + ["scl"], ["hn"])
        tp = self.ps[:, 5, :].bitcast(BF16)
        T5 = ("ps", 5)
        for h in range(4):
            tr(P, tp[:, 256 + h * 128:256 + (h + 1) * 128], self.hn[:, h, :], self.identb, ["hn", "identb"], [T5])
        for h in range(4):
            stt(P, self.sigo[:, h, js], tp[:, 256 + h * 128:256 + (h + 1) * 128], self.vec(V_GH + h),
                self.sigo[:, h, js], ALU.mult, ALU.mult, [T5, "sigo", "vecs"], ["sigo"])

    def o_group(self, a, b):
        P = self.P
        W = b - a
        for dt_ in range(8):
            bank = self.mmbank()
            for kt in range(KT):
                rhs = self.sigo[:, kt, 0:W] if kt < 4 else self.hpT[:, kt - 4, 0:W]
                mm(P, self.ps[:, bank, 0:W], self.wout[:, kt, dt_ * 128:(dt_ + 1) * 128], rhs, kt == 0, kt == KT - 1,
                   ["wout", "sigo", "hpT"], [("ps", bank)])
            tt(P, "dve", self.x[:, dt_, a:b], self.ps[:, bank, 0:W], self.x[:, dt_, a:b], ALU.add,
               [("ps", bank)] + xkeys(a, b), xkeys(a, b))

    def load_ffn_chunk(self, l, c, slot):
        P = self.P
        f0, f1 = FCH[c]
        T = f1 - f0
        su = self.d_w_up[l].rearrange("(kt p) n -> p kt n", p=128)
        dma(P, "pool", self.wu[slot][:, :, 0:T * 128], su[:, :, f0 * 128:f1 * 128], [], [("wu", slot)], ("wu", slot))
        dma(P, "pool", self.wu[slot][:, :, 512:512 + T * 128], su[:, :, DFF + f0 * 128:DFF + f1 * 128], [],
            [("wu", slot)], ("wu", slot))
        sd = self.d_w_down[l].rearrange("(i p) n -> p i n", p=128)
        dma(P, "pool", self.wd[slot][:, 0:T, :], sd[:, f0:f1, :], [], [("wd", slot)], ("wd", slot))

    def ffn_conv(self, bank, Wo, f, acc, kacc):
        P = self.P
        Wn = Wo + 2
        act(P, acc[:, 0:Wo], self.ps[:, bank, 2:Wn], AF.Identity, [("ps", bank), "vecs"], [kacc],
            bias=self.vec(V_FB + f), scale=self.vec(V_FC + f * 3 + 2))
        stt(P, acc[:, 0:Wo], self.ps[:, bank, 1:Wn - 1], self.vec(V_FC + f * 3 + 1), acc[:, 0:Wo], ALU.mult, ALU.add,
            [("ps", bank), kacc, "vecs"], [kacc])
        stt(P, acc[:, 0:Wo], self.ps[:, bank, 0:Wo], self.vec(V_FC + f * 3 + 0), acc[:, 0:Wo], ALU.mult, ALU.add,
            [("ps", bank), kacc, "vecs"], [kacc])

    def ffn_phase(self, l, tok0):
        P = self.P
        n = NT - tok0
        nwin = -(-n // 510)
        base = n // nwin
        wins = []
        s = tok0
        for i in range(nwin):
            Wo = base + (1 if i < n - base * nwin else 0)
            wins.append((s, Wo))
            s += Wo
        assert s == NT
        pair_i = 0
        dn_i = 0
        ab_i = 0
        for c, (f0, f1) in enumerate(FCH):
            slot = c % 2
            T = f1 - f0
            for (s, Wo) in wins:
                Wn = Wo + 2
                ab = ab_i % 2
                ab_i += 1
                for i in range(T):
                    pb = pair_i % 2
                    pair_i += 1
                    bg, bv = (0, 1) if pb == 0 else (2, 3)
                    for kt in range(KT):
                        mm(P, self.ps[:, bg, 0:Wn], self.wu[slot][:, kt, i * 128:(i + 1) * 128],
                           self.hT2[:, kt, s:s + Wn], kt == 0, kt == KT - 1, [("wu", slot), "hT2"], [("ps", bg)])
                    for kt in range(KT):
                        mm(P, self.ps[:, bv, 0:Wn], self.wu[slot][:, kt, 512 + i * 128:512 + (i + 1) * 128],
                           self.hT2[:, kt, s:s + Wn], kt == 0, kt == KT - 1, [("wu", slot), "hT2"], [("ps", bv)])
                    self.ffn_conv(bg, Wo, f0 + i, self.accg[pb], ("accg", pb))
                    self.ffn_conv(bv, Wo, FT + f0 + i, self.accv[pb], ("accv", pb))
                    act(P, self.sg[pb][:, 0:Wo], self.accg[pb][:, 0:Wo], AF.Silu, [("accg", pb)], [("sg", pb)])
                    tt(P, "dve", self.actb[ab][:, i, 0:Wo], self.sg[pb][:, 0:Wo], self.accv[pb][:, 0:Wo], ALU.mult,
                       [("sg", pb), ("accv", pb)], [("actb", ab)])
                for dt_ in range(8):
                    bank = 4 + dn_i % 4
                    dn_i += 1
                    for i in range(T):
                        mm(P, self.ps[:, bank, 0:Wo], self.wd[slot][:, i, dt_ * 128:(dt_ + 1) * 128],
                           self.actb[ab][:, i, 0:Wo], i == 0, i == T - 1, [("wd", slot), ("actb", ab)], [("ps", bank)])
                    tt(P, "dve", self.x[:, dt_, s:s + Wo], self.ps[:, bank, 0:Wo], self.x[:, dt_, s:s + Wo], ALU.add,
                       [("ps", bank)] + xkeys(s, s + Wo), xkeys(s, s + Wo))
            if c + 2 < len(FCH):
                self.load_ffn_chunk(l, c + 2, slot)

    def build_main(self):
        nc, P = self.nc, self.P
        l = self.layer
        self.declare()
        import contextlib

        with contextlib.ExitStack() as st:
            sb = lambda name, shape, dt_: st.enter_context(nc.sbuf_tensor("sb_" + name, shape, dt_))[:]
            self.x = sb("x", [128, KT, NT], F32)
            self.cf = sb("cf", [128, 512], F32)
            self.percore = sb("percore", [128, 80], F32)
            self.vecs = sb("vecs", [128, 2, VW], F32)
            self.identb = sb("identb", [128, 128], BF16)
            self.onesb = sb("onesb", [128, 128], BF16)
            self.epsb = sb("epsb", [128, 1], F32)
            self.wA = sb("wA", [128, KT, 1024], BF16)
            self.wu = [sb("wu0", [128, KT, 1024], BF16), None]
            self.wd = [sb("wd0", [128, 4, 1024], BF16), None]
            self.qkcarry = sb("qkcarry", [128, 4, 3], F32)
            self.ucarry = sb("ucarry", [128, 4, 16], F32)
            self.gsb = sb("gsb", [128, 4, 8], F32)
            self.lfn = sb("lfn", [128, 4, 4], F32)
            self.cS = sb("cS", [128, NTL, 4], F32)
            self.emb = sb("emb", [128, NTL, 4], F32)
            self.Gp = sb("Gp", [128, NTL, 2], F32)
            self.Cst = sb("Cst", [128, 2, 129], F32)
            self.Cb = sb("Cb", [128, 2, 129], BF16)
            self.small = sb("small", [128, 32], F32)
            self.tmp16 = sb("tmp16", [128, 16], F32)
            self.u_bytes = (nc.sbuf_bytes_remaining - 256) // 64 * 64
            self.U = sb("U", [128, self.u_bytes // 2], BF16)
            self.ps = st.enter_context(nc.psum_tensor("ps", [128, 8, 512], F32))[:]
            self.mm_banks, self.n_mm, self.mm_i = [0, 1, 2], 3, 0
            self.sq_i = self.pre_i = self.dT_i = 0

            self.u_off = 0
            cv = self.carve
            self.wB = cv(BF16, [128, KT, 1032])
            self.wout = cv(BF16, [128, KT, 1024])
            self.wpool = cv(BF16, [128, 4, 128])
            self.sq = [cv(BF16, [128, KT, 128]) for _ in range(2)]
            self.rstd = cv(F32, [128, 512])
            self.hT = cv(BF16, [128, KT, 512])
            self.pre = [cv(F32, [128, 528]) for _ in range(2)]
            self.acc = [cv(F32, [128, 528]) for _ in range(2)]
            self.qTz = cv(BF16, [128, 2, 2, 512])
            self.kTw = cv(BF16, [128, 2, 512])
            self.ktok = cv(BF16, [128, 4, 256])
            self.vt = cv(BF16, [128, 4, 4, 129])
            self.sigo = cv(BF16, [128, 4, 512])
            self.hpT = cv(BF16, [128, 4, 512])
            self.dT = [cv(BF16, [128, 512]) for _ in range(2)]
            self.PTm = cv(BF16, [128, 4, 128])
            self.hn = cv(BF16, [128, 4, 128])
            self.sqN = cv(F32, [128, 4, 128])
            self.xtok = [self.sigo.rearrange("p a b -> p (a b)").bitcast(F32),
                         self.hpT.rearrange("p a b -> p (a b)").bitcast(F32)]
            self.xtok_keys = ["sigo", "hpT"]

            self.load_common()
            self.load_w_in(l)
            so = self.d_w_out[l].rearrange("(kt p) n -> p kt n", p=128)
            dma(P, "pool", self.wout, so, [], ["wout"], "wout")
            dma(P, "pool", self.wpool, self.d_w_pool[l].rearrange("g c d -> c g d"), [], ["wpool"], "wpool")
            self.load_ffn_chunk(l, 0, 0)
            self.load_x()
            dma(P, "sp", self.Cst.rearrange("p a b -> p (a b)"), self.d_state_in, [], ["Cst"], "cst")
            act(P, self.Cb, self.Cst, AF.Copy, ["Cst"], ["Cb"], scale=0.125)
            P.op("dve", lambda e: e.memset(self.qkcarry, 0.0), [], [("qkc", c) for c in range(4)])
            P.op("dve", lambda e: e.memset(self.ucarry, 0.0), [], [("ucar", c) for c in range(4)])
            P.op("pool", lambda e: e.memset(self.qTz[64:128, :, 0, :], 0.0), [], ["qTw"])
            P.op("pool", lambda e: e.memset(self.qTz[0:64, :, 1, :], 0.0), [], ["qTw"])

            for g, (t0, t1) in enumerate(GROUPS):
                a, b = t0 * 128, t1 * 128
                W = b - a
                ntg = t1 - t0
                self.rmsnorm(a, b, V_GMIX, self.hT, ["hT"])
                for ct in range(4):
                    dst = None if ct < 2 else self.kTw[:, ct - 2, 0:W]
                    self.qk_tile(g, ct, W, dst, "qTw" if ct < 2 else "kTw")
                self.gates_group(g, t0, ntg)
                for j in range(ntg):
                    if t0 + j >= 1:
                        self.v_tile(j, t0 + j)
                        self.k_transpose(j)
                for i in range(4):
                    self.o_tile(i, W)
                for gi in range(4):
                    self.u_tile(g, gi, W)
                for j in range(ntg):
                    if t0 + j >= 1:
                        self.s2_tile(j, t0 + j)
                self.o_group(a, b)

            P.barrier()
            self.u_off = 0
            self.wu[1] = cv(BF16, [128, KT, 1024])
            self.wd[1] = cv(BF16, [128, 4, 1024])
            self.hT2 = cv(BF16, [128, KT, NT + 2])
            self.sq = [cv(BF16, [128, KT, 128]) for _ in range(2)]
            self.rstd = cv(F32, [128, 512])
            self.accg = [cv(F32, [128, 512]) for _ in range(2)]
            self.accv = [cv(F32, [128, 512]) for _ in range(2)]
            self.sg = [cv(F32, [128, 512]) for _ in range(2)]
            self.actb = [cv(BF16, [128, 4, 512]) for _ in range(2)]
            self.load_ffn_chunk(l, 1, 1)
            P.op("dve", lambda e: e.memset(self.hT2[:, :, 0:2], 0.0), [], ["hT2"])
            for g, (t0, t1) in enumerate(GROUPS):
                a, b = t0 * 128, t1 * 128
                self.rmsnorm(a, b, V_GF, self.hT2[:, :, 2 + a:2 + b], ["hT2"])
            self.ffn_phase(l, 128)

            P.barrier()
            self.u_off = 0
            self.sq = [cv(BF16, [128, KT, 128]) for _ in range(2)]
            self.rstd = cv(F32, [128, 512])
            yT = [cv(F32, [128, KT, 128]) for _ in range(2)]
            ytok = [cv(F32, [128, D]) for _ in range(2)]
            for t in range(2, NTL):
                b2 = t % 2
                a, b = t * 128, (t + 1) * 128
                if self.final:
                    self.rmsnorm(a, b, V_FIN, yT[b2], [("yT", b2)])
                banks = (4, 5) if b2 == 0 else (6, 7)
                for kt in range(KT):
                    bk = banks[kt // 4]
                    if self.final:
                        src, sk = yT[b2][:, kt, :], [("yT", b2)]
                    else:
                        src, sk = self.x[:, kt, a:b], [("x", t)]
                    tr(P, self.ps[:, bk, (kt % 4) * 128:(kt % 4 + 1) * 128], src, self.cf[:, 0:128], sk + ["cf"],
                       [("ps", bk)])
                cp(P, "act", ytok[b2][:, 0:512], self.ps[:, banks[0], :], [("ps", banks[0])], [("ytok", b2)])
                cp(P, "dve", ytok[b2][:, 512:1024], self.ps[:, banks[1], :], [("ps", banks[1])], [("ytok", b2)])
                o = dma(P, "sp", self.d_out[(t - 2) * 128:(t - 1) * 128, :], ytok[b2], [("ytok", b2)], [("yo", t)],
                        ("yout", b2))
                P.final.append(o)
            P.emit()
        return nc


    def declare_fused(self):
        nc = self.nc
        dt = nc.dram_tensor
        self.d_x = dt("x_loc", [NT, D], F32, kind="ExternalInput").ap()
        self.d_consts = dt("consts", [128, 512], F32, kind="ExternalInput").ap()
        self.d_percore = dt("percore", [128, 80], F32, kind="ExternalInput").ap()
        self.d_vecs = dt("vecs", [2, 128, VW], F32, kind="ExternalInput").ap()
        self.d_w_in = dt("w_in", [2, D, INW], F32, kind="ExternalInput").ap()
        self.d_w_pool = dt("w_pool", [2, 4, 128, 128], F32, kind="ExternalInput").ap()
        self.d_w_out = dt("w_out", [2, D, D], F32, kind="ExternalInput").ap()
        self.d_w_up = dt("w_up", [2, D, 2 * DFF], F32, kind="ExternalInput").ap()
        self.d_w_down = dt("w_down", [2, DFF, D], F32, kind="ExternalInput").ap()
        self.d_out = dt("y", [2048, D], F32, kind="ExternalOutput").ap()
        self.d_st_loc = dt("cc_loc", [128, SW], F32).ap()
        self.d_st_all = dt("cc_all", [8 * 128, SW], F32).ap()
        self.d_h_loc = dt("cch_loc", [128, 16], F32).ap()
        self.d_h_all = dt("cch_all", [8 * 128, 16], F32).ap()

    def w_in_src(self, l):
        return self.d_w_in[l].rearrange("(kt p) n -> p kt n", p=128)

    def phase_pass1(self, inj, pub):
        P = self.P
        self.make_cb = False
        P.op("dve", lambda e: e.memset(self.qkcarry, 0.0), [], [("qkc", c) for c in range(4)])
        P.op("dve", lambda e: e.memset(self.Cst, 0.0), [], ["Cst"])
        for g, (t0, t1) in enumerate(GROUPS):
            if t0 >= pub:
                break
            a, b = t0 * 128, t1 * 128
            W = b - a
            ntg = t1 - t0
            self.rmsnorm(a, b, V_GMIX, self.hT, ["hT"])
            for ct in (2, 3):
                self.qk_tile(g, ct, W, self.kTw[:, ct - 2, 0:W], "kTw")
            self.gates_group(g, t0, ntg)
            for j in range(ntg):
                t = t0 + j
                if t < inj or t >= pub:
                    continue
                self.v_tile(j, t)
                self.k_transpose(j)
                self.kv_update(j, t)

    def phase_exchange(self):
        P = self.P
        cflat = self.Cst.rearrange("p a b -> p (a b)")
        dma(P, "sp", self.d_st_loc, cflat, ["Cst"], ["stloc"], "stloc")
        P.op("pool", lambda e: e.collective_compute("AllGather", ALU.bypass, replica_groups=[list(range(8))],
                                                    ins=[self.d_st_loc], outs=[self.d_st_all]),
             ["stloc"], ["stalld"], dma=True, semkey="cc", inc=1, cost=1.0, lat=30.0)
        dma(P, "sp", self.stall, self.d_st_all.rearrange("(r p) n -> p r n", p=128), ["stalld"], ["stall"], "stall")
        ts(P, "dve", cflat, self.stall[:, 0, :], self.percore[:, 65:66], None, ALU.mult, None,
           ["stall", "percore"], ["Cst"])
        for r in range(1, 8):
            stt(P, cflat, self.stall[:, r, :], self.percore[:, 65 + r:66 + r], cflat, ALU.mult, ALU.add,
                ["stall", "percore", "Cst"], ["Cst"])
        act(P, self.Cb, self.Cst, AF.Copy, ["Cst"], ["Cb"], scale=0.125)

    def phase_halo_exchange(self):
        P = self.P
        v3 = self.xh_s.rearrange("p (a b) -> p a b", a=8)
        cp(P, "dve", v3, self.x[:, :, NT - 2:NT], [("x", NTL - 1)], ["xh_s"])
        dma(P, "sp", self.d_h_loc, self.xh_s, ["xh_s"], ["hloc"], "hloc")
        P.op("pool", lambda e: e.collective_compute("AllGather", ALU.bypass, replica_groups=[list(range(8))],
                                                    ins=[self.d_h_loc], outs=[self.d_h_all]),
             ["hloc"], ["halld"], dma=True, semkey="cc2", inc=1, cost=1.0, lat=15.0)
        dma(P, "sp", self.hall, self.d_h_all.rearrange("(r p) n -> p r n", p=128), ["halld"], ["hall"], "hall")
        ts(P, "dve", self.xh_s, self.hall[:, 0, :], self.percore[:, 65:66], None, ALU.mult, None,
           ["hall", "percore"], ["xh_s"])
        for r in range(1, 8):
            stt(P, self.xh_s, self.hall[:, r, :], self.percore[:, 65 + r:66 + r], self.xh_s, ALU.mult, ALU.add,
                ["hall", "percore", "xh_s"], ["xh_s"])
        cp(P, "dve", self.x[:, :, 254:256], v3, ["xh_s"], [("x", 1)])

    def phase_mixer(self, inj, first_out_group):
        P = self.P
        self.make_cb = True
        P.op("dve", lambda e: e.memset(self.qkcarry, 0.0), [], [("qkc", c) for c in range(4)])
        P.op("dve", lambda e: e.memset(self.ucarry, 0.0), [], [("ucar", c) for c in range(4)])
        P.op("pool", lambda e: e.memset(self.qTz[64:128, :, 0, :], 0.0), [], ["qTw"])
        P.op("pool", lambda e: e.memset(self.qTz[0:64, :, 1, :], 0.0), [], ["qTw"])
        for g, (t0, t1) in enumerate(GROUPS):
            a, b = t0 * 128, t1 * 128
            W = b - a
            ntg = t1 - t0
            self.rmsnorm(a, b, V_GMIX, self.hT, ["hT"])
            for ct in range(4):
                dst = None if ct < 2 else self.kTw[:, ct - 2, 0:W]
                self.qk_tile(g, ct, W, dst, "qTw" if ct < 2 else "kTw")
            self.gates_group(g, t0, ntg)
            for j in range(ntg):
                if t0 + j >= inj:
                    self.v_tile(j, t0 + j)
                    self.k_transpose(j)
            for i in range(4):
                self.o_tile(i, W)
            for gi in range(4):
                self.u_tile(g, gi, W)
            for j in range(ntg):
                if t0 + j >= inj:
                    self.s2_tile(j, t0 + j)
            if g >= first_out_group:
                self.o_group(a, b)

    def build_fused(self):
        nc, P = self.nc, self.P
        self.declare_fused()
        import contextlib

        with contextlib.ExitStack() as st:
            sb = lambda name, shape, dt_: st.enter_context(nc.sbuf_tensor("sb_" + name, shape, dt_))[:]
            self.x = sb("x", [128, KT, NT], F32)
            self.cf = sb("cf", [128, 512], F32)
            self.percore = sb("percore", [128, 80], F32)
            self.vecs = sb("vecs", [128, 2, VW], F32)
            self.identb = sb("identb", [128, 128], BF16)
            self.onesb = sb("onesb", [128, 128], BF16)
            self.epsb = sb("epsb", [128, 1], F32)
            self.wA = sb("wA", [128, KT, 1024], BF16)
            self.wu = [sb("wu0", [128, KT, 1024], BF16), None]
            self.wd = [sb("wd0", [128, 4, 1024], BF16), None]
            self.qkcarry = sb("qkcarry", [128, 4, 3], F32)
            self.ucarry = sb("ucarry", [128, 4, 16], F32)
            self.gsb = sb("gsb", [128, 4, 8], F32)
            self.lfn = sb("lfn", [128, 4, 4], F32)
            self.cS = sb("cS", [128, NTL, 4], F32)
            self.emb = sb("emb", [128, NTL, 4], F32)
            self.Gp = sb("Gp", [128, NTL, 2], F32)
            self.Cst = sb("Cst", [128, 2, 129], F32)
            self.Cb = sb("Cb", [128, 2, 129], BF16)
            self.small = sb("small", [128, 32], F32)
            self.tmp16 = sb("tmp16", [128, 16], F32)
            self.xh_s = sb("xh_s", [128, 16], F32)
            self.hall = sb("hall", [128, 8, 16], F32)
            self.u_bytes = (nc.sbuf_bytes_remaining - 256) // 64 * 64
            self.U = sb("U", [128, self.u_bytes // 2], BF16)
            self.ps = st.enter_context(nc.psum_tensor("ps", [128, 8, 512], F32))[:]
            self.mm_banks, self.n_mm, self.mm_i = [0, 1, 2], 3, 0
            self.sq_i = self.pre_i = self.dT_i = 0
            cv = self.carve

            self.u_off = 0
            self.xtok = [cv(F32, [128, D]) for _ in range(2)]
            self.load_common()
            dma(P, "pool", self.wA, self.w_in_src(0)[:, :, 0:1024], [], ["wA"], "wA")
            self.layer = 0
            self.load_ffn_chunk(0, 0, 0)
            self.load_x()

            for l in (0, 1):
                self.layer = l
                inj = 1 if l == 0 else 2
                pub = 17 if l == 0 else 18
                P.barrier()
                self.u_off = 0
                self.wB = cv(BF16, [128, KT, 1032])
                self.sq = [cv(BF16, [128, KT, 128]) for _ in range(2)]
                self.rstd = cv(F32, [128, 512])
                self.hT = cv(BF16, [128, KT, 512])
                self.pre = [cv(F32, [128, 528]) for _ in range(2)]
                self.acc = [cv(F32, [128, 528]) for _ in range(2)]
                self.kTw = cv(BF16, [128, 2, 512])
                self.ktok = cv(BF16, [128, 4, 256])
                self.vt = cv(BF16, [128, 4, 4, 129])
                self.stall = cv(F32, [128, 8, SW])
                dma(P, "pool", self.wB[:, :, 512:520], self.w_in_src(l)[:, :, 1536:1544], [], ["wB"], "wB")
                if l == 1:
                    ts(P, "dve", self.x[:, :, 0:256], self.x[:, :, 0:256], self.percore[:, 64:65], None, ALU.mult,
                       None, xkeys(0, 256) + ["percore"], xkeys(0, 256))
                self.phase_pass1(inj, pub)
                self.phase_exchange()
                P.barrier()
                self.u_off = 0
                self.wB = cv(BF16, [128, KT, 1032])
                self.wout = cv(BF16, [128, KT, 1024])
                self.wpool = cv(BF16, [128, 4, 128])
                self.sq = [cv(BF16, [128, KT, 128]) for _ in range(2)]
                self.rstd = cv(F32, [128, 512])
                self.hT = cv(BF16, [128, KT, 512])
                self.pre = [cv(F32, [128, 528]) for _ in range(2)]
                self.acc = [cv(F32, [128, 528]) for _ in range(2)]
                self.qTz = cv(BF16, [128, 2, 2, 512])
                self.kTw = cv(BF16, [128, 2, 512])
                self.ktok = cv(BF16, [128, 4, 256])
                self.vt = cv(BF16, [128, 4, 4, 129])
                self.sigo = cv(BF16, [128, 4, 512])
                self.hpT = cv(BF16, [128, 4, 512])
                self.dT = [cv(BF16, [128, 512]) for _ in range(2)]
                self.PTm = cv(BF16, [128, 4, 128])
                self.hn = cv(BF16, [128, 4, 128])
                self.sqN = cv(F32, [128, 4, 128])
                dma(P, "pool", self.wB, self.w_in_src(l)[:, :, 1024:INW], [], ["wB"], "wB")
                dma(P, "pool", self.wout, self.d_w_out[l].rearrange("(kt p) n -> p kt n", p=128), [], ["wout"], "wout")
                dma(P, "pool", self.wpool, self.d_w_pool[l].rearrange("g c d -> c g d"), [], ["wpool"], "wpool")
                self.phase_mixer(inj, 0 if l == 0 else 1)
                if l == 1:
                    self.phase_halo_exchange()
                P.barrier()
                self.u_off = 0
                self.wu[1] = cv(BF16, [128, KT, 1024])
                self.wd[1] = cv(BF16, [128, 4, 1024])
                self.hT2 = cv(BF16, [128, KT, NT + 2])
                self.sq = [cv(BF16, [128, KT, 128]) for _ in range(2)]
                self.rstd = cv(F32, [128, 512])
                self.accg = [cv(F32, [128, 512]) for _ in range(2)]
                self.accv = [cv(F32, [128, 512]) for _ in range(2)]
                self.sg = [cv(F32, [128, 512]) for _ in range(2)]
                self.actb = [cv(BF16, [128, 4, 512]) for _ in range(2)]
                self.load_ffn_chunk(l, 1, 1)
                if l == 0:
                    dma(P, "pool", self.wA, self.w_in_src(1)[:, :, 0:1024], [], ["wA"], "wA")
                P.op("dve", lambda e: e.memset(self.hT2[:, :, 0:2], 0.0), [], ["hT2"])
                for g, (t0, t1) in enumerate(GROUPS):
                    a, b = t0 * 128, t1 * 128
                    self.rmsnorm(a, b, V_GF, self.hT2[:, :, 2 + a:2 + b], ["hT2"])
                self.ffn_phase(l, 128 * inj)
                if l == 0:
                    self.load_ffn_chunk(1, 0, 0)

            self.layer = 1
            P.barrier()
            self.u_off = 0
            self.sq = [cv(BF16, [128, KT, 128]) for _ in range(2)]
            self.rstd = cv(F32, [128, 512])
            yT = [cv(F32, [128, KT, 128]) for _ in range(2)]
            ytok = [cv(F32, [128, D]) for _ in range(2)]
            for t in range(2, NTL):
                b2 = t % 2
                a, b = t * 128, (t + 1) * 128
                self.rmsnorm(a, b, V_FIN, yT[b2], [("yT", b2)])
                banks = (4, 5) if b2 == 0 else (6, 7)
                for kt in range(KT):
                    bk = banks[kt // 4]
                    tr(P, self.ps[:, bk, (kt % 4) * 128:(kt % 4 + 1) * 128], yT[b2][:, kt, :], self.cf[:, 0:128],
                       [("yT", b2), "cf"], [("ps", bk)])
                cp(P, "act", ytok[b2][:, 0:512], self.ps[:, banks[0], :], [("ps", banks[0])], [("ytok", b2)])
                cp(P, "dve", ytok[b2][:, 512:1024], self.ps[:, banks[1], :], [("ps", banks[1])], [("ytok", b2)])
                o = dma(P, "sp", self.d_out[(t - 2) * 128:(t - 1) * 128, :], ytok[b2], [("ytok", b2)], [("yo", t)],
                        ("yout", b2))
                P.final.append(o)
            P.emit()
        return nc

    def build_state(self):
        nc, P = self.nc, self.P
        l = self.layer
        self.declare()
        import contextlib

        with contextlib.ExitStack() as st:
            sb = lambda name, shape, dt_: st.enter_context(nc.sbuf_tensor("sb_" + name, shape, dt_))[:]
            self.x = sb("x", [128, KT, NT], F32)
            self.cf = sb("cf", [128, 512], F32)
            self.percore = sb("percore", [128, 80], F32)
            self.vecs = sb("vecs", [128, 2, VW], F32)
            self.identb = sb("identb", [128, 128], BF16)
            self.onesb = sb("onesb", [128, 128], BF16)
            self.epsb = sb("epsb", [128, 1], F32)
            self.wA = sb("wA", [128, KT, 1024], BF16)
            self.wB = sb("wB", [128, KT, 1032], BF16)
            self.xtok = [sb("xtok%d" % i, [128, D], F32) for i in range(2)]
            self.sq = [sb("sq%d" % i, [128, KT, 128], BF16) for i in range(2)]
            self.rstd = sb("rstd", [128, 512], F32)
            self.hT = sb("hT", [128, KT, 512], BF16)
            self.pre = [sb("pre%d" % i, [128, 515], F32) for i in range(2)]
            self.acc = [sb("acc%d" % i, [128, 512], F32) for i in range(2)]
            self.qkcarry = sb("qkcarry", [128, 4, 3], F32)
            self.kTw = sb("kTw", [128, 2, 512], BF16)
            self.ktok = sb("ktok", [128, 4, 256], BF16)
            self.vt = sb("vt", [128, 4, 4, 129], BF16)
            self.gsb = sb("gsb", [128, 4, 8], F32)
            self.lfn = sb("lfn", [128, 4, 4], F32)
            self.cS = sb("cS", [128, NTL, 4], F32)
            self.emb = sb("emb", [128, NTL, 4], F32)
            self.Gp = sb("Gp", [128, NTL, 2], F32)
            self.Cst = sb("Cst", [128, 2, 129], F32)
            self.ps = st.enter_context(nc.psum_tensor("ps", [128, 8, 512], F32))[:]
            self.mm_banks, self.n_mm, self.mm_i = [0, 1, 2], 3, 0
            self.sq_i = 0
            self.pre_i = 0

            self.load_common()
            self.load_w_in(l, only_k_v_g=True)
            self.load_x()
            P.op("dve", lambda e: e.memset(self.qkcarry, 0.0), [], [("qkc", c) for c in range(4)])
            P.op("dve", lambda e: e.memset(self.Cst, 0.0), [], ["Cst"])
            for g, (t0, t1) in enumerate(GROUPS):
                if t0 >= 17:
                    break
                a, b = t0 * 128, t1 * 128
                W = b - a
                ntg = t1 - t0
                self.rmsnorm(a, b, V_GMIX, self.hT, ["hT"])
                for ct in (2, 3):
                    self.qk_tile(g, ct, W, self.kTw[:, ct - 2, 0:W], "kTw")
                self.gates_group(g, t0, ntg)
                for j in range(ntg):
                    t = t0 + j
                    if t < 1 or t >= 17:
                        continue
                    self.v_tile(j, t)
                    self.k_transpose(j)
                    self.kv_update(j, t)
            o = dma(P, "sp", self.d_state_out, self.Cst.rearrange("p a b -> p (a b)"), ["Cst"], ["dout"], "dout")
            P.final.append(o)
            P.emit()
        return nc


def build_main_prog(layer, final):
    b = Builder("main", layer, final)
    return b.build_main()


def build_state_prog(layer):
    b = Builder("state", layer, False)
    return b.build_state()


def make_consts():
    c = np.zeros((128, 512), np.float32)
    c[:, 0:128] = np.eye(128, dtype=np.float32)
    tri = (np.arange(128)[:, None] <= np.arange(128)[None, :]).astype(np.float32)
    c[:, 128:256] = tri
    c[:, 256:384] = 1.0
    c[:, 384:512] = tri * 0.125
    return c


def make_percore():
    pcs = []
    for c in range(8):
        pc = np.zeros((128, 80), np.float32)
        first = (c % 2 == 0)
        for g, w in enumerate((2, 4, 8, 16)):
            for i in range(16):
                pc[:, g * 16 + i] = 1.0 / (min(i + 1, w) if first else w)
        pc[:, 64] = 0.0 if first else 1.0
        if not first:
            pc[:, 65 + c - 1] = 1.0
        pcs.append(pc)
    return pcs


def make_vecs(inp):
    v = np.zeros((2, 128, VW), np.float32)
    for l in range(2):
        v[l, :, V_GMIX:V_GMIX + 8] = inp["mix_norm"][l].reshape(8, 128).T
        v[l, :, V_QKC:V_QKC + 16] = inp["w_qk_conv"][l].reshape(4, 4, 128).transpose(2, 1, 0).reshape(128, 16)
        v[l, :, V_BG:V_BG + 8] = inp["b_gates"][l][None, :]
        v[l, :, V_GH:V_GH + 4] = inp["head_norm"][l].reshape(4, 128).T
        v[l, :, V_PS:V_PS + 4] = inp["pool_scale"][l].reshape(4, 128).T
        v[l, :, V_GF:V_GF + 8] = inp["ffn_norm"][l].reshape(8, 128).T
        v[l, :, V_FC:V_FC + 132] = inp["w_ffn_conv"][l].reshape(3, 44, 128).transpose(2, 1, 0).reshape(128, 132)
        v[l, :, V_FB:V_FB + 44] = inp["b_ffn_conv"][l].reshape(44, 128).T
        v[l, :, V_FIN:V_FIN + 8] = inp["final_norm"].reshape(8, 128).T
    return v


def make_xloc(xfull):
    out = []
    for c in range(8):
        b, half = c // 2, c % 2
        xl = np.zeros((NT, D), np.float32)
        s = half * 2048
        xl[256:] = xfull[b, s:s + 2048]
        if half == 1:
            xl[:256] = xfull[b, s - 256:s]
        out.append(xl)
    return out


def host_prep(inp):
    return dict(consts=make_consts(), percore=make_percore(), vecs=make_vecs(inp), x_loc=make_xloc(inp["x"]))


def _run_layer(l, final, host, xlocs, inp):
    common = dict(consts=host["consts"], vecs=host["vecs"], w_in=inp["w_in"])
    nc_s = build_state_prog(l)
    maps = [dict(common, x_loc=xlocs[c], percore=host["percore"][c]) for c in range(8)]
    res = run_bass_kernel_spmd(nc_s, maps, core_ids=list(range(8)))
    states = [np.asarray(res.results[c]["state_out"], np.float32) for c in range(8)]
    zero = np.zeros((128, SW), np.float32)
    nc_m = build_main_prog(l, final)
    maps = [dict(common, x_loc=xlocs[c], percore=host["percore"][c], w_pool=inp["w_pool"], w_out=inp["w_out"],
                 w_up=inp["w_up"], w_down=inp["w_down"], state_in=(states[c - 1] if c % 2 == 1 else zero))
            for c in range(8)]
    res = run_bass_kernel_spmd(nc_m, maps, core_ids=list(range(8)))
    y = np.zeros((4, 4096, D), np.float32)
    for c in range(8):
        y[c // 2, (c % 2) * 2048:(c % 2 + 1) * 2048] = res.results[c]["y"]
    return y


def build_fused_prog():
    b = Builder("fused", 0, True)
    return b.build_fused()


def kernel(**inputs):
    inp = {k: np.ascontiguousarray(np.asarray(v, dtype=np.float32)) for k, v in inputs.items()}
    host = host_prep(inp)
    nc = build_fused_prog()
    maps = [dict(consts=host["consts"], vecs=host["vecs"], x_loc=host["x_loc"][c], percore=host["percore"][c],
                 w_in=inp["w_in"], w_pool=inp["w_pool"], w_out=inp["w_out"], w_up=inp["w_up"], w_down=inp["w_down"])
            for c in range(8)]
    res = run_bass_kernel_spmd(nc, maps, core_ids=list(range(8)))
    y = np.zeros((4, 4096, D), np.float32)
    for c in range(8):
        y[c // 2, (c % 2) * 2048:(c % 2 + 1) * 2048] = res.results[c]["y"]
    return y
```

```python
import numpy as np
import concourse.bass as bass
import concourse.mybir as mybir
from concourse.bass_utils import run_bass_kernel_spmd

F32 = mybir.dt.float32
BF16 = mybir.dt.bfloat16
AF = mybir.ActivationFunctionType
ALU = mybir.AluOpType
AX = mybir.AxisListType

D = 1024
KT = 8
NTL = 18
NT = NTL * 128
INW = 2056
DFF = 2816
FT = 22
EPS = 1e-6
GROUPS = [(0, 2), (2, 6), (6, 10), (10, 14), (14, 18)]
FCH = [(0, 4), (4, 8), (8, 12), (12, 16), (16, 19), (19, 22)]
VW = 232
V_GMIX, V_QKC, V_BG, V_GH, V_PS, V_GF, V_FC, V_FB, V_FIN = 0, 8, 24, 32, 36, 40, 48, 180, 224
SW = 258


class Prog:
    ENGS = ("pe", "act", "dve", "pool", "sp")

    def __init__(self, nc):
        self.nc = nc
        self.ops = []
        self.last_w = {}
        self.readers = {}
        self.final = []
        self.phase = 0
        self.inorder_engs = ("act", "dve", "pool", "sp")

    def barrier(self):
        self.phase += 1
        self.last_w = {}
        self.readers = {}

    def op(self, eng, fn, r=(), w=(), dma=False, semkey=None, inc=16, cost=0.2, lat=None, tab=None):
        i = len(self.ops)
        deps = set()
        for k in r:
            if k in self.last_w:
                deps.add(self.last_w[k])
        for k in w:
            if k in self.last_w:
                deps.add(self.last_w[k])
            for rd in self.readers.get(k, ()):
                deps.add(rd)
        deps.discard(i)
        self.ops.append(dict(eng=eng, fn=fn, deps=deps, dma=dma, semkey=semkey, val=None, inc=inc, phase=self.phase,
                             cost=cost, lat=(cost if lat is None else lat), tab=tab, wkey=tuple(w)))
        for k in r:
            self.readers.setdefault(k, []).append(i)
        for k in w:
            self.last_w[k] = i
            self.readers[k] = []
        return i

    def schedule(self):
        import heapq
        ops = self.ops
        order = {e: [] for e in self.ENGS}
        nph = self.phase + 1
        byphase = [[] for _ in range(nph)]
        for i, o in enumerate(ops):
            byphase[o["phase"]].append(i)
        tnow = 0.0
        for ph in range(nph):
            ids = byphase[ph]
            if not ids:
                continue
            unit_of = {}
            units = []
            last_pe_unit = None
            for i in ids:
                o = ops[i]
                if o["eng"] == "pe" and not o["dma"]:
                    wk = o.get("wkey")
                    if (last_pe_unit is not None and units[last_pe_unit][1] == wk
                            and len(units[last_pe_unit][0]) < 40
                            and all(d < units[last_pe_unit][0][0] or unit_of.get(d) == last_pe_unit
                                    for d in o["deps"])):
                        units[last_pe_unit][0].append(i)
                        unit_of[i] = last_pe_unit
                        continue
                    units.append(([i], wk))
                    last_pe_unit = len(units) - 1
                    unit_of[i] = last_pe_unit
                else:
                    units.append(([i], None))
                    unit_of[i] = len(units) - 1
            nu = len(units)
            ueng = [ops[units[u][0][0]]["eng"] for u in range(nu)]
            udeps = [set() for _ in range(nu)]
            for u in range(nu):
                for m in units[u][0]:
                    for d in ops[m]["deps"]:
                        if d in unit_of and unit_of[d] != u:
                            udeps[u].add(unit_of[d])
            succ = [[] for _ in range(nu)]
            indeg = [0] * nu
            for u in range(nu):
                indeg[u] = len(udeps[u])
                for d in udeps[u]:
                    succ[d].append(u)
            ready = {e: [] for e in self.ENGS}
            for u in range(nu):
                if indeg[u] == 0:
                    heapq.heappush(ready[ueng[u]], u)
            inorder = getattr(self, "inorder_engs", ())
            nxt = {e: [u for u in range(nu) if ueng[u] == e] for e in inorder}
            nptr = {e: 0 for e in inorder}
            efree = {e: tnow for e in self.ENGS}
            ufin = [0.0] * nu
            rdy_t = [tnow] * nu
            cur_tab = None
            left = nu
            while left:
                best = None
                for e in self.ENGS:
                    if not ready[e]:
                        continue
                    if e in inorder:
                        want = nxt[e][nptr[e]]
                        cands = [want] if want in ready[e] else []
                    else:
                        cands = heapq.nsmallest(6, ready[e])
                    for u in cands:
                        st_ = max(efree[e], rdy_t[u])
                        o0 = ops[units[u][0][0]]
                        if e == "act" and o0["tab"] is not None and o0["tab"] != cur_tab:
                            st_ += 1.3
                        key = (st_, u)
                        if best is None or key < best[0]:
                            best = (key, u, e, st_)
                _, u, e, st_ = best
                if e in inorder:
                    nptr[e] += 1
                ready[e].remove(u)
                heapq.heapify(ready[e])
                o0 = ops[units[u][0][0]]
                if e == "act" and o0["tab"] is not None:
                    cur_tab = o0["tab"]
                t = st_
                lat_extra = 0.0
                for m in units[u][0]:
                    t += ops[m]["cost"]
                    lat_extra = ops[m]["lat"] - ops[m]["cost"]
                    order[e].append(m)
                efree[e] = t
                ufin[u] = t + lat_extra
                left -= 1
                for s_ in succ[u]:
                    lat_sync = 0.0 if (ueng[s_] == "pe" and e == "pe" and not o0["dma"]) else 0.15
                    rdy_t[s_] = max(rdy_t[s_], ufin[u] + lat_sync)
                    indeg[s_] -= 1
                    if indeg[s_] == 0:
                        heapq.heappush(ready[ueng[s_]], s_)
            tnow = max(list(efree.values()) + ufin)
        self.est_total = tnow
        return order

    def check_progress(self, order, skip):
        ops = self.ops
        sem = {}
        ptr = {e: 0 for e in self.ENGS}
        total = sum(len(v) for v in order.values())
        done = 0
        while done < total:
            prog = False
            for e in self.ENGS:
                while ptr[e] < len(order[e]):
                    i = order[e][ptr[e]]
                    o = ops[i]
                    reqs = list(o["cdeps"].values()) + [di for di in o["deps"] if ops[di]["dma"]]
                    ok = True
                    for di in reqs:
                        d = ops[di]
                        key = ("d", d["semkey"]) if d["dma"] else ("e", d["eng"])
                        if sem.get(key, 0) < d["val"]:
                            ok = False
                            break
                    if not ok:
                        break
                    if o["val"] is not None:
                        key = ("d", o["semkey"]) if o["dma"] else ("e", e)
                        sem[key] = sem.get(key, 0) + (o["inc"] if o["dma"] else 1)
                        assert sem[key] == o["val"], ("sem value mismatch", i, e, sem[key], o["val"])
                    ptr[e] += 1
                    done += 1
                    prog = True
            if not prog:
                stuck = {e: (order[e][ptr[e]] if ptr[e] < len(order[e]) else None) for e in self.ENGS}
                raise RuntimeError("deadlock in semaphore protocol: %r" % (stuck,))

    def emit(self):
        nc = self.nc
        ops = self.ops
        order = self.schedule()

        def skip(o, d):
            return (not o["dma"]) and (not d["dma"]) and o["eng"] == "pe" and d["eng"] == "pe"

        last_eng = {}
        last_dma = {}
        extra = {}
        cur_phase_last = {}
        nph = self.phase + 1
        per_phase_order = {e: {} for e in self.ENGS}
        for e in self.ENGS:
            for i in order[e]:
                per_phase_order[e].setdefault(ops[i]["phase"], []).append(i)
        acc_last = set()
        for ph in range(nph):
            if ph > 0 and acc_last:
                for e in self.ENGS:
                    lst = per_phase_order[e].get(ph)
                    if lst:
                        extra[lst[0]] = set(acc_last)
            for e in self.ENGS:
                lst = per_phase_order[e].get(ph)
                if not lst:
                    continue
                comp = [i for i in lst if not ops[i]["dma"]]
                if comp:
                    last_eng[e] = comp[-1]
                for i in lst:
                    if ops[i]["dma"]:
                        last_dma[ops[i]["semkey"]] = i
            acc_last = set(last_eng.values()) | set(last_dma.values())
        for i, s_ in extra.items():
            ops[i]["deps"] = set(ops[i]["deps"]) | (s_ - {i})

        pos = {}
        for e in self.ENGS:
            for n_, i in enumerate(order[e]):
                pos[i] = n_
        needed = set()
        for o in ops:
            latest = {}
            for di in o["deps"]:
                d = ops[di]
                if skip(o, d) or d["dma"]:
                    continue
                if d["eng"] not in latest or pos[di] > pos[latest[d["eng"]]]:
                    latest[d["eng"]] = di
            needed.update(latest.values())
            o["cdeps"] = latest
        for i in self.final:
            needed.add(i)
        cnt = {e: 0 for e in self.ENGS}
        dcnt = {}
        for e in self.ENGS:
            for i in order[e]:
                o = ops[i]
                if o["dma"]:
                    k = o["semkey"]
                    dcnt[k] = dcnt.get(k, 0) + o["inc"]
                    o["val"] = dcnt[k]
                elif i in needed:
                    cnt[e] += 1
                    o["val"] = cnt[e]
        self.check_progress(order, skip)
        import contextlib

        with contextlib.ExitStack() as st:
            esem = {e: st.enter_context(nc.semaphore("s_" + e)) for e in self.ENGS}
            dsem = {}
            for k in dcnt:
                dsem[k] = st.enter_context(nc.semaphore("d_%d" % len(dsem)))
            block = st.enter_context(nc.Block())

            def section(ename):
                def body(eng):
                    waited = {}

                    def wait_all(dlist):
                        req = {}
                        for di in dlist:
                            d = ops[di]
                            if d["dma"]:
                                key = ("d", d["semkey"])
                                s = dsem[d["semkey"]]
                            else:
                                key = ("e", d["eng"])
                                s = esem[d["eng"]]
                            if d["val"] > req.get(key, (None, 0))[1]:
                                req[key] = (s, d["val"])
                        for key, (s, v) in req.items():
                            if waited.get(key, 0) >= v:
                                continue
                            eng.wait_ge(s, v)
                            waited[key] = v

                    for i in order[ename]:
                        o = ops[i]
                        wait_all(list(o["cdeps"].values()) + [di for di in o["deps"] if ops[di]["dma"]])
                        ins = o["fn"](eng)
                        if o["val"] is not None:
                            if o["dma"]:
                                ins.then_inc(dsem[o["semkey"]], o["inc"])
                            else:
                                ins.then_inc(esem[ename], 1)
                    if ename == "sp":
                        wait_all(self.final)

                return body

            block.tensor(section("pe"))
            block.scalar(section("act"))
            block.vector(section("dve"))
            block.gpsimd(section("pool"))
            block.sync(section("sp"))


def fsz(ap):
    n = 1
    for s in ap.shape[1:]:
        n *= int(s)
    return n


def mm(P, out, lhsT, rhs, start, stop, r, w):
    n = max(fsz(rhs), 64)
    c = n / 1950.0 * (4.0 if rhs.dtype == F32 else 1.0) + 0.012
    return P.op("pe", lambda e: e.matmul(out, lhsT, rhs, start=start, stop=stop), r, w, cost=c, lat=c + 0.2)


def tr(P, out, in_, ident, r, w):
    c = max(fsz(in_), 64) / 1950.0 * (4.0 if in_.dtype == F32 else 1.0) + 0.03
    return P.op("pe", lambda e: e.transpose(out, in_, ident), r, w, cost=c, lat=c + 0.2)


_TAB = {AF.Exp: "ln_exp", AF.Ln: "ln_exp", AF.Silu: "silu", AF.Sigmoid: "sigmoid"}


def act(P, out, in_, func, r, w, bias=0.0, scale=1.0):
    c = 0.2 + fsz(out) / 1150.0
    if not isinstance(bias, float):
        c += 0.09
    if not isinstance(scale, float):
        c += 0.09
    return P.op("act", lambda e: e.activation(out, in_, func, bias=bias, scale=scale), r, w, cost=c,
                tab=_TAB.get(func))


def _vcost(eng, n, f=1.0):
    if eng == "pool":
        return 0.15 + n * f / 480.0
    return 0.07 + n * f / 960.0


def tt(P, eng, out, in0, in1, op, r, w):
    return P.op(eng, lambda e: e.tensor_tensor(out, in0, in1, op), r, w, cost=_vcost(eng, fsz(out)))


def stt(P, out, in0, scalar, in1, op0, op1, r, w):
    return P.op("dve", lambda e: e.scalar_tensor_tensor(out, in0, scalar, in1, op0, op1), r, w,
                cost=_vcost("dve", fsz(out), 1.42))


def ts(P, eng, out, in0, s1, s2, op0, op1, r, w):
    c = _vcost(eng, fsz(out))
    if s2 is None:
        return P.op(eng, lambda e: e.tensor_scalar(out, in0, s1, None, op0), r, w, cost=c)
    return P.op(eng, lambda e: e.tensor_scalar(out, in0, s1, s2, op0, op1), r, w, cost=c)


def cp(P, eng, out, in_, r, w):
    if eng == "act":
        return P.op("act", lambda e: e.copy(out, in_), r, w, cost=0.2 + fsz(out) / 1150.0)
    return P.op(eng, lambda e: e.tensor_copy(out, in_), r, w, cost=_vcost(eng, fsz(out)))


def dma(P, eng, out, in_, r, w, semkey):
    nbytes = fsz(out) * 128 * (4 if in_.dtype == F32 else 2)
    lat = 2.5 + nbytes / 150e3
    return P.op(eng, lambda e: e.dma_start(out=out, in_=in_), r, w, dma=True, semkey=semkey,
                cost=(1.5 if eng == "pool" else 0.1), lat=lat)


def xkeys(a, b):
    return [("x", t) for t in range(a // 128, (b - 1) // 128 + 1)]


class Builder:
    def __init__(self, mode, layer, final):
        self.mode = mode
        self.layer = layer
        self.final = final
        self.nc = bass.Bass("TRN2", target_bir_lowering=False)
        self.P = Prog(self.nc)

    def declare(self):
        nc = self.nc
        dt = nc.dram_tensor
        self.d_x = dt("x_loc", [NT, D], F32, kind="ExternalInput").ap()
        self.d_consts = dt("consts", [128, 512], F32, kind="ExternalInput").ap()
        self.d_percore = dt("percore", [128, 80], F32, kind="ExternalInput").ap()
        self.d_vecs = dt("vecs", [2, 128, VW], F32, kind="ExternalInput").ap()
        self.d_w_in = dt("w_in", [2, D, INW], F32, kind="ExternalInput").ap()
        if self.mode == "main":
            self.d_w_pool = dt("w_pool", [2, 4, 128, 128], F32, kind="ExternalInput").ap()
            self.d_w_out = dt("w_out", [2, D, D], F32, kind="ExternalInput").ap()
            self.d_w_up = dt("w_up", [2, D, 2 * DFF], F32, kind="ExternalInput").ap()
            self.d_w_down = dt("w_down", [2, DFF, D], F32, kind="ExternalInput").ap()
            self.d_state_in = dt("state_in", [128, SW], F32, kind="ExternalInput").ap()
            self.d_out = dt("y", [2048, D], F32, kind="ExternalOutput").ap()
        else:
            self.d_state_out = dt("state_out", [128, SW], F32, kind="ExternalOutput").ap()

    def carve(self, nbytes_dtype, shape):
        dtype = nbytes_dtype
        n = int(np.prod(shape[1:]))
        nb = n * (4 if dtype == F32 else 2)
        nb = (nb + 63) // 64 * 64
        off = self.u_off
        assert off + nb <= self.u_bytes, ("U overflow", off, nb, self.u_bytes)
        self.u_off += nb
        ap = self.U[:, off // 2: off // 2 + nb // 2]
        if dtype == F32:
            ap = ap.bitcast(F32)
        ap = ap[:, 0:n]
        if len(shape) == 3:
            ap = ap.rearrange("p (a b) -> p a b", a=shape[1])
        elif len(shape) == 4:
            ap = ap.rearrange("p (a b c) -> p a b c", a=shape[1], b=shape[2])
        return ap

    def vec(self, off, n=1):
        return self.vecs[:, self.layer, off:off + n]

    def rmsnorm(self, a, b, gv_off, out_ap, out_keys, okey_r=()):
        P = self.P
        W = b - a
        xk = xkeys(a, b)
        bank = self.mmbank()
        ssps = self.ps[:, bank, 0:W]
        nsub = W // 128
        for j in range(nsub):
            sb = self.sq_i % 2
            self.sq_i += 1
            sq = self.sq[sb]
            act(P, sq, self.x[:, :, a + j * 128: a + (j + 1) * 128], AF.Square, xk, [("sq", sb)])
            for kt in range(KT):
                mm(P, self.ps[:, bank, j * 128:(j + 1) * 128], self.onesb, sq[:, kt, :], kt == 0, kt == KT - 1,
                   [("sq", sb), "onesb"], [("ps", bank)])
        act(P, self.rstd[:, 0:W], ssps, AF.Ln, [("ps", bank)], ["rstd"], bias=self.epsb[:, 0:1], scale=1.0 / D)
        act(P, self.rstd[:, 0:W], self.rstd[:, 0:W], AF.Exp, ["rstd"], ["rstd"], scale=-0.5)
        for kt in range(KT):
            stt(P, out_ap[:, kt, 0:W], self.x[:, kt, a:b], self.vec(gv_off + kt), self.rstd[:, 0:W],
                ALU.mult, ALU.mult, xk + ["rstd", "vecs"] + list(okey_r), out_keys)

    def mmbank(self):
        b = self.mm_i % self.n_mm
        self.mm_i += 1
        return self.mm_banks[b]

    def load_common(self):
        P = self.P
        dma(P, "sp", self.cf, self.d_consts, [], ["cf"], "cf")
        dma(P, "sp", self.percore, self.d_percore, [], ["percore"], "percore")
        dma(P, "sp", self.vecs, self.d_vecs.rearrange("l p v -> p l v"), [], ["vecs"], "vecs")
        cp(P, "dve", self.identb, self.cf[:, 0:128], ["cf"], ["identb"])
        cp(P, "dve", self.onesb, self.cf[:, 256:384], ["cf"], ["onesb"])
        P.op("dve", lambda e: e.memset(self.epsb, EPS), [], ["epsb"])

    def load_x(self):
        P = self.P
        for t in range(NTL):
            b = t % 2
            xt = self.xtok[b]
            xk_ = getattr(self, "xtok_keys", [("xtok", 0), ("xtok", 1)])[b]
            dma(P, "sp", xt, self.d_x[t * 128:(t + 1) * 128, :], [], [xk_], ("xtok", b))
            banks = (0, 1) if b == 0 else (2, 3)
            for kt in range(KT):
                bk = banks[kt // 4]
                tr(P, self.ps[:, bk, (kt % 4) * 128:(kt % 4 + 1) * 128], xt[:, kt * 128:(kt + 1) * 128],
                   self.cf[:, 0:128], [xk_, "cf"], [("ps", bk)])
            for hf in range(2):
                bk = banks[hf]
                src = self.ps[:, bk, :].rearrange("p (a b) -> p a b", a=4)
                dst = self.x[:, hf * 4:(hf + 1) * 4, t * 128:(t + 1) * 128]
                eng = "act" if hf == 0 else "dve"
                cp(P, eng, dst, src, [("ps", bk)], [("x", t)])

    def load_w_in(self, l, only_k_v_g=False):
        P = self.P
        src = self.d_w_in[l].rearrange("(kt p) n -> p kt n", p=128)
        dma(P, "pool", self.wA, src[:, :, 0:1024], [], ["wA"], "wA")
        if only_k_v_g:
            dma(P, "pool", self.wB[:, :, 512:520], src[:, :, 1536:1544], [], ["wB"], "wB")
        else:
            dma(P, "pool", self.wB, src[:, :, 1024:INW], [], ["wB"], "wB")

    def qk_tile(self, g, ct, W, dst, dst_key):
        P = self.P
        bank = self.mmbank()
        for kt in range(KT):
            mm(P, self.ps[:, bank, 0:W], self.wA[:, kt, ct * 128:(ct + 1) * 128], self.hT[:, kt, 0:W],
               kt == 0, kt == KT - 1, ["wA", "hT"], [("ps", bank)])
        pb = self.pre_i % 2
        self.pre_i += 1
        pre = self.pre[pb]
        acc = self.acc[pb]
        cp(P, "act", pre[:, 3:3 + W], self.ps[:, bank, 0:W], [("ps", bank)], [("pre", pb)])
        cp(P, "pool", pre[:, 0:3], self.qkcarry[:, ct, :], [("qkc", ct)], [("pre", pb)])
        cp(P, "pool", self.qkcarry[:, ct, :], pre[:, W:W + 3], [("pre", pb)], [("qkc", ct)])
        wv = lambda j: self.vec(V_QKC + ct * 4 + j)
        ts(P, "dve", acc[:, 0:W], pre[:, 3:3 + W], wv(3), None, ALU.mult, None, [("pre", pb), "vecs"], [("acc", pb)])
        for j in (2, 1, 0):
            stt(P, acc[:, 0:W], pre[:, j:j + W], wv(j), acc[:, 0:W], ALU.mult, ALU.add,
                [("pre", pb), ("acc", pb), "vecs"], [("acc", pb)])
        if dst is None:
            for e in range(2):
                es = slice(e * 64, (e + 1) * 64)
                act(P, self.qTz[es, ct, e, 0:W], acc[es, 0:W], AF.Silu, [("acc", pb)], [dst_key])
        else:
            act(P, dst, acc[:, 0:W], AF.Silu, [("acc", pb)], [dst_key])

    def gates_group(self, g, t0, ntg):
        P = self.P
        G3 = ("ps", 3)
        for j in range(ntg):
            for kt in range(KT):
                mm(P, self.ps[:, 3, 32 + j * 8: 40 + j * 8], self.hT[:, kt, j * 128:(j + 1) * 128],
                   self.wB[:, kt, 512:520], kt == 0, kt == KT - 1, ["hT", "wB"], [G3])
        gv = self.ps[:, 3, 32:32 + ntg * 8].rearrange("p (a b) -> p a b", a=ntg)
        bg = self.vecs[:, self.layer, V_BG:V_BG + 8].unsqueeze(1).to_broadcast([128, ntg, 8])
        tt(P, "dve", self.gsb[:, 0:ntg, :], gv, bg, ALU.add, [G3, "vecs"], ["gsb"])
        act(P, self.lfn[:, 0:ntg, :], self.gsb[:, 0:ntg, 4:8], AF.Exp, ["gsb"], ["lfn"], scale=-1.0)
        act(P, self.lfn[:, 0:ntg, :], self.lfn[:, 0:ntg, :], AF.Ln, ["lfn"], ["lfn"], bias=1.0)
        lf2 = self.lfn[:, 0:ntg, :]
        mm(P, self.ps[:, 3, 0:ntg * 4], self.cf[:, 128:256], lf2, True, True, ["cf", "lfn"], [G3])
        mm(P, self.ps[:, 3, 16:16 + ntg * 4], self.cf[:, 256:384], lf2, True, True, ["cf", "lfn"], [G3])
        bneg = self.ps[:, 3, 0:ntg * 4].rearrange("p (a b) -> p a b", a=ntg)
        totn = self.ps[:, 3, 16:16 + ntg * 4].rearrange("p (a b) -> p a b", a=ntg)
        tt(P, "dve", self.cS[:, t0:t0 + ntg, :], self.gsb[:, 0:ntg, 0:4], bneg, ALU.add, [G3, "gsb"], ["cS"])
        act(P, self.cS[:, t0:t0 + ntg, :], self.cS[:, t0:t0 + ntg, :], AF.Exp, ["cS"], ["cS"])
        act(P, self.emb[:, t0:t0 + ntg, :], bneg, AF.Exp, [G3], ["emb"])
        for e in range(2):
            sl = slice(e * 64, (e + 1) * 64)
            act(P, self.Gp[sl, t0:t0 + ntg, :], totn[sl, :, e::2], AF.Exp, [G3], ["Gp"], scale=-1.0)

    def v_tile(self, j, t):
        P = self.P
        bank = self.mmbank()
        for kt in range(KT):
            mm(P, self.ps[:, bank, :], self.hT[:, kt, j * 128:(j + 1) * 128], self.wA[:, kt, 512:1024],
               kt == 0, kt == KT - 1, ["hT", "wA"], [("ps", bank)])
        src = self.ps[:, bank, :].rearrange("p (a b) -> p a b", a=4)
        cb = self.cS[:, t, :].unsqueeze(2).to_broadcast([128, 4, 128])
        vb = j % 4
        tt(P, "dve", self.vt[:, vb, :, 0:128], src, cb, ALU.mult, [("ps", bank), "cS"], [("vt", vb)])
        cp(P, "dve", self.vt[:, vb, :, 128:129], self.cS[:, t, :].unsqueeze(2), ["cS"], [("vt", vb)])

    def k_transpose(self, j):
        P = self.P
        T5 = ("ps", 5)
        tp = self.ps[:, 5, :].bitcast(BF16)
        for pr in range(2):
            tr(P, tp[:, pr * 128:(pr + 1) * 128], self.kTw[:, pr, j * 128:(j + 1) * 128], self.identb,
               ["kTw", "identb"], [T5])
        cp(P, "act", self.ktok[:, j % 4, :], tp[:, 0:256], [T5], [("ktok", j % 4)])

    def kv_update(self, j, t):
        P = self.P
        vb = j % 4
        for h in range(4):
            pr, e = h // 2, h % 2
            bk = 6 + pr
            mm(P, self.ps[e * 64:(e + 1) * 64, bk, 258:387], self.ktok[:, j % 4, h * 64:(h + 1) * 64],
               self.vt[:, vb, h, :], True, True, [("ktok", j % 4), ("vt", vb)], [("ps", bk)])
        kv = self.ps[:, 6:8, 258:387]
        tt(P, "dve", self.Cst, self.Cst, kv, ALU.add, ["Cst", ("ps", 6), ("ps", 7)], ["Cst"])
        gb = self.Gp[:, t, :].unsqueeze(2).to_broadcast([128, 2, 129])
        tt(P, "dve", self.Cst, self.Cst, gb, ALU.mult, ["Cst", "Gp"], ["Cst"])
        if getattr(self, "make_cb", self.mode == "main"):
            act(P, self.Cb, self.Cst, AF.Copy, ["Cst"], ["Cb"], scale=0.125)


    def o_tile(self, i, W):
        P = self.P
        bank = self.mmbank()
        for kt in range(KT):
            mm(P, self.ps[:, bank, 0:W], self.wB[:, kt, i * 128:(i + 1) * 128], self.hT[:, kt, 0:W],
               kt == 0, kt == KT - 1, ["wB", "hT"], [("ps", bank)])
        act(P, self.sigo[:, i, 0:W], self.ps[:, bank, 0:W], AF.Sigmoid, [("ps", bank)], ["sigo"])

    def u_tile(self, g, gi, W):
        P = self.P
        bank = self.mmbank()
        for kt in range(KT):
            mm(P, self.ps[:, bank, 0:W], self.wB[:, kt, 520 + gi * 128:520 + (gi + 1) * 128], self.hT[:, kt, 0:W],
               kt == 0, kt == KT - 1, ["wB", "hT"], [("ps", bank)])
        ub, sA, sB = self.pre[0], self.pre[1], self.acc[0]
        kU, kA, kB = ("pre", 0), ("pre", 1), ("acc", 0)
        cp(P, "act", ub[:, 16:16 + W], self.ps[:, bank, 0:W], [("ps", bank)], [kU])
        cp(P, "pool", ub[:, 0:16], self.ucarry[:, gi, :], [("ucar", gi)], [kU])
        cp(P, "pool", self.ucarry[:, gi, :], ub[:, W:W + 16], [kU], [("ucar", gi)])
        E = 16 + W
        tt(P, "pool", sA[:, 1:E], ub[:, 1:E], ub[:, 0:E - 1], ALU.add, [kU], [kA])
        fin, kf = sA, kA
        if gi >= 1:
            tt(P, "pool", sB[:, 3:E], sA[:, 3:E], sA[:, 1:E - 2], ALU.add, [kA], [kB])
            fin, kf = sB, kB
        if gi >= 2:
            tt(P, "pool", sA[:, 7:E], sB[:, 7:E], sB[:, 3:E - 4], ALU.add, [kB], [kA])
            fin, kf = sA, kA
        if gi >= 3:
            tt(P, "pool", sB[:, 15:E], sA[:, 15:E], sA[:, 7:E - 8], ALU.add, [kA], [kB])
            fin, kf = sB, kB
        w = float(2 ** (gi + 1))
        db = self.dT_i % 2
        self.dT_i += 1
        dT = self.dT[db]
        stt(P, dT[:, 0:W], fin[:, 16:16 + W], 1.0 / w, ub[:, 16:16 + W], ALU.mult, ALU.subtract,
            [kf, kU], [("dT", db)])
        if g == 1:
            tt(P, "dve", self.tmp16, fin[:, 16:32], self.percore[:, gi * 16:(gi + 1) * 16], ALU.mult,
               [kf, "percore"], ["tmp16"])
            tt(P, "dve", dT[:, 0:16], self.tmp16, ub[:, 16:32], ALU.subtract, ["tmp16", kU], [("dT", db)])
        bank2 = self.mmbank()
        mm(P, self.ps[:, bank2, 0:W], self.wpool[:, gi, :], dT[:, 0:W], True, True, ["wpool", ("dT", db)],
           [("ps", bank2)])
        act(P, self.hpT[:, gi, 0:W], self.ps[:, bank2, 0:W], AF.Copy, [("ps", bank2), "vecs"], ["hpT"],
            scale=self.vec(V_PS + gi))

    def s2_tile(self, j, t):
        P = self.P
        js = slice(j * 128, (j + 1) * 128)
        vb = j % 4
        for h in range(4):
            pr, e = h // 2, h % 2
            es = slice(e * 64, (e + 1) * 64)
            mm(P, self.ps[:, 4, h * 128:(h + 1) * 128], self.kTw[:, pr, js], self.qTz[:, pr, e, js], True, True,
               ["kTw", "qTw"], [("ps", 4)])
        ptv = self.ps[:, 4, :].rearrange("p (a b) -> p a b", a=4)
        mb = self.cf[:, 384:512].unsqueeze(1).to_broadcast([128, 4, 128])
        tt(P, "dve", self.PTm, ptv, mb, ALU.mult, [("ps", 4), "cf"], ["PTm"])
        for h in range(4):
            pr, e = h // 2, h % 2
            es = slice(e * 64, (e + 1) * 64)
            bk = 6 + pr
            o = self.ps[:, bk, e * 129:(e + 1) * 129]
            mm(P, o, self.PTm[:, h, :], self.vt[:, vb, h, :], True, False, ["PTm", ("vt", vb)], [("ps", bk)])
            mm(P, o, self.qTz[:, pr, e, js], self.Cb[:, pr, :], False, True, ["qTw", "Cb"], [("ps", bk)])
        self.kv_update(j, t)
        N4 = self.ps[:, 6:8, 0:258].rearrange("p a (e c) -> p a e c", e=2)
        NK = [("ps", 6), ("ps", 7)]
        sm = self.small
        v22 = lambda c0: sm[:, c0:c0 + 4].rearrange("p (a e) -> p a e", a=2)
        den, rden, ssq, t1, t2, scl = v22(0), v22(4), v22(8), v22(12), v22(16), v22(20)
        embv = self.emb[:, t, :].rearrange("p (a e) -> p a e", a=2)
        act(P, den, N4[:, :, :, 128], AF.Abs, NK, ["den"])
        tt(P, "dve", den, den, embv, ALU.max, ["den", "emb"], ["den"])
        P.op("dve", lambda e_: e_.reciprocal(rden, den), ["den"], ["rden"])
        sq4 = self.sqN.rearrange("p (a e) d -> p a e d", a=2)
        act(P, sq4, N4[:, :, :, 0:128], AF.Square, NK, ["sqN"])
        P.op("dve", lambda e_: e_.tensor_reduce(ssq, sq4, AX.X, ALU.add), ["sqN"], ["ssq"])
        tt(P, "dve", t1, rden, rden, ALU.mult, ["rden"], ["t1"])
        tt(P, "dve", t2, ssq, t1, ALU.mult, ["ssq", "t1"], ["t2"])
        act(P, t2, t2, AF.Ln, ["t2"], ["t2"], bias=self.epsb[:, 0:1], scale=1.0 / 128.0)
        act(P, t2, t2, AF.Exp, ["t2"], ["t2"], scale=-0.5)
        tt(P, "dve", scl, rden, t2, ALU.mult, ["rden", "t2"], ["scl"])
        hn4 = self.hn.rearrange("p (a e) d -> p a e d", a=2)
        tt(P, "dve", hn4, N4[:, :, :, 0:128], scl.unsqueeze(3).to_broadcast([128, 2, 2, 128]), ALU.mult,
           NK + ["scl"], ["hn"])
        tp = self.ps[:, 5, :].bitcast(BF16)
        T5 = ("ps", 5)
        for h in range(4):
            tr(P, tp[:, 256 + h * 128:256 + (h + 1) * 128], self.hn[:, h, :], self.identb, ["hn", "identb"], [T5])
        for h in range(4):
            stt(P, self.sigo[:, h, js], tp[:, 256 + h * 128:256 + (h + 1) * 128], self.vec(V_GH + h),
                self.sigo[:, h, js], ALU.mult, ALU.mult, [T5, "sigo", "vecs"], ["sigo"])

    def o_group(self, a, b):
        P = self.P
        W = b - a
        for dt_ in range(8):
            bank = self.mmbank()
            for kt in range(KT):
                rhs = self.sigo[:, kt, 0:W] if kt < 4 else self.hpT[:, kt - 4, 0:W]
                mm(P, self.ps[:, bank, 0:W], self.wout[:, kt, dt_ * 128:(dt_ + 1) * 128], rhs, kt == 0, kt == KT - 1,
                   ["wout", "sigo", "hpT"], [("ps", bank)])
            tt(P, "dve", self.x[:, dt_, a:b], self.ps[:, bank, 0:W], self.x[:, dt_, a:b], ALU.add,
               [("ps", bank)] + xkeys(a, b), xkeys(a, b))

    def load_ffn_chunk(self, l, c, slot):
        P = self.P
        f0, f1 = FCH[c]
        T = f1 - f0
        su = self.d_w_up[l].rearrange("(kt p) n -> p kt n", p=128)
        dma(P, "pool", self.wu[slot][:, :, 0:T * 128], su[:, :, f0 * 128:f1 * 128], [], [("wu", slot)], ("wu", slot))
        dma(P, "pool", self.wu[slot][:, :, 512:512 + T * 128], su[:, :, DFF + f0 * 128:DFF + f1 * 128], [],
            [("wu", slot)], ("wu", slot))
        sd = self.d_w_down[l].rearrange("(i p) n -> p i n", p=128)
        dma(P, "pool", self.wd[slot][:, 0:T, :], sd[:, f0:f1, :], [], [("wd", slot)], ("wd", slot))

    def ffn_conv(self, bank, Wo, f, acc, kacc):
        P = self.P
        Wn = Wo + 2
        act(P, acc[:, 0:Wo], self.ps[:, bank, 2:Wn], AF.Identity, [("ps", bank), "vecs"], [kacc],
            bias=self.vec(V_FB + f), scale=self.vec(V_FC + f * 3 + 2))
        stt(P, acc[:, 0:Wo], self.ps[:, bank, 1:Wn - 1], self.vec(V_FC + f * 3 + 1), acc[:, 0:Wo], ALU.mult, ALU.add,
            [("ps", bank), kacc, "vecs"], [kacc])
        stt(P, acc[:, 0:Wo], self.ps[:, bank, 0:Wo], self.vec(V_FC + f * 3 + 0), acc[:, 0:Wo], ALU.mult, ALU.add,
            [("ps", bank), kacc, "vecs"], [kacc])

    def ffn_phase(self, l, tok0):
        P = self.P
        n = NT - tok0
        nwin = -(-n // 510)
        base = n // nwin
        wins = []
        s = tok0
        for i in range(nwin):
            Wo = base + (1 if i < n - base * nwin else 0)
            wins.append((s, Wo))
            s += Wo
        assert s == NT
        pair_i = 0
        dn_i = 0
        ab_i = 0
        for c, (f0, f1) in enumerate(FCH):
            slot = c % 2
            T = f1 - f0
            for (s, Wo) in wins:
                Wn = Wo + 2
                ab = ab_i % 2
                ab_i += 1
                for i in range(T):
                    pb = pair_i % 2
                    pair_i += 1
                    bg, bv = (0, 1) if pb == 0 else (2, 3)
                    for kt in range(KT):
                        mm(P, self.ps[:, bg, 0:Wn], self.wu[slot][:, kt, i * 128:(i + 1) * 128],
                           self.hT2[:, kt, s:s + Wn], kt == 0, kt == KT - 1, [("wu", slot), "hT2"], [("ps", bg)])
                    for kt in range(KT):
                        mm(P, self.ps[:, bv, 0:Wn], self.wu[slot][:, kt, 512 + i * 128:512 + (i + 1) * 128],
                           self.hT2[:, kt, s:s + Wn], kt == 0, kt == KT - 1, [("wu", slot), "hT2"], [("ps", bv)])
                    self.ffn_conv(bg, Wo, f0 + i, self.accg[pb], ("accg", pb))
                    self.ffn_conv(bv, Wo, FT + f0 + i, self.accv[pb], ("accv", pb))
                    act(P, self.sg[pb][:, 0:Wo], self.accg[pb][:, 0:Wo], AF.Silu, [("accg", pb)], [("sg", pb)])
                    tt(P, "dve", self.actb[ab][:, i, 0:Wo], self.sg[pb][:, 0:Wo], self.accv[pb][:, 0:Wo], ALU.mult,
                       [("sg", pb), ("accv", pb)], [("actb", ab)])
                for dt_ in range(8):
                    bank = 4 + dn_i % 4
                    dn_i += 1
                    for i in range(T):
                        mm(P, self.ps[:, bank, 0:Wo], self.wd[slot][:, i, dt_ * 128:(dt_ + 1) * 128],
                           self.actb[ab][:, i, 0:Wo], i == 0, i == T - 1, [("wd", slot), ("actb", ab)], [("ps", bank)])
                    tt(P, "dve", self.x[:, dt_, s:s + Wo], self.ps[:, bank, 0:Wo], self.x[:, dt_, s:s + Wo], ALU.add,
                       [("ps", bank)] + xkeys(s, s + Wo), xkeys(s, s + Wo))
            if c + 2 < len(FCH):
                self.load_ffn_chunk(l, c + 2, slot)

    def build_main(self):
        nc, P = self.nc, self.P
        l = self.layer
        self.declare()
        import contextlib

        with contextlib.ExitStack() as st:
            sb = lambda name, shape, dt_: st.enter_context(nc.sbuf_tensor("sb_" + name, shape, dt_))[:]
            self.x = sb("x", [128, KT, NT], F32)
            self.cf = sb("cf", [128, 512], F32)
            self.percore = sb("percore", [128, 80], F32)
            self.vecs = sb("vecs", [128, 2, VW], F32)
            self.identb = sb("identb", [128, 128], BF16)
            self.onesb = sb("onesb", [128, 128], BF16)
            self.epsb = sb("epsb", [128, 1], F32)
            self.wA = sb("wA", [128, KT, 1024], BF16)
            self.wu = [sb("wu0", [128, KT, 1024], BF16), None]
            self.wd = [sb("wd0", [128, 4, 1024], BF16), None]
            self.qkcarry = sb("qkcarry", [128, 4, 3], F32)
            self.ucarry = sb("ucarry", [128, 4, 16], F32)
            self.gsb = sb("gsb", [128, 4, 8], F32)
            self.lfn = sb("lfn", [128, 4, 4], F32)
            self.cS = sb("cS", [128, NTL, 4], F32)
            self.emb = sb("emb", [128, NTL, 4], F32)
            self.Gp = sb("Gp", [128, NTL, 2], F32)
            self.Cst = sb("Cst", [128, 2, 129], F32)
            self.Cb = sb("Cb", [128, 2, 129], BF16)
            self.small = sb("small", [128, 32], F32)
            self.tmp16 = sb("tmp16", [128, 16], F32)
            self.u_bytes = (nc.sbuf_bytes_remaining - 256) // 64 * 64
            self.U = sb("U", [128, self.u_bytes // 2], BF16)
            self.ps = st.enter_context(nc.psum_tensor("ps", [128, 8, 512], F32))[:]
            self.mm_banks, self.n_mm, self.mm_i = [0, 1, 2], 3, 0
            self.sq_i = self.pre_i = self.dT_i = 0

            self.u_off = 0
            cv = self.carve
            self.wB = cv(BF16, [128, KT, 1032])
            self.wout = cv(BF16, [128, KT, 1024])
            self.wpool = cv(BF16, [128, 4, 128])
            self.sq = [cv(BF16, [128, KT, 128]) for _ in range(2)]
            self.rstd = cv(F32, [128, 512])
            self.hT = cv(BF16, [128, KT, 512])
            self.pre = [cv(F32, [128, 528]) for _ in range(2)]
            self.acc = [cv(F32, [128, 528]) for _ in range(2)]
            self.qTz = cv(BF16, [128, 2, 2, 512])
            self.kTw = cv(BF16, [128, 2, 512])
            self.ktok = cv(BF16, [128, 4, 256])
            self.vt = cv(BF16, [128, 4, 4, 129])
            self.sigo = cv(BF16, [128, 4, 512])
            self.hpT = cv(BF16, [128, 4, 512])
            self.dT = [cv(BF16, [128, 512]) for _ in range(2)]
            self.PTm = cv(BF16, [128, 4, 128])
            self.hn = cv(BF16, [128, 4, 128])
            self.sqN = cv(F32, [128, 4, 128])
            self.xtok = [self.sigo.rearrange("p a b -> p (a b)").bitcast(F32),
                         self.hpT.rearrange("p a b -> p (a b)").bitcast(F32)]
            self.xtok_keys = ["sigo", "hpT"]

            self.load_common()
            self.load_w_in(l)
            so = self.d_w_out[l].rearrange("(kt p) n -> p kt n", p=128)
            dma(P, "pool", self.wout, so, [], ["wout"], "wout")
            dma(P, "pool", self.wpool, self.d_w_pool[l].rearrange("g c d -> c g d"), [], ["wpool"], "wpool")
            self.load_ffn_chunk(l, 0, 0)
            self.load_x()
            dma(P, "sp", self.Cst.rearrange("p a b -> p (a b)"), self.d_state_in, [], ["Cst"], "cst")
            act(P, self.Cb, self.Cst, AF.Copy, ["Cst"], ["Cb"], scale=0.125)
            P.op("dve", lambda e: e.memset(self.qkcarry, 0.0), [], [("qkc", c) for c in range(4)])
            P.op("dve", lambda e: e.memset(self.ucarry, 0.0), [], [("ucar", c) for c in range(4)])
            P.op("pool", lambda e: e.memset(self.qTz[64:128, :, 0, :], 0.0), [], ["qTw"])
            P.op("pool", lambda e: e.memset(self.qTz[0:64, :, 1, :], 0.0), [], ["qTw"])

            for g, (t0, t1) in enumerate(GROUPS):
                a, b = t0 * 128, t1 * 128
                W = b - a
                ntg = t1 - t0
                self.rmsnorm(a, b, V_GMIX, self.hT, ["hT"])
                for ct in range(4):
                    dst = None if ct < 2 else self.kTw[:, ct - 2, 0:W]
                    self.qk_tile(g, ct, W, dst, "qTw" if ct < 2 else "kTw")
                self.gates_group(g, t0, ntg)
                for j in range(ntg):
                    if t0 + j >= 1:
                        self.v_tile(j, t0 + j)
                        self.k_transpose(j)
                for i in range(4):
                    self.o_tile(i, W)
                for gi in range(4):
                    self.u_tile(g, gi, W)
                for j in range(ntg):
                    if t0 + j >= 1:
                        self.s2_tile(j, t0 + j)
                self.o_group(a, b)

            P.barrier()
            self.u_off = 0
            self.wu[1] = cv(BF16, [128, KT, 1024])
            self.wd[1] = cv(BF16, [128, 4, 1024])
            self.hT2 = cv(BF16, [128, KT, NT + 2])
            self.sq = [cv(BF16, [128, KT, 128]) for _ in range(2)]
            self.rstd = cv(F32, [128, 512])
            self.accg = [cv(F32, [128, 512]) for _ in range(2)]
            self.accv = [cv(F32, [128, 512]) for _ in range(2)]
            self.sg = [cv(F32, [128, 512]) for _ in range(2)]
            self.actb = [cv(BF16, [128, 4, 512]) for _ in range(2)]
            self.load_ffn_chunk(l, 1, 1)
            P.op("dve", lambda e: e.memset(self.hT2[:, :, 0:2], 0.0), [], ["hT2"])
            for g, (t0, t1) in enumerate(GROUPS):
                a, b = t0 * 128, t1 * 128
                self.rmsnorm(a, b, V_GF, self.hT2[:, :, 2 + a:2 + b], ["hT2"])
            self.ffn_phase(l, 128)

            P.barrier()
            self.u_off = 0
            self.sq = [cv(BF16, [128, KT, 128]) for _ in range(2)]
            self.rstd = cv(F32, [128, 512])
            yT = [cv(F32, [128, KT, 128]) for _ in range(2)]
            ytok = [cv(F32, [128, D]) for _ in range(2)]
            for t in range(2, NTL):
                b2 = t % 2
                a, b = t * 128, (t + 1) * 128
                if self.final:
                    self.rmsnorm(a, b, V_FIN, yT[b2], [("yT", b2)])
                banks = (4, 5) if b2 == 0 else (6, 7)
                for kt in range(KT):
                    bk = banks[kt // 4]
                    if self.final:
                        src, sk = yT[b2][:, kt, :], [("yT", b2)]
                    else:
                        src, sk = self.x[:, kt, a:b], [("x", t)]
                    tr(P, self.ps[:, bk, (kt % 4) * 128:(kt % 4 + 1) * 128], src, self.cf[:, 0:128], sk + ["cf"],
                       [("ps", bk)])
                cp(P, "act", ytok[b2][:, 0:512], self.ps[:, banks[0], :], [("ps", banks[0])], [("ytok", b2)])
                cp(P, "dve", ytok[b2][:, 512:1024], self.ps[:, banks[1], :], [("ps", banks[1])], [("ytok", b2)])
                o = dma(P, "sp", self.d_out[(t - 2) * 128:(t - 1) * 128, :], ytok[b2], [("ytok", b2)], [("yo", t)],
                        ("yout", b2))
                P.final.append(o)
            P.emit()
        return nc


    def declare_fused(self):
        nc = self.nc
        dt = nc.dram_tensor
        self.d_x = dt("x_loc", [NT, D], F32, kind="ExternalInput").ap()
        self.d_consts = dt("consts", [128, 512], F32, kind="ExternalInput").ap()
        self.d_percore = dt("percore", [128, 80], F32, kind="ExternalInput").ap()
        self.d_vecs = dt("vecs", [2, 128, VW], F32, kind="ExternalInput").ap()
        self.d_w_in = dt("w_in", [2, D, INW], F32, kind="ExternalInput").ap()
        self.d_w_pool = dt("w_pool", [2, 4, 128, 128], F32, kind="ExternalInput").ap()
        self.d_w_out = dt("w_out", [2, D, D], F32, kind="ExternalInput").ap()
        self.d_w_up = dt("w_up", [2, D, 2 * DFF], F32, kind="ExternalInput").ap()
        self.d_w_down = dt("w_down", [2, DFF, D], F32, kind="ExternalInput").ap()
        self.d_out = dt("y", [2048, D], F32, kind="ExternalOutput").ap()
        self.d_st_loc = dt("cc_loc", [128, SW], F32).ap()
        self.d_st_all = dt("cc_all", [8 * 128, SW], F32).ap()
        self.d_h_loc = dt("cch_loc", [128, 16], F32).ap()
        self.d_h_all = dt("cch_all", [8 * 128, 16], F32).ap()

    def w_in_src(self, l):
        return self.d_w_in[l].rearrange("(kt p) n -> p kt n", p=128)

    def phase_pass1(self, inj, pub):
        P = self.P
        self.make_cb = False
        P.op("dve", lambda e: e.memset(self.qkcarry, 0.0), [], [("qkc", c) for c in range(4)])
        P.op("dve", lambda e: e.memset(self.Cst, 0.0), [], ["Cst"])
        for g, (t0, t1) in enumerate(GROUPS):
            if t0 >= pub:
                break
            a, b = t0 * 128, t1 * 128
            W = b - a
            ntg = t1 - t0
            self.rmsnorm(a, b, V_GMIX, self.hT, ["hT"])
            for ct in (2, 3):
                self.qk_tile(g, ct, W, self.kTw[:, ct - 2, 0:W], "kTw")
            self.gates_group(g, t0, ntg)
            for j in range(ntg):
                t = t0 + j
                if t < inj or t >= pub:
                    continue
                self.v_tile(j, t)
                self.k_transpose(j)
                self.kv_update(j, t)

    def phase_exchange(self):
        P = self.P
        cflat = self.Cst.rearrange("p a b -> p (a b)")
        dma(P, "sp", self.d_st_loc, cflat, ["Cst"], ["stloc"], "stloc")
        P.op("pool", lambda e: e.collective_compute("AllGather", ALU.bypass, replica_groups=[list(range(8))],
                                                    ins=[self.d_st_loc], outs=[self.d_st_all]),
             ["stloc"], ["stalld"], dma=True, semkey="cc", inc=1, cost=1.0, lat=30.0)
        dma(P, "sp", self.stall, self.d_st_all.rearrange("(r p) n -> p r n", p=128), ["stalld"], ["stall"], "stall")
        ts(P, "dve", cflat, self.stall[:, 0, :], self.percore[:, 65:66], None, ALU.mult, None,
           ["stall", "percore"], ["Cst"])
        for r in range(1, 8):
            stt(P, cflat, self.stall[:, r, :], self.percore[:, 65 + r:66 + r], cflat, ALU.mult, ALU.add,
                ["stall", "percore", "Cst"], ["Cst"])
        act(P, self.Cb, self.Cst, AF.Copy, ["Cst"], ["Cb"], scale=0.125)

    def phase_halo_exchange(self):
        P = self.P
        v3 = self.xh_s.rearrange("p (a b) -> p a b", a=8)
        cp(P, "dve", v3, self.x[:, :, NT - 2:NT], [("x", NTL - 1)], ["xh_s"])
        dma(P, "sp", self.d_h_loc, self.xh_s, ["xh_s"], ["hloc"], "hloc")
        P.op("pool", lambda e: e.collective_compute("AllGather", ALU.bypass, replica_groups=[list(range(8))],
                                                    ins=[self.d_h_loc], outs=[self.d_h_all]),
             ["hloc"], ["halld"], dma=True, semkey="cc2", inc=1, cost=1.0, lat=15.0)
        dma(P, "sp", self.hall, self.d_h_all.rearrange("(r p) n -> p r n", p=128), ["halld"], ["hall"], "hall")
        ts(P, "dve", self.xh_s, self.hall[:, 0, :], self.percore[:, 65:66], None, ALU.mult, None,
           ["hall", "percore"], ["xh_s"])
        for r in range(1, 8):
            stt(P, self.xh_s, self.hall[:, r, :], self.percore[:, 65 + r:66 + r], self.xh_s, ALU.mult, ALU.add,
                ["hall", "percore", "xh_s"], ["xh_s"])
        cp(P, "dve", self.x[:, :, 254:256], v3, ["xh_s"], [("x", 1)])

    def phase_mixer(self, inj, first_out_group):
        P = self.P
        self.make_cb = True
        P.op("dve", lambda e: e.memset(self.qkcarry, 0.0), [], [("qkc", c) for c in range(4)])
        P.op("dve", lambda e: e.memset(self.ucarry, 0.0), [], [("ucar", c) for c in range(4)])
        P.op("pool", lambda e: e.memset(self.qTz[64:128, :, 0, :], 0.0), [], ["qTw"])
        P.op("pool", lambda e: e.memset(self.qTz[0:64, :, 1, :], 0.0), [], ["qTw"])
        for g, (t0, t1) in enumerate(GROUPS):
            a, b = t0 * 128, t1 * 128
            W = b - a
            ntg = t1 - t0
            self.rmsnorm(a, b, V_GMIX, self.hT, ["hT"])
            for ct in range(4):
                dst = None if ct < 2 else self.kTw[:, ct - 2, 0:W]
                self.qk_tile(g, ct, W, dst, "qTw" if ct < 2 else "kTw")
            self.gates_group(g, t0, ntg)
            for j in range(ntg):
                if t0 + j >= inj:
                    self.v_tile(j, t0 + j)
                    self.k_transpose(j)
            for i in range(4):
                self.o_tile(i, W)
            for gi in range(4):
                self.u_tile(g, gi, W)
            for j in range(ntg):
                if t0 + j >= inj:
                    self.s2_tile(j, t0 + j)
            if g >= first_out_group:
                self.o_group(a, b)

    def build_fused(self):
        nc, P = self.nc, self.P
        self.declare_fused()
        import contextlib

        with contextlib.ExitStack() as st:
            sb = lambda name, shape, dt_: st.enter_context(nc.sbuf_tensor("sb_" + name, shape, dt_))[:]
            self.x = sb("x", [128, KT, NT], F32)
            self.cf = sb("cf", [128, 512], F32)
            self.percore = sb("percore", [128, 80], F32)
            self.vecs = sb("vecs", [128, 2, VW], F32)
            self.identb = sb("identb", [128, 128], BF16)
            self.onesb = sb("onesb", [128, 128], BF16)
            self.epsb = sb("epsb", [128, 1], F32)
            self.wA = sb("wA", [128, KT, 1024], BF16)
            self.wu = [sb("wu0", [128, KT, 1024], BF16), None]
            self.wd = [sb("wd0", [128, 4, 1024], BF16), None]
            self.qkcarry = sb("qkcarry", [128, 4, 3], F32)
            self.ucarry = sb("ucarry", [128, 4, 16], F32)
            self.gsb = sb("gsb", [128, 4, 8], F32)
            self.lfn = sb("lfn", [128, 4, 4], F32)
            self.cS = sb("cS", [128, NTL, 4], F32)
            self.emb = sb("emb", [128, NTL, 4], F32)
            self.Gp = sb("Gp", [128, NTL, 2], F32)
            self.Cst = sb("Cst", [128, 2, 129], F32)
            self.Cb = sb("Cb", [128, 2, 129], BF16)
            self.small = sb("small", [128, 32], F32)
            self.tmp16 = sb("tmp16", [128, 16], F32)
            self.xh_s = sb("xh_s", [128, 16], F32)
            self.hall = sb("hall", [128, 8, 16], F32)
            self.u_bytes = (nc.sbuf_bytes_remaining - 256) // 64 * 64
            self.U = sb("U", [128, self.u_bytes // 2], BF16)
            self.ps = st.enter_context(nc.psum_tensor("ps", [128, 8, 512], F32))[:]
            self.mm_banks, self.n_mm, self.mm_i = [0, 1, 2], 3, 0
            self.sq_i = self.pre_i = self.dT_i = 0
            cv = self.carve

            self.u_off = 0
            self.xtok = [cv(F32, [128, D]) for _ in range(2)]
            self.load_common()
            dma(P, "pool", self.wA, self.w_in_src(0)[:, :, 0:1024], [], ["wA"], "wA")
            self.layer = 0
            self.load_ffn_chunk(0, 0, 0)
            self.load_x()

            for l in (0, 1):
                self.layer = l
                inj = 1 if l == 0 else 2
                pub = 17 if l == 0 else 18
                P.barrier()
                self.u_off = 0
                self.wB = cv(BF16, [128, KT, 1032])
                self.sq = [cv(BF16, [128, KT, 128]) for _ in range(2)]
                self.rstd = cv(F32, [128, 512])
                self.hT = cv(BF16, [128, KT, 512])
                self.pre = [cv(F32, [128, 528]) for _ in range(2)]
                self.acc = [cv(F32, [128, 528]) for _ in range(2)]
                self.kTw = cv(BF16, [128, 2, 512])
                self.ktok = cv(BF16, [128, 4, 256])
                self.vt = cv(BF16, [128, 4, 4, 129])
                self.stall = cv(F32, [128, 8, SW])
                dma(P, "pool", self.wB[:, :, 512:520], self.w_in_src(l)[:, :, 1536:1544], [], ["wB"], "wB")
                if l == 1:
                    ts(P, "dve", self.x[:, :, 0:256], self.x[:, :, 0:256], self.percore[:, 64:65], None, ALU.mult,
                       None, xkeys(0, 256) + ["percore"], xkeys(0, 256))
                self.phase_pass1(inj, pub)
                self.phase_exchange()
                P.barrier()
                self.u_off = 0
                self.wB = cv(BF16, [128, KT, 1032])
                self.wout = cv(BF16, [128, KT, 1024])
                self.wpool = cv(BF16, [128, 4, 128])
                self.sq = [cv(BF16, [128, KT, 128]) for _ in range(2)]
                self.rstd = cv(F32, [128, 512])
                self.hT = cv(BF16, [128, KT, 512])
                self.pre = [cv(F32, [128, 528]) for _ in range(2)]
                self.acc = [cv(F32, [128, 528]) for _ in range(2)]
                self.qTz = cv(BF16, [128, 2, 2, 512])
                self.kTw = cv(BF16, [128, 2, 512])
                self.ktok = cv(BF16, [128, 4, 256])
                self.vt = cv(BF16, [128, 4, 4, 129])
                self.sigo = cv(BF16, [128, 4, 512])
                self.hpT = cv(BF16, [128, 4, 512])
                self.dT = [cv(BF16, [128, 512]) for _ in range(2)]
                self.PTm = cv(BF16, [128, 4, 128])
                self.hn = cv(BF16, [128, 4, 128])
                self.sqN = cv(F32, [128, 4, 128])
                dma(P, "pool", self.wB, self.w_in_src(l)[:, :, 1024:INW], [], ["wB"], "wB")
                dma(P, "pool", self.wout, self.d_w_out[l].rearrange("(kt p) n -> p kt n", p=128), [], ["wout"], "wout")
                dma(P, "pool", self.wpool, self.d_w_pool[l].rearrange("g c d -> c g d"), [], ["wpool"], "wpool")
                self.phase_mixer(inj, 0 if l == 0 else 1)
                if l == 1:
                    self.phase_halo_exchange()
                P.barrier()
                self.u_off = 0
                self.wu[1] = cv(BF16, [128, KT, 1024])
                self.wd[1] = cv(BF16, [128, 4, 1024])
                self.hT2 = cv(BF16, [128, KT, NT + 2])
                self.sq = [cv(BF16, [128, KT, 128]) for _ in range(2)]
                self.rstd = cv(F32, [128, 512])
                self.accg = [cv(F32, [128, 512]) for _ in range(2)]
                self.accv = [cv(F32, [128, 512]) for _ in range(2)]
                self.sg = [cv(F32, [128, 512]) for _ in range(2)]
                self.actb = [cv(BF16, [128, 4, 512]) for _ in range(2)]
                self.load_ffn_chunk(l, 1, 1)
                if l == 0:
                    dma(P, "pool", self.wA, self.w_in_src(1)[:, :, 0:1024], [], ["wA"], "wA")
                P.op("dve", lambda e: e.memset(self.hT2[:, :, 0:2], 0.0), [], ["hT2"])
                for g, (t0, t1) in enumerate(GROUPS):
                    a, b = t0 * 128, t1 * 128
                    self.rmsnorm(a, b, V_GF, self.hT2[:, :, 2 + a:2 + b], ["hT2"])
                self.ffn_phase(l, 128 * inj)
                if l == 0:
                    self.load_ffn_chunk(1, 0, 0)

            self.layer = 1
            P.barrier()
            self.u_off = 0
            self.sq = [cv(BF16, [128, KT, 128]) for _ in range(2)]
            self.rstd = cv(F32, [128, 512])
            yT = [cv(F32, [128, KT, 128]) for _ in range(2)]
            ytok = [cv(F32, [128, D]) for _ in range(2)]
            for t in range(2, NTL):
                b2 = t % 2
                a, b = t * 128, (t + 1) * 128
                self.rmsnorm(a, b, V_FIN, yT[b2], [("yT", b2)])
                banks = (4, 5) if b2 == 0 else (6, 7)
                for kt in range(KT):
                    bk = banks[kt // 4]
                    tr(P, self.ps[:, bk, (kt % 4) * 128:(kt % 4 + 1) * 128], yT[b2][:, kt, :], self.cf[:, 0:128],
                       [("yT", b2), "cf"], [("ps", bk)])
                cp(P, "act", ytok[b2][:, 0:512], self.ps[:, banks[0], :], [("ps", banks[0])], [("ytok", b2)])
                cp(P, "dve", ytok[b2][:, 512:1024], self.ps[:, banks[1], :], [("ps", banks[1])], [("ytok", b2)])
                o = dma(P, "sp", self.d_out[(t - 2) * 128:(t - 1) * 128, :], ytok[b2], [("ytok", b2)], [("yo", t)],
                        ("yout", b2))
                P.final.append(o)
            P.emit()
        return nc

    def build_state(self):
        nc, P = self.nc, self.P
        l = self.layer
        self.declare()
        import contextlib

        with contextlib.ExitStack() as st:
            sb = lambda name, shape, dt_: st.enter_context(nc.sbuf_tensor("sb_" + name, shape, dt_))[:]
            self.x = sb("x", [128, KT, NT], F32)
            self.cf = sb("cf", [128, 512], F32)
            self.percore = sb("percore", [128, 80], F32)
            self.vecs = sb("vecs", [128, 2, VW], F32)
            self.identb = sb("identb", [128, 128], BF16)
            self.onesb = sb("onesb", [128, 128], BF16)
            self.epsb = sb("epsb", [128, 1], F32)
            self.wA = sb("wA", [128, KT, 1024], BF16)
            self.wB = sb("wB", [128, KT, 1032], BF16)
            self.xtok = [sb("xtok%d" % i, [128, D], F32) for i in range(2)]
            self.sq = [sb("sq%d" % i, [128, KT, 128], BF16) for i in range(2)]
            self.rstd = sb("rstd", [128, 512], F32)
            self.hT = sb("hT", [128, KT, 512], BF16)
            self.pre = [sb("pre%d" % i, [128, 515], F32) for i in range(2)]
            self.acc = [sb("acc%d" % i, [128, 512], F32) for i in range(2)]
            self.qkcarry = sb("qkcarry", [128, 4, 3], F32)
            self.kTw = sb("kTw", [128, 2, 512], BF16)
            self.ktok = sb("ktok", [128, 4, 256], BF16)
            self.vt = sb("vt", [128, 4, 4, 129], BF16)
            self.gsb = sb("gsb", [128, 4, 8], F32)
            self.lfn = sb("lfn", [128, 4, 4], F32)
            self.cS = sb("cS", [128, NTL, 4], F32)
            self.emb = sb("emb", [128, NTL, 4], F32)
            self.Gp = sb("Gp", [128, NTL, 2], F32)
            self.Cst = sb("Cst", [128, 2, 129], F32)
            self.ps = st.enter_context(nc.psum_tensor("ps", [128, 8, 512], F32))[:]
            self.mm_banks, self.n_mm, self.mm_i = [0, 1, 2], 3, 0
            self.sq_i = 0
            self.pre_i = 0

            self.load_common()
            self.load_w_in(l, only_k_v_g=True)
            self.load_x()
            P.op("dve", lambda e: e.memset(self.qkcarry, 0.0), [], [("qkc", c) for c in range(4)])
            P.op("dve", lambda e: e.memset(self.Cst, 0.0), [], ["Cst"])
            for g, (t0, t1) in enumerate(GROUPS):
                if t0 >= 17:
                    break
                a, b = t0 * 128, t1 * 128
                W = b - a
                ntg = t1 - t0
                self.rmsnorm(a, b, V_GMIX, self.hT, ["hT"])
                for ct in (2, 3):
                    self.qk_tile(g, ct, W, self.kTw[:, ct - 2, 0:W], "kTw")
                self.gates_group(g, t0, ntg)
                for j in range(ntg):
                    t = t0 + j
                    if t < 1 or t >= 17:
                        continue
                    self.v_tile(j, t)
                    self.k_transpose(j)
                    self.kv_update(j, t)
            o = dma(P, "sp", self.d_state_out, self.Cst.rearrange("p a b -> p (a b)"), ["Cst"], ["dout"], "dout")
            P.final.append(o)
            P.emit()
        return nc


def build_main_prog(layer, final):
    b = Builder("main", layer, final)
    return b.build_main()


def build_state_prog(layer):
    b = Builder("state", layer, False)
    return b.build_state()


def make_consts():
    c = np.zeros((128, 512), np.float32)
    c[:, 0:128] = np.eye(128, dtype=np.float32)
    tri = (np.arange(128)[:, None] <= np.arange(128)[None, :]).astype(np.float32)
    c[:, 128:256] = tri
    c[:, 256:384] = 1.0
    c[:, 384:512] = tri * 0.125
    return c


def make_percore():
    pcs = []
    for c in range(8):
        pc = np.zeros((128, 80), np.float32)
        first = (c % 2 == 0)
        for g, w in enumerate((2, 4, 8, 16)):
            for i in range(16):
                pc[:, g * 16 + i] = 1.0 / (min(i + 1, w) if first else w)
        pc[:, 64] = 0.0 if first else 1.0
        if not first:
            pc[:, 65 + c - 1] = 1.0
        pcs.append(pc)
    return pcs


def make_vecs(inp):
    v = np.zeros((2, 128, VW), np.float32)
    for l in range(2):
        v[l, :, V_GMIX:V_GMIX + 8] = inp["mix_norm"][l].reshape(8, 128).T
        v[l, :, V_QKC:V_QKC + 16] = inp["w_qk_conv"][l].reshape(4, 4, 128).transpose(2, 1, 0).reshape(128, 16)
        v[l, :, V_BG:V_BG + 8] = inp["b_gates"][l][None, :]
        v[l, :, V_GH:V_GH + 4] = inp["head_norm"][l].reshape(4, 128).T
        v[l, :, V_PS:V_PS + 4] = inp["pool_scale"][l].reshape(4, 128).T
        v[l, :, V_GF:V_GF + 8] = inp["ffn_norm"][l].reshape(8, 128).T
        v[l, :, V_FC:V_FC + 132] = inp["w_ffn_conv"][l].reshape(3, 44, 128).transpose(2, 1, 0).reshape(128, 132)
        v[l, :, V_FB:V_FB + 44] = inp["b_ffn_conv"][l].reshape(44, 128).T
        v[l, :, V_FIN:V_FIN + 8] = inp["final_norm"].reshape(8, 128).T
    return v


def make_xloc(xfull):
    out = []
    for c in range(8):
        b, half = c // 2, c % 2
        xl = np.zeros((NT, D), np.float32)
        s = half * 2048
        xl[256:] = xfull[b, s:s + 2048]
        if half == 1:
            xl[:256] = xfull[b, s - 256:s]
        out.append(xl)
    return out


def host_prep(inp):
    return dict(consts=make_consts(), percore=make_percore(), vecs=make_vecs(inp), x_loc=make_xloc(inp["x"]))


def _run_layer(l, final, host, xlocs, inp):
    common = dict(consts=host["consts"], vecs=host["vecs"], w_in=inp["w_in"])
    nc_s = build_state_prog(l)
    maps = [dict(common, x_loc=xlocs[c], percore=host["percore"][c]) for c in range(8)]
    res = run_bass_kernel_spmd(nc_s, maps, core_ids=list(range(8)))
    states = [np.asarray(res.results[c]["state_out"], np.float32) for c in range(8)]
    zero = np.zeros((128, SW), np.float32)
    nc_m = build_main_prog(l, final)
    maps = [dict(common, x_loc=xlocs[c], percore=host["percore"][c], w_pool=inp["w_pool"], w_out=inp["w_out"],
                 w_up=inp["w_up"], w_down=inp["w_down"], state_in=(states[c - 1] if c % 2 == 1 else zero))
            for c in range(8)]
    res = run_bass_kernel_spmd(nc_m, maps, core_ids=list(range(8)))
    y = np.zeros((4, 4096, D), np.float32)
    for c in range(8):
        y[c // 2, (c % 2) * 2048:(c % 2 + 1) * 2048] = res.results[c]["y"]
    return y


def build_fused_prog():
    b = Builder("fused", 0, True)
    return b.build_fused()


def kernel(**inputs):
    inp = {k: np.ascontiguousarray(np.asarray(v, dtype=np.float32)) for k, v in inputs.items()}
    host = host_prep(inp)
    nc = build_fused_prog()
    maps = [dict(consts=host["consts"], vecs=host["vecs"], x_loc=host["x_loc"][c], percore=host["percore"][c],
                 w_in=inp["w_in"], w_pool=inp["w_pool"], w_out=inp["w_out"], w_up=inp["w_up"], w_down=inp["w_down"])
            for c in range(8)]
    res = run_bass_kernel_spmd(nc, maps, core_ids=list(range(8)))
    y = np.zeros((4, 4096, D), np.float32)
    for c in range(8):
        y[c // 2, (c % 2) * 2048:(c % 2 + 1) * 2048] = res.results[c]["y"]
    return y
```

```python
import numpy as np
import concourse.bass as bass
import concourse.mybir as mybir
from concourse.bass_utils import run_bass_kernel_spmd

F32 = mybir.dt.float32
BF16 = mybir.dt.bfloat16
AF = mybir.ActivationFunctionType
ALU = mybir.AluOpType
AX = mybir.AxisListType

D = 1024
KT = 8
NTL = 18
NT = NTL * 128
INW = 2056
DFF = 2816
FT = 22
EPS = 1e-6
GROUPS = [(0, 2), (2, 6), (6, 10), (10, 14), (14, 18)]
FCH = [(0, 4), (4, 8), (8, 12), (12, 16), (16, 19), (19, 22)]
VW = 232
V_GMIX, V_QKC, V_BG, V_GH, V_PS, V_GF, V_FC, V_FB, V_FIN = 0, 8, 24, 32, 36, 40, 48, 180, 224
SW = 258


class Prog:
    ENGS = ("pe", "act", "dve", "pool", "sp")

    def __init__(self, nc):
        self.nc = nc
        self.ops = []
        self.last_w = {}
        self.readers = {}
        self.final = []
        self.phase = 0
        self.inorder_engs = ("act", "pool", "sp")

    def barrier(self):
        self.phase += 1
        self.last_w = {}
        self.readers = {}

    def op(self, eng, fn, r=(), w=(), dma=False, semkey=None, inc=16, cost=0.2, lat=None, tab=None):
        i = len(self.ops)
        deps = set()
        for k in r:
            if k in self.last_w:
                deps.add(self.last_w[k])
        for k in w:
            if k in self.last_w:
                deps.add(self.last_w[k])
            for rd in self.readers.get(k, ()):
                deps.add(rd)
        deps.discard(i)
        self.ops.append(dict(eng=eng, fn=fn, deps=deps, dma=dma, semkey=semkey, val=None, inc=inc, phase=self.phase,
                             cost=cost, lat=(cost if lat is None else lat), tab=tab, wkey=tuple(w)))
        for k in r:
            self.readers.setdefault(k, []).append(i)
        for k in w:
            self.last_w[k] = i
            self.readers[k] = []
        return i

    def schedule(self):
        import heapq
        ops = self.ops
        order = {e: [] for e in self.ENGS}
        nph = self.phase + 1
        byphase = [[] for _ in range(nph)]
        for i, o in enumerate(ops):
            byphase[o["phase"]].append(i)
        tnow = 0.0
        for ph in range(nph):
            ids = byphase[ph]
            if not ids:
                continue
            unit_of = {}
            units = []
            last_pe_unit = None
            for i in ids:
                o = ops[i]
                if o["eng"] == "pe" and not o["dma"]:
                    wk = o.get("wkey")
                    if (last_pe_unit is not None and units[last_pe_unit][1] == wk
                            and len(units[last_pe_unit][0]) < 40
                            and all(d < units[last_pe_unit][0][0] or unit_of.get(d) == last_pe_unit
                                    for d in o["deps"])):
                        units[last_pe_unit][0].append(i)
                        unit_of[i] = last_pe_unit
                        continue
                    units.append(([i], wk))
                    last_pe_unit = len(units) - 1
                    unit_of[i] = last_pe_unit
                else:
                    units.append(([i], None))
                    unit_of[i] = len(units) - 1
            nu = len(units)
            ueng = [ops[units[u][0][0]]["eng"] for u in range(nu)]
            udeps = [set() for _ in range(nu)]
            for u in range(nu):
                for m in units[u][0]:
                    for d in ops[m]["deps"]:
                        if d in unit_of and unit_of[d] != u:
                            udeps[u].add(unit_of[d])
            succ = [[] for _ in range(nu)]
            indeg = [0] * nu
            for u in range(nu):
                indeg[u] = len(udeps[u])
                for d in udeps[u]:
                    succ[d].append(u)
            ready = {e: [] for e in self.ENGS}
            for u in range(nu):
                if indeg[u] == 0:
                    heapq.heappush(ready[ueng[u]], u)
            aseg = {}
            seg_left = {}
            cur, segid = None, 0
            for u in range(nu):
                if ueng[u] != "act":
                    continue
                tb = ops[units[u][0][0]]["tab"]
                if tb is not None and tb != cur:
                    if cur is not None:
                        segid += 1
                    cur = tb
                aseg[u] = segid
                seg_left[segid] = seg_left.get(segid, 0) + 1
            act_seg = 0
            inorder = getattr(self, "inorder_engs", ())
            nxt = {e: [u for u in range(nu) if ueng[u] == e] for e in inorder}
            nptr = {e: 0 for e in inorder}
            efree = {e: tnow for e in self.ENGS}
            ufin = [0.0] * nu
            rdy_t = [tnow] * nu
            cur_tab = None
            left = nu
            while left:
                best = None
                for e in self.ENGS:
                    if not ready[e]:
                        continue
                    if e in inorder:
                        want = nxt[e][nptr[e]]
                        cands = [want] if want in ready[e] else []
                    elif e == "act":
                        while act_seg in seg_left and seg_left[act_seg] == 0:
                            act_seg += 1
                        cands = [u for u in heapq.nsmallest(24, ready[e]) if aseg[u] == act_seg][:6]
                    else:
                        cands = heapq.nsmallest(6, ready[e])
                    for u in cands:
                        st_ = max(efree[e], rdy_t[u])
                        o0 = ops[units[u][0][0]]
                        if e == "act" and o0["tab"] is not None and o0["tab"] != cur_tab:
                            st_ += 1.3
                        key = (st_, u)
                        if best is None or key < best[0]:
                            best = (key, u, e, st_)
                _, u, e, st_ = best
                if e in inorder:
                    nptr[e] += 1
                ready[e].remove(u)
                heapq.heapify(ready[e])
                if e == "act":
                    seg_left[aseg[u]] -= 1
                o0 = ops[units[u][0][0]]
                if e == "act" and o0["tab"] is not None:
                    cur_tab = o0["tab"]
                t = st_
                lat_extra = 0.0
                for m in units[u][0]:
                    t += ops[m]["cost"]
                    lat_extra = ops[m]["lat"] - ops[m]["cost"]
                    order[e].append(m)
                efree[e] = t
                ufin[u] = t + lat_extra
                left -= 1
                for s_ in succ[u]:
                    lat_sync = 0.0 if (ueng[s_] == "pe" and e == "pe" and not o0["dma"]) else 0.15
                    rdy_t[s_] = max(rdy_t[s_], ufin[u] + lat_sync)
                    indeg[s_] -= 1
                    if indeg[s_] == 0:
                        heapq.heappush(ready[ueng[s_]], s_)
            tnow = max(list(efree.values()) + ufin)
        self.est_total = tnow
        return order

    def check_progress(self, order, skip):
        ops = self.ops
        sem = {}
        ptr = {e: 0 for e in self.ENGS}
        total = sum(len(v) for v in order.values())
        done = 0
        while done < total:
            prog = False
            for e in self.ENGS:
                while ptr[e] < len(order[e]):
                    i = order[e][ptr[e]]
                    o = ops[i]
                    reqs = list(o["cdeps"].values()) + [di for di in o["deps"] if ops[di]["dma"]]
                    ok = True
                    for di in reqs:
                        d = ops[di]
                        key = ("d", d["semkey"]) if d["dma"] else ("e", d["eng"])
                        if sem.get(key, 0) < d["val"]:
                            ok = False
                            break
                    if not ok:
                        break
                    if o["val"] is not None:
                        key = ("d", o["semkey"]) if o["dma"] else ("e", e)
                        sem[key] = sem.get(key, 0) + (o["inc"] if o["dma"] else 1)
                        assert sem[key] == o["val"], ("sem value mismatch", i, e, sem[key], o["val"])
                    ptr[e] += 1
                    done += 1
                    prog = True
            if not prog:
                stuck = {e: (order[e][ptr[e]] if ptr[e] < len(order[e]) else None) for e in self.ENGS}
                raise RuntimeError("deadlock in semaphore protocol: %r" % (stuck,))

    def emit(self):
        nc = self.nc
        ops = self.ops
        order = self.schedule()

        def skip(o, d):
            return (not o["dma"]) and (not d["dma"]) and o["eng"] == "pe" and d["eng"] == "pe"

        last_eng = {}
        last_dma = {}
        extra = {}
        cur_phase_last = {}
        nph = self.phase + 1
        per_phase_order = {e: {} for e in self.ENGS}
        for e in self.ENGS:
            for i in order[e]:
                per_phase_order[e].setdefault(ops[i]["phase"], []).append(i)
        acc_last = set()
        for ph in range(nph):
            if ph > 0 and acc_last:
                for e in self.ENGS:
                    lst = per_phase_order[e].get(ph)
                    if lst:
                        extra[lst[0]] = set(acc_last)
            for e in self.ENGS:
                lst = per_phase_order[e].get(ph)
                if not lst:
                    continue
                comp = [i for i in lst if not ops[i]["dma"]]
                if comp:
                    last_eng[e] = comp[-1]
                for i in lst:
                    if ops[i]["dma"]:
                        last_dma[ops[i]["semkey"]] = i
            acc_last = set(last_eng.values()) | set(last_dma.values())
        for i, s_ in extra.items():
            ops[i]["deps"] = set(ops[i]["deps"]) | (s_ - {i})

        pos = {}
        for e in self.ENGS:
            for n_, i in enumerate(order[e]):
                pos[i] = n_
        needed = set()
        for o in ops:
            latest = {}
            for di in o["deps"]:
                d = ops[di]
                if skip(o, d) or d["dma"]:
                    continue
                if d["eng"] not in latest or pos[di] > pos[latest[d["eng"]]]:
                    latest[d["eng"]] = di
            needed.update(latest.values())
            o["cdeps"] = latest
        for i in self.final:
            needed.add(i)
        cnt = {e: 0 for e in self.ENGS}
        dcnt = {}
        for e in self.ENGS:
            for i in order[e]:
                o = ops[i]
                if o["dma"]:
                    k = o["semkey"]
                    dcnt[k] = dcnt.get(k, 0) + o["inc"]
                    o["val"] = dcnt[k]
                elif i in needed:
                    cnt[e] += 1
                    o["val"] = cnt[e]
        self.check_progress(order, skip)
        import contextlib

        with contextlib.ExitStack() as st:
            esem = {e: st.enter_context(nc.semaphore("s_" + e)) for e in self.ENGS}
            dsem = {}
            for k in dcnt:
                dsem[k] = st.enter_context(nc.semaphore("d_%d" % len(dsem)))
            block = st.enter_context(nc.Block())

            def section(ename):
                def body(eng):
                    waited = {}

                    def wait_all(dlist):
                        req = {}
                        for di in dlist:
                            d = ops[di]
                            if d["dma"]:
                                key = ("d", d["semkey"])
                                s = dsem[d["semkey"]]
                            else:
                                key = ("e", d["eng"])
                                s = esem[d["eng"]]
                            if d["val"] > req.get(key, (None, 0))[1]:
                                req[key] = (s, d["val"])
                        for key, (s, v) in req.items():
                            if waited.get(key, 0) >= v:
                                continue
                            eng.wait_ge(s, v)
                            waited[key] = v

                    for i in order[ename]:
                        o = ops[i]
                        wait_all(list(o["cdeps"].values()) + [di for di in o["deps"] if ops[di]["dma"]])
                        ins = o["fn"](eng)
                        if o["val"] is not None:
                            if o["dma"]:
                                ins.then_inc(dsem[o["semkey"]], o["inc"])
                            else:
                                ins.then_inc(esem[ename], 1)
                    if ename == "sp":
                        wait_all(self.final)

                return body

            block.tensor(section("pe"))
            block.scalar(section("act"))
            block.vector(section("dve"))
            block.gpsimd(section("pool"))
            block.sync(section("sp"))


def fsz(ap):
    n = 1
    for s in ap.shape[1:]:
        n *= int(s)
    return n


def mm(P, out, lhsT, rhs, start, stop, r, w):
    n = max(fsz(rhs), 64)
    c = n / 1950.0 * (4.0 if rhs.dtype == F32 else 1.0) + 0.012
    return P.op("pe", lambda e: e.matmul(out, lhsT, rhs, start=start, stop=stop), r, w, cost=c, lat=c + 0.2)


def tr(P, out, in_, ident, r, w):
    c = max(fsz(in_), 64) / 1950.0 * (4.0 if in_.dtype == F32 else 1.0) + 0.03
    return P.op("pe", lambda e: e.transpose(out, in_, ident), r, w, cost=c, lat=c + 0.2)


_TAB = {AF.Exp: "ln_exp", AF.Ln: "ln_exp", AF.Silu: "silu", AF.Sigmoid: "sigmoid"}


def act(P, out, in_, func, r, w, bias=0.0, scale=1.0):
    c = 0.2 + fsz(out) / 1150.0
    if not isinstance(bias, float):
        c += 0.09
    if not isinstance(scale, float):
        c += 0.09
    return P.op("act", lambda e: e.activation(out, in_, func, bias=bias, scale=scale), r, w, cost=c,
                tab=_TAB.get(func))


def _vcost(eng, n, f=1.0):
    if eng == "pool":
        return 0.15 + n * f / 480.0
    return 0.07 + n * f / 960.0


def tt(P, eng, out, in0, in1, op, r, w):
    return P.op(eng, lambda e: e.tensor_tensor(out, in0, in1, op), r, w, cost=_vcost(eng, fsz(out)))


def stt(P, out, in0, scalar, in1, op0, op1, r, w):
    return P.op("dve", lambda e: e.scalar_tensor_tensor(out, in0, scalar, in1, op0, op1), r, w,
                cost=_vcost("dve", fsz(out), 1.42))


def ts(P, eng, out, in0, s1, s2, op0, op1, r, w):
    c = _vcost(eng, fsz(out))
    if s2 is None:
        return P.op(eng, lambda e: e.tensor_scalar(out, in0, s1, None, op0), r, w, cost=c)
    return P.op(eng, lambda e: e.tensor_scalar(out, in0, s1, s2, op0, op1), r, w, cost=c)


def cp(P, eng, out, in_, r, w):
    if eng == "act":
        return P.op("act", lambda e: e.copy(out, in_), r, w, cost=0.2 + fsz(out) / 1150.0)
    return P.op(eng, lambda e: e.tensor_copy(out, in_), r, w, cost=_vcost(eng, fsz(out)))


def dma(P, eng, out, in_, r, w, semkey):
    nbytes = fsz(out) * 128 * (4 if in_.dtype == F32 else 2)
    lat = 2.5 + nbytes / 150e3
    return P.op(eng, lambda e: e.dma_start(out=out, in_=in_), r, w, dma=True, semkey=semkey,
                cost=(1.5 if eng == "pool" else 0.1), lat=lat)


def xkeys(a, b):
    return [("x", t) for t in range(a // 128, (b - 1) // 128 + 1)]


class Builder:
    def __init__(self, mode, layer, final):
        self.mode = mode
        self.layer = layer
        self.final = final
        self.nc = bass.Bass("TRN2", target_bir_lowering=False)
        self.P = Prog(self.nc)

    def declare(self):
        nc = self.nc
        dt = nc.dram_tensor
        self.d_x = dt("x_loc", [NT, D], F32, kind="ExternalInput").ap()
        self.d_consts = dt("consts", [128, 512], F32, kind="ExternalInput").ap()
        self.d_percore = dt("percore", [128, 80], F32, kind="ExternalInput").ap()
        self.d_vecs = dt("vecs", [2, 128, VW], F32, kind="ExternalInput").ap()
        self.d_w_in = dt("w_in", [2, D, INW], F32, kind="ExternalInput").ap()
        if self.mode == "main":
            self.d_w_pool = dt("w_pool", [2, 4, 128, 128], F32, kind="ExternalInput").ap()
            self.d_w_out = dt("w_out", [2, D, D], F32, kind="ExternalInput").ap()
            self.d_w_up = dt("w_up", [2, D, 2 * DFF], F32, kind="ExternalInput").ap()
            self.d_w_down = dt("w_down", [2, DFF, D], F32, kind="ExternalInput").ap()
            self.d_state_in = dt("state_in", [128, SW], F32, kind="ExternalInput").ap()
            self.d_out = dt("y", [2048, D], F32, kind="ExternalOutput").ap()
        else:
            self.d_state_out = dt("state_out", [128, SW], F32, kind="ExternalOutput").ap()

    def carve(self, nbytes_dtype, shape):
        dtype = nbytes_dtype
        n = int(np.prod(shape[1:]))
        nb = n * (4 if dtype == F32 else 2)
        nb = (nb + 63) // 64 * 64
        off = self.u_off
        assert off + nb <= self.u_bytes, ("U overflow", off, nb, self.u_bytes)
        self.u_off += nb
        ap = self.U[:, off // 2: off // 2 + nb // 2]
        if dtype == F32:
            ap = ap.bitcast(F32)
        ap = ap[:, 0:n]
        if len(shape) == 3:
            ap = ap.rearrange("p (a b) -> p a b", a=shape[1])
        elif len(shape) == 4:
            ap = ap.rearrange("p (a b c) -> p a b c", a=shape[1], b=shape[2])
        return ap

    def vec(self, off, n=1):
        return self.vecs[:, self.layer, off:off + n]

    def rmsnorm(self, a, b, gv_off, out_ap, out_keys, okey_r=()):
        P = self.P
        W = b - a
        xk = xkeys(a, b)
        bank = self.mmbank()
        ssps = self.ps[:, bank, 0:W]
        nsub = W // 128
        for j in range(nsub):
            sb = self.sq_i % 2
            self.sq_i += 1
            sq = self.sq[sb]
            act(P, sq, self.x[:, :, a + j * 128: a + (j + 1) * 128], AF.Square, xk, [("sq", sb)])
            for kt in range(KT):
                mm(P, self.ps[:, bank, j * 128:(j + 1) * 128], self.onesb, sq[:, kt, :], kt == 0, kt == KT - 1,
                   [("sq", sb), "onesb"], [("ps", bank)])
        act(P, self.rstd[:, 0:W], ssps, AF.Ln, [("ps", bank)], ["rstd"], bias=self.epsb[:, 0:1], scale=1.0 / D)
        act(P, self.rstd[:, 0:W], self.rstd[:, 0:W], AF.Exp, ["rstd"], ["rstd"], scale=-0.5)
        for kt in range(KT):
            stt(P, out_ap[:, kt, 0:W], self.x[:, kt, a:b], self.vec(gv_off + kt), self.rstd[:, 0:W],
                ALU.mult, ALU.mult, xk + ["rstd", "vecs"] + list(okey_r), out_keys)

    def mmbank(self):
        b = self.mm_i % self.n_mm
        self.mm_i += 1
        return self.mm_banks[b]

    def load_common(self):
        P = self.P
        dma(P, "sp", self.cf, self.d_consts, [], ["cf"], "cf")
        dma(P, "sp", self.percore, self.d_percore, [], ["percore"], "percore")
        dma(P, "sp", self.vecs, self.d_vecs.rearrange("l p v -> p l v"), [], ["vecs"], "vecs")
        cp(P, "dve", self.identb, self.cf[:, 0:128], ["cf"], ["identb"])
        cp(P, "dve", self.onesb, self.cf[:, 256:384], ["cf"], ["onesb"])
        P.op("dve", lambda e: e.memset(self.epsb, EPS), [], ["epsb"])

    def load_x(self):
        P = self.P
        for t in range(NTL):
            b = t % 2
            xt = self.xtok[b]
            xk_ = getattr(self, "xtok_keys", [("xtok", 0), ("xtok", 1)])[b]
            dma(P, "sp", xt, self.d_x[t * 128:(t + 1) * 128, :], [], [xk_], ("xtok", b))
            banks = (0, 1) if b == 0 else (2, 3)
            for kt in range(KT):
                bk = banks[kt // 4]
                tr(P, self.ps[:, bk, (kt % 4) * 128:(kt % 4 + 1) * 128], xt[:, kt * 128:(kt + 1) * 128],
                   self.cf[:, 0:128], [xk_, "cf"], [("ps", bk)])
            for hf in range(2):
                bk = banks[hf]
                src = self.ps[:, bk, :].rearrange("p (a b) -> p a b", a=4)
                dst = self.x[:, hf * 4:(hf + 1) * 4, t * 128:(t + 1) * 128]
                eng = "act" if hf == 0 else "dve"
                cp(P, eng, dst, src, [("ps", bk)], [("x", t)])

    def load_w_in(self, l, only_k_v_g=False):
        P = self.P
        src = self.d_w_in[l].rearrange("(kt p) n -> p kt n", p=128)
        dma(P, "pool", self.wA, src[:, :, 0:1024], [], ["wA"], "wA")
        if only_k_v_g:
            dma(P, "pool", self.wB[:, :, 512:520], src[:, :, 1536:1544], [], ["wB"], "wB")
        else:
            dma(P, "pool", self.wB, src[:, :, 1024:INW], [], ["wB"], "wB")

    def qk_tile(self, g, ct, W, dst, dst_key):
        P = self.P
        bank = self.mmbank()
        for kt in range(KT):
            mm(P, self.ps[:, bank, 0:W], self.wA[:, kt, ct * 128:(ct + 1) * 128], self.hT[:, kt, 0:W],
               kt == 0, kt == KT - 1, ["wA", "hT"], [("ps", bank)])
        pb = self.pre_i % 2
        self.pre_i += 1
        pre = self.pre[pb]
        acc = self.acc[pb]
        cp(P, "act", pre[:, 3:3 + W], self.ps[:, bank, 0:W], [("ps", bank)], [("pre", pb)])
        cp(P, "pool", pre[:, 0:3], self.qkcarry[:, ct, :], [("qkc", ct)], [("pre", pb)])
        cp(P, "pool", self.qkcarry[:, ct, :], pre[:, W:W + 3], [("pre", pb)], [("qkc", ct)])
        wv = lambda j: self.vec(V_QKC + ct * 4 + j)
        ts(P, "dve", acc[:, 0:W], pre[:, 3:3 + W], wv(3), None, ALU.mult, None, [("pre", pb), "vecs"], [("acc", pb)])
        for j in (2, 1, 0):
            stt(P, acc[:, 0:W], pre[:, j:j + W], wv(j), acc[:, 0:W], ALU.mult, ALU.add,
                [("pre", pb), ("acc", pb), "vecs"], [("acc", pb)])
        if dst is None:
            for e in range(2):
                es = slice(e * 64, (e + 1) * 64)
                act(P, self.qTz[es, ct, e, 0:W], acc[es, 0:W], AF.Silu, [("acc", pb)], [dst_key])
        else:
            act(P, dst, acc[:, 0:W], AF.Silu, [("acc", pb)], [dst_key])

    def gates_group(self, g, t0, ntg):
        P = self.P
        G3 = ("ps", 3)
        for j in range(ntg):
            for kt in range(KT):
                mm(P, self.ps[:, 3, 32 + j * 8: 40 + j * 8], self.hT[:, kt, j * 128:(j + 1) * 128],
                   self.wB[:, kt, 512:520], kt == 0, kt == KT - 1, ["hT", "wB"], [G3])
        gv = self.ps[:, 3, 32:32 + ntg * 8].rearrange("p (a b) -> p a b", a=ntg)
        bg = self.vecs[:, self.layer, V_BG:V_BG + 8].unsqueeze(1).to_broadcast([128, ntg, 8])
        tt(P, "dve", self.gsb[:, 0:ntg, :], gv, bg, ALU.add, [G3, "vecs"], ["gsb"])
        act(P, self.lfn[:, 0:ntg, :], self.gsb[:, 0:ntg, 4:8], AF.Exp, ["gsb"], ["lfn"], scale=-1.0)
        act(P, self.lfn[:, 0:ntg, :], self.lfn[:, 0:ntg, :], AF.Ln, ["lfn"], ["lfn"], bias=1.0)
        lf2 = self.lfn[:, 0:ntg, :]
        mm(P, self.ps[:, 3, 0:ntg * 4], self.cf[:, 128:256], lf2, True, True, ["cf", "lfn"], [G3])
        mm(P, self.ps[:, 3, 16:16 + ntg * 4], self.cf[:, 256:384], lf2, True, True, ["cf", "lfn"], [G3])
        bneg = self.ps[:, 3, 0:ntg * 4].rearrange("p (a b) -> p a b", a=ntg)
        totn = self.ps[:, 3, 16:16 + ntg * 4].rearrange("p (a b) -> p a b", a=ntg)
        tt(P, "dve", self.cS[:, t0:t0 + ntg, :], self.gsb[:, 0:ntg, 0:4], bneg, ALU.add, [G3, "gsb"], ["cS"])
        act(P, self.cS[:, t0:t0 + ntg, :], self.cS[:, t0:t0 + ntg, :], AF.Exp, ["cS"], ["cS"])
        act(P, self.emb[:, t0:t0 + ntg, :], bneg, AF.Exp, [G3], ["emb"])
        for e in range(2):
            sl = slice(e * 64, (e + 1) * 64)
            act(P, self.Gp[sl, t0:t0 + ntg, :], totn[sl, :, e::2], AF.Exp, [G3], ["Gp"], scale=-1.0)

    def v_tile(self, j, t):
        P = self.P
        bank = self.mmbank()
        for kt in range(KT):
            mm(P, self.ps[:, bank, :], self.hT[:, kt, j * 128:(j + 1) * 128], self.wA[:, kt, 512:1024],
               kt == 0, kt == KT - 1, ["hT", "wA"], [("ps", bank)])
        src = self.ps[:, bank, :].rearrange("p (a b) -> p a b", a=4)
        cb = self.cS[:, t, :].unsqueeze(2).to_broadcast([128, 4, 128])
        vb = j % 4
        tt(P, "dve", self.vt[:, vb, :, 0:128], src, cb, ALU.mult, [("ps", bank), "cS"], [("vt", vb)])
        cp(P, "dve", self.vt[:, vb, :, 128:129], self.cS[:, t, :].unsqueeze(2), ["cS"], [("vt", vb)])

    def k_transpose(self, j):
        P = self.P
        T5 = ("ps", 5)
        tp = self.ps[:, 5, :].bitcast(BF16)
        for pr in range(2):
            tr(P, tp[:, pr * 128:(pr + 1) * 128], self.kTw[:, pr, j * 128:(j + 1) * 128], self.identb,
               ["kTw", "identb"], [T5])
        cp(P, "act", self.ktok[:, j % 4, :], tp[:, 0:256], [T5], [("ktok", j % 4)])

    def kv_update(self, j, t):
        P = self.P
        vb = j % 4
        for h in range(4):
            pr, e = h // 2, h % 2
            bk = 6 + pr
            mm(P, self.ps[e * 64:(e + 1) * 64, bk, 258:387], self.ktok[:, j % 4, h * 64:(h + 1) * 64],
               self.vt[:, vb, h, :], True, True, [("ktok", j % 4), ("vt", vb)], [("ps", bk)])
        kv = self.ps[:, 6:8, 258:387]
        tt(P, "dve", self.Cst, self.Cst, kv, ALU.add, ["Cst", ("ps", 6), ("ps", 7)], ["Cst"])
        gb = self.Gp[:, t, :].unsqueeze(2).to_broadcast([128, 2, 129])
        tt(P, "dve", self.Cst, self.Cst, gb, ALU.mult, ["Cst", "Gp"], ["Cst"])
        if getattr(self, "make_cb", self.mode == "main"):
            act(P, self.Cb, self.Cst, AF.Copy, ["Cst"], ["Cb"], scale=0.125)


    def o_tile(self, i, W):
        P = self.P
        bank = self.mmbank()
        for kt in range(KT):
            mm(P, self.ps[:, bank, 0:W], self.wB[:, kt, i * 128:(i + 1) * 128], self.hT[:, kt, 0:W],
               kt == 0, kt == KT - 1, ["wB", "hT"], [("ps", bank)])
        act(P, self.sigo[:, i, 0:W], self.ps[:, bank, 0:W], AF.Sigmoid, [("ps", bank)], ["sigo"])

    def u_tile(self, g, gi, W):
        P = self.P
        bank = self.mmbank()
        for kt in range(KT):
            mm(P, self.ps[:, bank, 0:W], self.wB[:, kt, 520 + gi * 128:520 + (gi + 1) * 128], self.hT[:, kt, 0:W],
               kt == 0, kt == KT - 1, ["wB", "hT"], [("ps", bank)])
        ub, sA, sB = self.pre[0], self.pre[1], self.acc[0]
        kU, kA, kB = ("pre", 0), ("pre", 1), ("acc", 0)
        cp(P, "act", ub[:, 16:16 + W], self.ps[:, bank, 0:W], [("ps", bank)], [kU])
        cp(P, "pool", ub[:, 0:16], self.ucarry[:, gi, :], [("ucar", gi)], [kU])
        cp(P, "pool", self.ucarry[:, gi, :], ub[:, W:W + 16], [kU], [("ucar", gi)])
        E = 16 + W
        tt(P, "pool", sA[:, 1:E], ub[:, 1:E], ub[:, 0:E - 1], ALU.add, [kU], [kA])
        fin, kf = sA, kA
        if gi >= 1:
            tt(P, "pool", sB[:, 3:E], sA[:, 3:E], sA[:, 1:E - 2], ALU.add, [kA], [kB])
            fin, kf = sB, kB
        if gi >= 2:
            tt(P, "pool", sA[:, 7:E], sB[:, 7:E], sB[:, 3:E - 4], ALU.add, [kB], [kA])
            fin, kf = sA, kA
        if gi >= 3:
            tt(P, "pool", sB[:, 15:E], sA[:, 15:E], sA[:, 7:E - 8], ALU.add, [kA], [kB])
            fin, kf = sB, kB
        w = float(2 ** (gi + 1))
        db = self.dT_i % 2
        self.dT_i += 1
        dT = self.dT[db]
        stt(P, dT[:, 0:W], fin[:, 16:16 + W], 1.0 / w, ub[:, 16:16 + W], ALU.mult, ALU.subtract,
            [kf, kU], [("dT", db)])
        if g == 1:
            tt(P, "dve", self.tmp16, fin[:, 16:32], self.percore[:, gi * 16:(gi + 1) * 16], ALU.mult,
               [kf, "percore"], ["tmp16"])
            tt(P, "dve", dT[:, 0:16], self.tmp16, ub[:, 16:32], ALU.subtract, ["tmp16", kU], [("dT", db)])
        bank2 = self.mmbank()
        mm(P, self.ps[:, bank2, 0:W], self.wpool[:, gi, :], dT[:, 0:W], True, True, ["wpool", ("dT", db)],
           [("ps", bank2)])
        act(P, self.hpT[:, gi, 0:W], self.ps[:, bank2, 0:W], AF.Copy, [("ps", bank2), "vecs"], ["hpT"],
            scale=self.vec(V_PS + gi))

    def s2_tile(self, j, t):
        P = self.P
        js = slice(j * 128, (j + 1) * 128)
        vb = j % 4
        for h in range(4):
            pr, e = h // 2, h % 2
            es = slice(e * 64, (e + 1) * 64)
            mm(P, self.ps[:, 4, h * 128:(h + 1) * 128], self.kTw[:, pr, js], self.qTz[:, pr, e, js], True, True,
               ["kTw", "qTw"], [("ps", 4)])
        ptv = self.ps[:, 4, :].rearrange("p (a b) -> p a b", a=4)
        mb = self.cf[:, 384:512].unsqueeze(1).to_broadcast([128, 4, 128])
        tt(P, "dve", self.PTm, ptv, mb, ALU.mult, [("ps", 4), "cf"], ["PTm"])
        for h in range(4):
            pr, e = h // 2, h % 2
            es = slice(e * 64, (e + 1) * 64)
            bk = 6 + pr
            o = self.ps[:, bk, e * 129:(e + 1) * 129]
            mm(P, o, self.PTm[:, h, :], self.vt[:, vb, h, :], True, False, ["PTm", ("vt", vb)], [("ps", bk)])
            mm(P, o, self.qTz[:, pr, e, js], self.Cb[:, pr, :], False, True, ["qTw", "Cb"], [("ps", bk)])
        self.kv_update(j, t)
        N4 = self.ps[:, 6:8, 0:258].rearrange("p a (e c) -> p a e c", e=2)
        NK = [("ps", 6), ("ps", 7)]
        sm = self.small
        v22 = lambda c0: sm[:, c0:c0 + 4].rearrange("p (a e) -> p a e", a=2)
        den, rden, ssq, t1, t2, scl = v22(0), v22(4), v22(8), v22(12), v22(16), v22(20)
        embv = self.emb[:, t, :].rearrange("p (a e) -> p a e", a=2)
        act(P, den, N4[:, :, :, 128], AF.Abs, NK, ["den"])
        tt(P, "dve", den, den, embv, ALU.max, ["den", "emb"], ["den"])
        P.op("dve", lambda e_: e_.reciprocal(rden, den), ["den"], ["rden"])
        sq4 = self.sqN.rearrange("p (a e) d -> p a e d", a=2)
        act(P, sq4, N4[:, :, :, 0:128], AF.Square, NK, ["sqN"])
        P.op("dve", lambda e_: e_.tensor_reduce(ssq, sq4, AX.X, ALU.add), ["sqN"], ["ssq"])
        tt(P, "dve", t1, rden, rden, ALU.mult, ["rden"], ["t1"])
        tt(P, "dve", t2, ssq, t1, ALU.mult, ["ssq", "t1"], ["t2"])
        act(P, t2, t2, AF.Ln, ["t2"], ["t2"], bias=self.epsb[:, 0:1], scale=1.0 / 128.0)
        act(P, t2, t2, AF.Exp, ["t2"], ["t2"], scale=-0.5)
        tt(P, "dve", scl, rden, t2, ALU.mult, ["rden", "t2"], ["scl"])
        hn4 = self.hn.rearrange("p (a e) d -> p a e d", a=2)
        tt(P, "dve", hn4, N4[:, :, :, 0:128], scl.unsqueeze(3).to_broadcast([128, 2, 2, 128]), ALU.mult,
           NK + ["scl"], ["hn"])
        tp = self.ps[:, 5, :].bitcast(BF16)
        T5 = ("ps", 5)
        for h in range(4):
            tr(P, tp[:, 256 + h * 128:256 + (h + 1) * 128], self.hn[:, h, :], self.identb, ["hn", "identb"], [T5])
        for h in range(4):
            stt(P, self.sigo[:, h, js], tp[:, 256 + h * 128:256 + (h + 1) * 128], self.vec(V_GH + h),
                self.sigo[:, h, js], ALU.mult, ALU.mult, [T5, "sigo", "vecs"], ["sigo"])

    def o_group(self, a, b):
        P = self.P
        W = b - a
        for dt_ in range(8):
            bank = self.mmbank()
            for kt in range(KT):
                rhs = self.sigo[:, kt, 0:W] if kt < 4 else self.hpT[:, kt - 4, 0:W]
                mm(P, self.ps[:, bank, 0:W], self.wout[:, kt, dt_ * 128:(dt_ + 1) * 128], rhs, kt == 0, kt == KT - 1,
                   ["wout", "sigo", "hpT"], [("ps", bank)])
            tt(P, "dve", self.x[:, dt_, a:b], self.ps[:, bank, 0:W], self.x[:, dt_, a:b], ALU.add,
               [("ps", bank)] + xkeys(a, b), xkeys(a, b))

    def load_ffn_chunk(self, l, c, slot):
        P = self.P
        f0, f1 = FCH[c]
        T = f1 - f0
        su = self.d_w_up[l].rearrange("(kt p) n -> p kt n", p=128)
        dma(P, "pool", self.wu[slot][:, :, 0:T * 128], su[:, :, f0 * 128:f1 * 128], [], [("wu", slot)], ("wu", slot))
        dma(P, "pool", self.wu[slot][:, :, 512:512 + T * 128], su[:, :, DFF + f0 * 128:DFF + f1 * 128], [],
            [("wu", slot)], ("wu", slot))
        sd = self.d_w_down[l].rearrange("(i p) n -> p i n", p=128)
        dma(P, "pool", self.wd[slot][:, 0:T, :], sd[:, f0:f1, :], [], [("wd", slot)], ("wd", slot))

    def ffn_conv(self, bank, Wo, f, acc, kacc):
        P = self.P
        Wn = Wo + 2
        act(P, acc[:, 0:Wo], self.ps[:, bank, 2:Wn], AF.Identity, [("ps", bank), "vecs"], [kacc],
            bias=self.vec(V_FB + f), scale=self.vec(V_FC + f * 3 + 2))
        stt(P, acc[:, 0:Wo], self.ps[:, bank, 1:Wn - 1], self.vec(V_FC + f * 3 + 1), acc[:, 0:Wo], ALU.mult, ALU.add,
            [("ps", bank), kacc, "vecs"], [kacc])
        stt(P, acc[:, 0:Wo], self.ps[:, bank, 0:Wo], self.vec(V_FC + f * 3 + 0), acc[:, 0:Wo], ALU.mult, ALU.add,
            [("ps", bank), kacc, "vecs"], [kacc])

    def ffn_phase(self, l, tok0):
        P = self.P
        n = NT - tok0
        nwin = -(-n // 510)
        base = n // nwin
        wins = []
        s = tok0
        for i in range(nwin):
            Wo = base + (1 if i < n - base * nwin else 0)
            wins.append((s, Wo))
            s += Wo
        assert s == NT
        pair_i = 0
        dn_i = 0
        ab_i = 0
        for c, (f0, f1) in enumerate(FCH):
            slot = c % 2
            T = f1 - f0
            for (s, Wo) in wins:
                Wn = Wo + 2
                ab = ab_i % 2
                ab_i += 1
                for i in range(T):
                    pb = pair_i % 2
                    pair_i += 1
                    bg, bv = (0, 1) if pb == 0 else (2, 3)
                    for kt in range(KT):
                        mm(P, self.ps[:, bg, 0:Wn], self.wu[slot][:, kt, i * 128:(i + 1) * 128],
                           self.hT2[:, kt, s:s + Wn], kt == 0, kt == KT - 1, [("wu", slot), "hT2"], [("ps", bg)])
                    for kt in range(KT):
                        mm(P, self.ps[:, bv, 0:Wn], self.wu[slot][:, kt, 512 + i * 128:512 + (i + 1) * 128],
                           self.hT2[:, kt, s:s + Wn], kt == 0, kt == KT - 1, [("wu", slot), "hT2"], [("ps", bv)])
                    self.ffn_conv(bg, Wo, f0 + i, self.accg[pb], ("accg", pb))
                    self.ffn_conv(bv, Wo, FT + f0 + i, self.accv[pb], ("accv", pb))
                    act(P, self.sg[pb][:, 0:Wo], self.accg[pb][:, 0:Wo], AF.Silu, [("accg", pb)], [("sg", pb)])
                    tt(P, "dve", self.actb[ab][:, i, 0:Wo], self.sg[pb][:, 0:Wo], self.accv[pb][:, 0:Wo], ALU.mult,
                       [("sg", pb), ("accv", pb)], [("actb", ab)])
                for dt_ in range(8):
                    bank = 4 + dn_i % 4
                    dn_i += 1
                    for i in range(T):
                        mm(P, self.ps[:, bank, 0:Wo], self.wd[slot][:, i, dt_ * 128:(dt_ + 1) * 128],
                           self.actb[ab][:, i, 0:Wo], i == 0, i == T - 1, [("wd", slot), ("actb", ab)], [("ps", bank)])
                    tt(P, "dve", self.x[:, dt_, s:s + Wo], self.ps[:, bank, 0:Wo], self.x[:, dt_, s:s + Wo], ALU.add,
                       [("ps", bank)] + xkeys(s, s + Wo), xkeys(s, s + Wo))
            if c + 2 < len(FCH):
                self.load_ffn_chunk(l, c + 2, slot)

    def build_main(self):
        nc, P = self.nc, self.P
        l = self.layer
        self.declare()
        import contextlib

        with contextlib.ExitStack() as st:
            sb = lambda name, shape, dt_: st.enter_context(nc.sbuf_tensor("sb_" + name, shape, dt_))[:]
            self.x = sb("x", [128, KT, NT], F32)
            self.cf = sb("cf", [128, 512], F32)
            self.percore = sb("percore", [128, 80], F32)
            self.vecs = sb("vecs", [128, 2, VW], F32)
            self.identb = sb("identb", [128, 128], BF16)
            self.onesb = sb("onesb", [128, 128], BF16)
            self.epsb = sb("epsb", [128, 1], F32)
            self.wA = sb("wA", [128, KT, 1024], BF16)
            self.wu = [sb("wu0", [128, KT, 1024], BF16), None]
            self.wd = [sb("wd0", [128, 4, 1024], BF16), None]
            self.qkcarry = sb("qkcarry", [128, 4, 3], F32)
            self.ucarry = sb("ucarry", [128, 4, 16], F32)
            self.gsb = sb("gsb", [128, 4, 8], F32)
            self.lfn = sb("lfn", [128, 4, 4], F32)
            self.cS = sb("cS", [128, NTL, 4], F32)
            self.emb = sb("emb", [128, NTL, 4], F32)
            self.Gp = sb("Gp", [128, NTL, 2], F32)
            self.Cst = sb("Cst", [128, 2, 129], F32)
            self.Cb = sb("Cb", [128, 2, 129], BF16)
            self.small = sb("small", [128, 32], F32)
            self.tmp16 = sb("tmp16", [128, 16], F32)
            self.u_bytes = (nc.sbuf_bytes_remaining - 256) // 64 * 64
            self.U = sb("U", [128, self.u_bytes // 2], BF16)
            self.ps = st.enter_context(nc.psum_tensor("ps", [128, 8, 512], F32))[:]
            self.mm_banks, self.n_mm, self.mm_i = [0, 1, 2], 3, 0
            self.sq_i = self.pre_i = self.dT_i = 0

            self.u_off = 0
            cv = self.carve
            self.wB = cv(BF16, [128, KT, 1032])
            self.wout = cv(BF16, [128, KT, 1024])
            self.wpool = cv(BF16, [128, 4, 128])
            self.sq = [cv(BF16, [128, KT, 128]) for _ in range(2)]
            self.rstd = cv(F32, [128, 512])
            self.hT = cv(BF16, [128, KT, 512])
            self.pre = [cv(F32, [128, 528]) for _ in range(2)]
            self.acc = [cv(F32, [128, 528]) for _ in range(2)]
            self.qTz = cv(BF16, [128, 2, 2, 512])
            self.kTw = cv(BF16, [128, 2, 512])
            self.ktok = cv(BF16, [128, 4, 256])
            self.vt = cv(BF16, [128, 4, 4, 129])
            self.sigo = cv(BF16, [128, 4, 512])
            self.hpT = cv(BF16, [128, 4, 512])
            self.dT = [cv(BF16, [128, 512]) for _ in range(2)]
            self.PTm = cv(BF16, [128, 4, 128])
            self.hn = cv(BF16, [128, 4, 128])
            self.sqN = cv(F32, [128, 4, 128])
            self.xtok = [self.sigo.rearrange("p a b -> p (a b)").bitcast(F32),
                         self.hpT.rearrange("p a b -> p (a b)").bitcast(F32)]
            self.xtok_keys = ["sigo", "hpT"]

            self.load_common()
            self.load_w_in(l)
            so = self.d_w_out[l].rearrange("(kt p) n -> p kt n", p=128)
            dma(P, "pool", self.wout, so, [], ["wout"], "wout")
            dma(P, "pool", self.wpool, self.d_w_pool[l].rearrange("g c d -> c g d"), [], ["wpool"], "wpool")
            self.load_ffn_chunk(l, 0, 0)
            self.load_x()
            dma(P, "sp", self.Cst.rearrange("p a b -> p (a b)"), self.d_state_in, [], ["Cst"], "cst")
            act(P, self.Cb, self.Cst, AF.Copy, ["Cst"], ["Cb"], scale=0.125)
            P.op("dve", lambda e: e.memset(self.qkcarry, 0.0), [], [("qkc", c) for c in range(4)])
            P.op("dve", lambda e: e.memset(self.ucarry, 0.0), [], [("ucar", c) for c in range(4)])
            P.op("pool", lambda e: e.memset(self.qTz[64:128, :, 0, :], 0.0), [], ["qTw"])
            P.op("pool", lambda e: e.memset(self.qTz[0:64, :, 1, :], 0.0), [], ["qTw"])

            for g, (t0, t1) in enumerate(GROUPS):
                a, b = t0 * 128, t1 * 128
                W = b - a
                ntg = t1 - t0
                self.rmsnorm(a, b, V_GMIX, self.hT, ["hT"])
                for ct in range(4):
                    dst = None if ct < 2 else self.kTw[:, ct - 2, 0:W]
                    self.qk_tile(g, ct, W, dst, "qTw" if ct < 2 else "kTw")
                self.gates_group(g, t0, ntg)
                for j in range(ntg):
                    if t0 + j >= 1:
                        self.v_tile(j, t0 + j)
                        self.k_transpose(j)
                for i in range(4):
                    self.o_tile(i, W)
                for gi in range(4):
                    self.u_tile(g, gi, W)
                for j in range(ntg):
                    if t0 + j >= 1:
                        self.s2_tile(j, t0 + j)
                self.o_group(a, b)

            P.barrier()
            self.u_off = 0
            self.wu[1] = cv(BF16, [128, KT, 1024])
            self.wd[1] = cv(BF16, [128, 4, 1024])
            self.hT2 = cv(BF16, [128, KT, NT + 2])
            self.sq = [cv(BF16, [128, KT, 128]) for _ in range(2)]
            self.rstd = cv(F32, [128, 512])
            self.accg = [cv(F32, [128, 512]) for _ in range(2)]
            self.accv = [cv(F32, [128, 512]) for _ in range(2)]
            self.sg = [cv(F32, [128, 512]) for _ in range(2)]
            self.actb = [cv(BF16, [128, 4, 512]) for _ in range(2)]
            self.load_ffn_chunk(l, 1, 1)
            P.op("dve", lambda e: e.memset(self.hT2[:, :, 0:2], 0.0), [], ["hT2"])
            for g, (t0, t1) in enumerate(GROUPS):
                a, b = t0 * 128, t1 * 128
                self.rmsnorm(a, b, V_GF, self.hT2[:, :, 2 + a:2 + b], ["hT2"])
            self.ffn_phase(l, 128)

            P.barrier()
            self.u_off = 0
            self.sq = [cv(BF16, [128, KT, 128]) for _ in range(2)]
            self.rstd = cv(F32, [128, 512])
            yT = [cv(F32, [128, KT, 128]) for _ in range(2)]
            ytok = [cv(F32, [128, D]) for _ in range(2)]
            for t in range(2, NTL):
                b2 = t % 2
                a, b = t * 128, (t + 1) * 128
                if self.final:
                    self.rmsnorm(a, b, V_FIN, yT[b2], [("yT", b2)])
                banks = (4, 5) if b2 == 0 else (6, 7)
                for kt in range(KT):
                    bk = banks[kt // 4]
                    if self.final:
                        src, sk = yT[b2][:, kt, :], [("yT", b2)]
                    else:
                        src, sk = self.x[:, kt, a:b], [("x", t)]
                    tr(P, self.ps[:, bk, (kt % 4) * 128:(kt % 4 + 1) * 128], src, self.cf[:, 0:128], sk + ["cf"],
                       [("ps", bk)])
                cp(P, "act", ytok[b2][:, 0:512], self.ps[:, banks[0], :], [("ps", banks[0])], [("ytok", b2)])
                cp(P, "dve", ytok[b2][:, 512:1024], self.ps[:, banks[1], :], [("ps", banks[1])], [("ytok", b2)])
                o = dma(P, "sp", self.d_out[(t - 2) * 128:(t - 1) * 128, :], ytok[b2], [("ytok", b2)], [("yo", t)],
                        ("yout", b2))
                P.final.append(o)
            P.emit()
        return nc


    def declare_fused(self):
        nc = self.nc
        dt = nc.dram_tensor
        self.d_x = dt("x_loc", [NT, D], F32, kind="ExternalInput").ap()
        self.d_consts = dt("consts", [128, 512], F32, kind="ExternalInput").ap()
        self.d_percore = dt("percore", [128, 80], F32, kind="ExternalInput").ap()
        self.d_vecs = dt("vecs", [2, 128, VW], F32, kind="ExternalInput").ap()
        self.d_w_in = dt("w_in", [2, D, INW], F32, kind="ExternalInput").ap()
        self.d_w_pool = dt("w_pool", [2, 4, 128, 128], F32, kind="ExternalInput").ap()
        self.d_w_out = dt("w_out", [2, D, D], F32, kind="ExternalInput").ap()
        self.d_w_up = dt("w_up", [2, D, 2 * DFF], F32, kind="ExternalInput").ap()
        self.d_w_down = dt("w_down", [2, DFF, D], F32, kind="ExternalInput").ap()
        self.d_out = dt("y", [2048, D], F32, kind="ExternalOutput").ap()
        self.d_st_loc = dt("cc_loc", [128, SW], F32).ap()
        self.d_st_all = dt("cc_all", [8 * 128, SW], F32).ap()
        self.d_h_loc = dt("cch_loc", [128, 16], F32).ap()
        self.d_h_all = dt("cch_all", [8 * 128, 16], F32).ap()

    def w_in_src(self, l):
        return self.d_w_in[l].rearrange("(kt p) n -> p kt n", p=128)

    def phase_pass1(self, inj, pub):
        P = self.P
        self.make_cb = False
        P.op("dve", lambda e: e.memset(self.qkcarry, 0.0), [], [("qkc", c) for c in range(4)])
        P.op("dve", lambda e: e.memset(self.Cst, 0.0), [], ["Cst"])
        glist = [(g, t0, t1) for g, (t0, t1) in enumerate(GROUPS) if t0 < pub]
        self.rmsnorm(glist[0][1] * 128, glist[0][2] * 128, V_GMIX, self.hT, ["hT"])
        for gi_, (g, t0, t1) in enumerate(glist):
            a, b = t0 * 128, t1 * 128
            W = b - a
            ntg = t1 - t0
            for ct in (2, 3):
                self.qk_tile(g, ct, W, self.kTw[:, ct - 2, 0:W], "kTw")
            self.gates_group(g, t0, ntg)
            live = [j for j in range(ntg) if inj <= t0 + j < pub]
            for j in live:
                self.v_tile(j, t0 + j)
                self.k_transpose(j)
            if gi_ + 1 < len(glist):
                self.rmsnorm(glist[gi_ + 1][1] * 128, glist[gi_ + 1][2] * 128, V_GMIX, self.hT, ["hT"])
            for j in live:
                self.kv_update(j, t0 + j)

    def phase_exchange(self):
        P = self.P
        cflat = self.Cst.rearrange("p a b -> p (a b)")
        dma(P, "sp", self.d_st_loc, cflat, ["Cst"], ["stloc"], "stloc")
        P.op("pool", lambda e: e.collective_compute("AllGather", ALU.bypass, replica_groups=[list(range(8))],
                                                    ins=[self.d_st_loc], outs=[self.d_st_all]),
             ["stloc"], ["stalld"], dma=True, semkey="cc", inc=1, cost=1.0, lat=30.0)
        dma(P, "sp", self.stall, self.d_st_all.rearrange("(r p) n -> p r n", p=128), ["stalld"], ["stall"], "stall")
        ts(P, "dve", cflat, self.stall[:, 0, :], self.percore[:, 65:66], None, ALU.mult, None,
           ["stall", "percore"], ["Cst"])
        for r in range(1, 8):
            stt(P, cflat, self.stall[:, r, :], self.percore[:, 65 + r:66 + r], cflat, ALU.mult, ALU.add,
                ["stall", "percore", "Cst"], ["Cst"])
        act(P, self.Cb, self.Cst, AF.Copy, ["Cst"], ["Cb"], scale=0.125)

    def phase_halo_exchange(self):
        P = self.P
        v3 = self.xh_s.rearrange("p (a b) -> p a b", a=8)
        cp(P, "dve", v3, self.x[:, :, NT - 2:NT], [("x", NTL - 1)], ["xh_s"])
        dma(P, "sp", self.d_h_loc, self.xh_s, ["xh_s"], ["hloc"], "hloc")
        P.op("pool", lambda e: e.collective_compute("AllGather", ALU.bypass, replica_groups=[list(range(8))],
                                                    ins=[self.d_h_loc], outs=[self.d_h_all]),
             ["hloc"], ["halld"], dma=True, semkey="cc2", inc=1, cost=1.0, lat=15.0)
        dma(P, "sp", self.hall, self.d_h_all.rearrange("(r p) n -> p r n", p=128), ["halld"], ["hall"], "hall")
        ts(P, "dve", self.xh_s, self.hall[:, 0, :], self.percore[:, 65:66], None, ALU.mult, None,
           ["hall", "percore"], ["xh_s"])
        for r in range(1, 8):
            stt(P, self.xh_s, self.hall[:, r, :], self.percore[:, 65 + r:66 + r], self.xh_s, ALU.mult, ALU.add,
                ["hall", "percore", "xh_s"], ["xh_s"])
        cp(P, "dve", self.x[:, :, 254:256], v3, ["xh_s"], [("x", 1)])

    def phase_mixer(self, inj, first_out_group):
        P = self.P
        self.make_cb = True
        P.op("dve", lambda e: e.memset(self.qkcarry, 0.0), [], [("qkc", c) for c in range(4)])
        P.op("dve", lambda e: e.memset(self.ucarry, 0.0), [], [("ucar", c) for c in range(4)])
        P.op("pool", lambda e: e.memset(self.qTz[64:128, :, 0, :], 0.0), [], ["qTw"])
        P.op("pool", lambda e: e.memset(self.qTz[0:64, :, 1, :], 0.0), [], ["qTw"])
        self.rmsnorm(GROUPS[0][0] * 128, GROUPS[0][1] * 128, V_GMIX, self.hT, ["hT"])
        for g, (t0, t1) in enumerate(GROUPS):
            a, b = t0 * 128, t1 * 128
            W = b - a
            ntg = t1 - t0
            for ct in range(4):
                dst = None if ct < 2 else self.kTw[:, ct - 2, 0:W]
                self.qk_tile(g, ct, W, dst, "qTw" if ct < 2 else "kTw")
            self.gates_group(g, t0, ntg)
            for j in range(ntg):
                if t0 + j >= inj:
                    self.v_tile(j, t0 + j)
                    self.k_transpose(j)
            for i in range(4):
                self.o_tile(i, W)
            for gi in range(4):
                self.u_tile(g, gi, W)
            if g + 1 < len(GROUPS):
                self.rmsnorm(GROUPS[g + 1][0] * 128, GROUPS[g + 1][1] * 128, V_GMIX, self.hT, ["hT"])
            for j in range(ntg):
                if t0 + j >= inj:
                    self.s2_tile(j, t0 + j)
            if g >= first_out_group:
                self.o_group(a, b)

    def build_fused(self):
        nc, P = self.nc, self.P
        self.declare_fused()
        import contextlib

        with contextlib.ExitStack() as st:
            sb = lambda name, shape, dt_: st.enter_context(nc.sbuf_tensor("sb_" + name, shape, dt_))[:]
            self.x = sb("x", [128, KT, NT], F32)
            self.cf = sb("cf", [128, 512], F32)
            self.percore = sb("percore", [128, 80], F32)
            self.vecs = sb("vecs", [128, 2, VW], F32)
            self.identb = sb("identb", [128, 128], BF16)
            self.onesb = sb("onesb", [128, 128], BF16)
            self.epsb = sb("epsb", [128, 1], F32)
            self.wA = sb("wA", [128, KT, 1024], BF16)
            self.wu = [sb("wu0", [128, KT, 1024], BF16), None]
            self.wd = [sb("wd0", [128, 4, 1024], BF16), None]
            self.qkcarry = sb("qkcarry", [128, 4, 3], F32)
            self.ucarry = sb("ucarry", [128, 4, 16], F32)
            self.gsb = sb("gsb", [128, 4, 8], F32)
            self.lfn = sb("lfn", [128, 4, 4], F32)
            self.cS = sb("cS", [128, NTL, 4], F32)
            self.emb = sb("emb", [128, NTL, 4], F32)
            self.Gp = sb("Gp", [128, NTL, 2], F32)
            self.Cst = sb("Cst", [128, 2, 129], F32)
            self.Cb = sb("Cb", [128, 2, 129], BF16)
            self.small = sb("small", [128, 32], F32)
            self.tmp16 = sb("tmp16", [128, 16], F32)
            self.xh_s = sb("xh_s", [128, 16], F32)
            self.hall = sb("hall", [128, 8, 16], F32)
            self.u_bytes = (nc.sbuf_bytes_remaining - 256) // 64 * 64
            self.U = sb("U", [128, self.u_bytes // 2], BF16)
            self.ps = st.enter_context(nc.psum_tensor("ps", [128, 8, 512], F32))[:]
            self.mm_banks, self.n_mm, self.mm_i = [0, 1, 2], 3, 0
            self.sq_i = self.pre_i = self.dT_i = 0
            cv = self.carve

            self.u_off = 0
            self.xtok = [cv(F32, [128, D]) for _ in range(2)]
            self.load_common()
            dma(P, "pool", self.wA, self.w_in_src(0)[:, :, 0:1024], [], ["wA"], "wA")
            self.layer = 0
            self.load_ffn_chunk(0, 0, 0)
            self.load_x()

            for l in (0, 1):
                self.layer = l
                inj = 1 if l == 0 else 2
                pub = 17 if l == 0 else 18
                P.barrier()
                self.u_off = 0
                self.wB = cv(BF16, [128, KT, 1032])
                self.wout = cv(BF16, [128, KT, 1024])
                self.wpool = cv(BF16, [128, 4, 128])
                self.sq = [cv(BF16, [128, KT, 128]) for _ in range(2)]
                self.rstd = cv(F32, [128, 512])
                self.hT = cv(BF16, [128, KT, 512])
                self.pre = [cv(F32, [128, 528]) for _ in range(2)]
                self.acc = [cv(F32, [128, 528]) for _ in range(2)]
                self.kTw = cv(BF16, [128, 2, 512])
                self.ktok = cv(BF16, [128, 4, 256])
                self.vt = cv(BF16, [128, 4, 4, 129])
                self.stall = cv(F32, [128, 8, SW])
                dma(P, "pool", self.wB[:, :, 512:520], self.w_in_src(l)[:, :, 1536:1544], [], ["wB"], "wB")
                dma(P, "pool", self.wB[:, :, 0:512], self.w_in_src(l)[:, :, 1024:1536], [], ["wB2"], "wB2")
                dma(P, "pool", self.wB[:, :, 520:1032], self.w_in_src(l)[:, :, 1544:INW], [], ["wB2"], "wB2")
                dma(P, "pool", self.wout, self.d_w_out[l].rearrange("(kt p) n -> p kt n", p=128), [], ["wout"], "wout")
                dma(P, "pool", self.wpool, self.d_w_pool[l].rearrange("g c d -> c g d"), [], ["wpool"], "wpool")
                if l == 1:
                    ts(P, "dve", self.x[:, :, 0:256], self.x[:, :, 0:256], self.percore[:, 64:65], None, ALU.mult,
                       None, xkeys(0, 256) + ["percore"], xkeys(0, 256))
                self.phase_pass1(inj, pub)
                self.phase_exchange()
                P.barrier()
                self.u_off = 0
                self.wB = cv(BF16, [128, KT, 1032])
                self.wout = cv(BF16, [128, KT, 1024])
                self.wpool = cv(BF16, [128, 4, 128])
                self.sq = [cv(BF16, [128, KT, 128]) for _ in range(2)]
                self.rstd = cv(F32, [128, 512])
                self.hT = cv(BF16, [128, KT, 512])
                self.pre = [cv(F32, [128, 528]) for _ in range(2)]
                self.acc = [cv(F32, [128, 528]) for _ in range(2)]
                self.qTz = cv(BF16, [128, 2, 2, 512])
                self.kTw = cv(BF16, [128, 2, 512])
                self.ktok = cv(BF16, [128, 4, 256])
                self.vt = cv(BF16, [128, 4, 4, 129])
                self.sigo = cv(BF16, [128, 4, 512])
                self.hpT = cv(BF16, [128, 4, 512])
                self.dT = [cv(BF16, [128, 512]) for _ in range(2)]
                self.PTm = cv(BF16, [128, 4, 128])
                self.hn = cv(BF16, [128, 4, 128])
                self.sqN = cv(F32, [128, 4, 128])
                self.phase_mixer(inj, 0 if l == 0 else 1)
                if l == 1:
                    self.phase_halo_exchange()
                P.barrier()
                self.u_off = 0
                self.wu[1] = cv(BF16, [128, KT, 1024])
                self.wd[1] = cv(BF16, [128, 4, 1024])
                self.hT2 = cv(BF16, [128, KT, NT + 2])
                self.sq = [cv(BF16, [128, KT, 128]) for _ in range(2)]
                self.rstd = cv(F32, [128, 512])
                self.accg = [cv(F32, [128, 512]) for _ in range(2)]
                self.accv = [cv(F32, [128, 512]) for _ in range(2)]
                self.sg = [cv(F32, [128, 512]) for _ in range(2)]
                self.actb = [cv(BF16, [128, 4, 512]) for _ in range(2)]
                self.load_ffn_chunk(l, 1, 1)
                if l == 0:
                    dma(P, "pool", self.wA, self.w_in_src(1)[:, :, 0:1024], [], ["wA"], "wA")
                P.op("dve", lambda e: e.memset(self.hT2[:, :, 0:2], 0.0), [], ["hT2"])
                for g, (t0, t1) in enumerate(GROUPS):
                    a, b = t0 * 128, t1 * 128
                    self.rmsnorm(a, b, V_GF, self.hT2[:, :, 2 + a:2 + b], ["hT2"])
                self.ffn_phase(l, 128 * inj)
                if l == 0:
                    self.load_ffn_chunk(1, 0, 0)

            self.layer = 1
            P.barrier()
            self.u_off = 0
            self.sq = [cv(BF16, [128, KT, 128]) for _ in range(2)]
            self.rstd = cv(F32, [128, 512])
            yT = [cv(F32, [128, KT, 128]) for _ in range(2)]
            ytok = [cv(F32, [128, D]) for _ in range(2)]
            self.rmsnorm(2 * 128, 3 * 128, V_FIN, yT[0], [("yT", 0)])
            for t in range(2, NTL):
                b2 = t % 2
                a, b = t * 128, (t + 1) * 128
                if t + 1 < NTL:
                    self.rmsnorm((t + 1) * 128, (t + 2) * 128, V_FIN, yT[(t + 1) % 2], [("yT", (t + 1) % 2)])
                banks = (4, 5) if b2 == 0 else (6, 7)
                for kt in range(KT):
                    bk = banks[kt // 4]
                    tr(P, self.ps[:, bk, (kt % 4) * 128:(kt % 4 + 1) * 128], yT[b2][:, kt, :], self.cf[:, 0:128],
                       [("yT", b2), "cf"], [("ps", bk)])
                cp(P, "act", ytok[b2][:, 0:512], self.ps[:, banks[0], :], [("ps", banks[0])], [("ytok", b2)])
                cp(P, "dve", ytok[b2][:, 512:1024], self.ps[:, banks[1], :], [("ps", banks[1])], [("ytok", b2)])
                o = dma(P, "sp", self.d_out[(t - 2) * 128:(t - 1) * 128, :], ytok[b2], [("ytok", b2)], [("yo", t)],
                        ("yout", b2))
                P.final.append(o)
            P.emit()
        return nc

    def build_state(self):
        nc, P = self.nc, self.P
        l = self.layer
        self.declare()
        import contextlib

        with contextlib.ExitStack() as st:
            sb = lambda name, shape, dt_: st.enter_context(nc.sbuf_tensor("sb_" + name, shape, dt_))[:]
            self.x = sb("x", [128, KT, NT], F32)
            self.cf = sb("cf", [128, 512], F32)
            self.percore = sb("percore", [128, 80], F32)
            self.vecs = sb("vecs", [128, 2, VW], F32)
            self.identb = sb("identb", [128, 128], BF16)
            self.onesb = sb("onesb", [128, 128], BF16)
            self.epsb = sb("epsb", [128, 1], F32)
            self.wA = sb("wA", [128, KT, 1024], BF16)
            self.wB = sb("wB", [128, KT, 1032], BF16)
            self.xtok = [sb("xtok%d" % i, [128, D], F32) for i in range(2)]
            self.sq = [sb("sq%d" % i, [128, KT, 128], BF16) for i in range(2)]
            self.rstd = sb("rstd", [128, 512], F32)
            self.hT = sb("hT", [128, KT, 512], BF16)
            self.pre = [sb("pre%d" % i, [128, 515], F32) for i in range(2)]
            self.acc = [sb("acc%d" % i, [128, 512], F32) for i in range(2)]
            self.qkcarry = sb("qkcarry", [128, 4, 3], F32)
            self.kTw = sb("kTw", [128, 2, 512], BF16)
            self.ktok = sb("ktok", [128, 4, 256], BF16)
            self.vt = sb("vt", [128, 4, 4, 129], BF16)
            self.gsb = sb("gsb", [128, 4, 8], F32)
            self.lfn = sb("lfn", [128, 4, 4], F32)
            self.cS = sb("cS", [128, NTL, 4], F32)
            self.emb = sb("emb", [128, NTL, 4], F32)
            self.Gp = sb("Gp", [128, NTL, 2], F32)
            self.Cst = sb("Cst", [128, 2, 129], F32)
            self.ps = st.enter_context(nc.psum_tensor("ps", [128, 8, 512], F32))[:]
            self.mm_banks, self.n_mm, self.mm_i = [0, 1, 2], 3, 0
            self.sq_i = 0
            self.pre_i = 0

            self.load_common()
            self.load_w_in(l, only_k_v_g=True)
            self.load_x()
            P.op("dve", lambda e: e.memset(self.qkcarry, 0.0), [], [("qkc", c) for c in range(4)])
            P.op("dve", lambda e: e.memset(self.Cst, 0.0), [], ["Cst"])
            for g, (t0, t1) in enumerate(GROUPS):
                if t0 >= 17:
                    break
                a, b = t0 * 128, t1 * 128
                W = b - a
                ntg = t1 - t0
                self.rmsnorm(a, b, V_GMIX, self.hT, ["hT"])
                for ct in (2, 3):
                    self.qk_tile(g, ct, W, self.kTw[:, ct - 2, 0:W], "kTw")
                self.gates_group(g, t0, ntg)
                for j in range(ntg):
                    t = t0 + j
                    if t < 1 or t >= 17:
                        continue
                    self.v_tile(j, t)
                    self.k_transpose(j)
                    self.kv_update(j, t)
            o = dma(P, "sp", self.d_state_out, self.Cst.rearrange("p a b -> p (a b)"), ["Cst"], ["dout"], "dout")
            P.final.append(o)
            P.emit()
        return nc


def build_main_prog(layer, final):
    b = Builder("main", layer, final)
    return b.build_main()


def build_state_prog(layer):
    b = Builder("state", layer, False)
    return b.build_state()


def make_consts():
    c = np.zeros((128, 512), np.float32)
    c[:, 0:128] = np.eye(128, dtype=np.float32)
    tri = (np.arange(128)[:, None] <= np.arange(128)[None, :]).astype(np.float32)
    c[:, 128:256] = tri
    c[:, 256:384] = 1.0
    c[:, 384:512] = tri * 0.125
    return c


def make_percore():
    pcs = []
    for c in range(8):
        pc = np.zeros((128, 80), np.float32)
        first = (c % 2 == 0)
        for g, w in enumerate((2, 4, 8, 16)):
            for i in range(16):
                pc[:, g * 16 + i] = 1.0 / (min(i + 1, w) if first else w)
        pc[:, 64] = 0.0 if first else 1.0
        if not first:
            pc[:, 65 + c - 1] = 1.0
        pcs.append(pc)
    return pcs


def make_vecs(inp):
    v = np.zeros((2, 128, VW), np.float32)
    for l in range(2):
        v[l, :, V_GMIX:V_GMIX + 8] = inp["mix_norm"][l].reshape(8, 128).T
        v[l, :, V_QKC:V_QKC + 16] = inp["w_qk_conv"][l].reshape(4, 4, 128).transpose(2, 1, 0).reshape(128, 16)
        v[l, :, V_BG:V_BG + 8] = inp["b_gates"][l][None, :]
        v[l, :, V_GH:V_GH + 4] = inp["head_norm"][l].reshape(4, 128).T
        v[l, :, V_PS:V_PS + 4] = inp["pool_scale"][l].reshape(4, 128).T
        v[l, :, V_GF:V_GF + 8] = inp["ffn_norm"][l].reshape(8, 128).T
        v[l, :, V_FC:V_FC + 132] = inp["w_ffn_conv"][l].reshape(3, 44, 128).transpose(2, 1, 0).reshape(128, 132)
        v[l, :, V_FB:V_FB + 44] = inp["b_ffn_conv"][l].reshape(44, 128).T
        v[l, :, V_FIN:V_FIN + 8] = inp["final_norm"].reshape(8, 128).T
    return v


def make_xloc(xfull):
    out = []
    for c in range(8):
        b, half = c // 2, c % 2
        xl = np.zeros((NT, D), np.float32)
        s = half * 2048
        xl[256:] = xfull[b, s:s + 2048]
        if half == 1:
            xl[:256] = xfull[b, s - 256:s]
        out.append(xl)
    return out


def host_prep(inp):
    return dict(consts=make_consts(), percore=make_percore(), vecs=make_vecs(inp), x_loc=make_xloc(inp["x"]))


def _run_layer(l, final, host, xlocs, inp):
    common = dict(consts=host["consts"], vecs=host["vecs"], w_in=inp["w_in"])
    nc_s = build_state_prog(l)
    maps = [dict(common, x_loc=xlocs[c], percore=host["percore"][c]) for c in range(8)]
    res = run_bass_kernel_spmd(nc_s, maps, core_ids=list(range(8)))
    states = [np.asarray(res.results[c]["state_out"], np.float32) for c in range(8)]
    zero = np.zeros((128, SW), np.float32)
    nc_m = build_main_prog(l, final)
    maps = [dict(common, x_loc=xlocs[c], percore=host["percore"][c], w_pool=inp["w_pool"], w_out=inp["w_out"],
                 w_up=inp["w_up"], w_down=inp["w_down"], state_in=(states[c - 1] if c % 2 == 1 else zero))
            for c in range(8)]
    res = run_bass_kernel_spmd(nc_m, maps, core_ids=list(range(8)))
    y = np.zeros((4, 4096, D), np.float32)
    for c in range(8):
        y[c // 2, (c % 2) * 2048:(c % 2 + 1) * 2048] = res.results[c]["y"]
    return y


def build_fused_prog():
    b = Builder("fused", 0, True)
    return b.build_fused()


def kernel(**inputs):
    inp = {k: np.ascontiguousarray(np.asarray(v, dtype=np.float32)) for k, v in inputs.items()}
    host = host_prep(inp)
    nc = build_fused_prog()
    maps = [dict(consts=host["consts"], vecs=host["vecs"], x_loc=host["x_loc"][c], percore=host["percore"][c],
                 w_in=inp["w_in"], w_pool=inp["w_pool"], w_out=inp["w_out"], w_up=inp["w_up"], w_down=inp["w_down"])
            for c in range(8)]
    res = run_bass_kernel_spmd(nc, maps, core_ids=list(range(8)))
    y = np.zeros((4, 4096, D), np.float32)
    for c in range(8):
        y[c // 2, (c % 2) * 2048:(c % 2 + 1) * 2048] = res.results[c]["y"]
    return y
```
